# Optimizing a Trainium2 kernel written in Bass

```python
import math
import jax, jax.numpy as jnp
from jax import lax
import numpy as np

D_MODEL = 1024
BATCH = 4
SEQ = 4096
DEPTH = 2

HEAD_DIM = 64
N_MIXERS = 2
DSW_GROUPS = ((128, 1), (512, 4), (2048, 16))
DSW_N_GROUPS = len(DSW_GROUPS)
DSW_HEADS_PER_GROUP = D_MODEL // (2 * HEAD_DIM)
DSW_HEADS = DSW_N_GROUPS * DSW_HEADS_PER_GROUP
DSW_OUT_WIDTH = DSW_HEADS_PER_GROUP * HEAD_DIM
MOBA_HEADS = D_MODEL // HEAD_DIM
MOBA_BLOCK = 256
MOBA_TOPK = 3
MOBA_Q_CHUNK = 32
D_FF = 4 * D_MODEL
REL_BUCKETS = 32
REL_MAX_DISTANCE = 2048
BIAS_HEADS = max(DSW_HEADS, MOBA_HEADS)

N_A_LAYERS = (DEPTH + 1) // 2
N_B_LAYERS = DEPTH // 2
EPS = 1e-6
NEG = -1e30
SCALE = HEAD_DIM ** -0.5

kernel_name = "hybrid_dilated_moba_sqrelu"


def rmsnorm(x, g):
    xf = x.astype(jnp.float32)
    y = xf * lax.rsqrt(jnp.mean(xf * xf, axis=-1, keepdims=True) + EPS)
    return (y * g.astype(jnp.float32)).astype(x.dtype)


def t5_bucket(dist):
    n = jnp.maximum(dist, 0)
    max_exact = REL_BUCKETS // 2
    nf = jnp.maximum(n, 1).astype(jnp.float32)
    large = max_exact + (jnp.log(nf / max_exact) / math.log(REL_MAX_DISTANCE / max_exact)
                         * (REL_BUCKETS - max_exact)).astype(jnp.int32)
    large = jnp.minimum(large, REL_BUCKETS - 1)
    return jnp.where(n < max_exact, n, large)


def softmax_attend(logits, v, eq):
    m = jnp.max(logits, axis=-1, keepdims=True)
    p = jnp.exp(logits - m)
    den = jnp.sum(p, axis=-1, keepdims=True)
    o = jnp.einsum(eq, p / den, v.astype(jnp.float32))
    return o, (m + jnp.log(den))[..., 0]


def dsw_group(q, k, v, rel_bias, col0, window, dilation):
    B, H, S, hd = q.shape
    n = window // dilation
    blk = n
    L = S // dilation
    nb = -(-L // blk)
    Lp = nb * blk

    def to_sub(t):
        t = t.reshape(B, H, L, dilation, hd).transpose(0, 1, 3, 2, 4)
        return jnp.pad(t, ((0, 0), (0, 0), (0, 0), (0, Lp - L), (0, 0)))

    def band(t):
        t = jnp.pad(t, ((0, 0), (0, 0), (0, 0), (blk, 0), (0, 0))).reshape(B, H, dilation, nb + 1, blk, hd)
        return jnp.concatenate([t[:, :, :, :-1], t[:, :, :, 1:]], axis=4)

    qs, ks, vs = to_sub(q), to_sub(k), to_sub(v)
    qb = qs.reshape(B, H, dilation, nb, blk, hd)
    kb, vb = band(ks), band(vs)
    logits = jnp.einsum('bhrnid,bhrnjd->bhrnij', qb, kb, preferred_element_type=jnp.float32) * SCALE

    i = jnp.arange(blk)[:, None]
    j = jnp.arange(2 * blk)[None, :]
    dist = blk + i - j
    in_band = (dist >= 0) & (dist <= n)
    first = (jnp.arange(nb) == 0)[:, None, None] & (j < blk)[None]
    mask = in_band[None] & ~first
    bias = rel_bias[t5_bucket(dist * dilation)][..., col0:col0 + H].transpose(2, 0, 1)
    logits = jnp.where(mask[None, None, None], logits + bias[None, :, None, None], NEG)

    o, lse = softmax_attend(logits, vb, 'bhrnij,bhrnjd->bhrnid')
    o = o.reshape(B, H, dilation, Lp, hd)[:, :, :, :L].transpose(0, 1, 3, 2, 4).reshape(B, H, S, hd)
    lse = lse.reshape(B, H, dilation, Lp)[..., :L].transpose(0, 1, 3, 2).reshape(B, H, S)
    return o, lse


def dsw_mixer(h, w_qkv, q_gain, k_gain, w_o, rel_bias):
    B, S, _ = h.shape
    G, Hg = DSW_N_GROUPS, DSW_HEADS_PER_GROUP
    qkv = (h @ w_qkv).reshape(B, S, 3, G, Hg, HEAD_DIM)
    q = rmsnorm(qkv[:, :, 0], q_gain)
    k = rmsnorm(qkv[:, :, 1], k_gain)
    v = qkv[:, :, 2]
    outs, lses = [], []
    for g, (window, dilation) in enumerate(DSW_GROUPS):
        o, lse = dsw_group(q[:, :, g].transpose(0, 2, 1, 3), k[:, :, g].transpose(0, 2, 1, 3),
                           v[:, :, g].transpose(0, 2, 1, 3), rel_bias, g * Hg, window, dilation)
        outs.append(o)
        lses.append(lse)
    wts = jax.nn.softmax(jnp.stack(lses), axis=0)
    o = jnp.einsum('gbhs,gbhsd->bshd', wts, jnp.stack(outs)).reshape(B, S, DSW_OUT_WIDTH)
    return o.astype(h.dtype) @ w_o


def gather_blocks(blocks, idx):
    return jax.vmap(jax.vmap(lambda bl, ix: bl[ix]))(blocks, idx)


def moba_mixer(h, w_qkv, q_gain, k_gain, w_o, rel_bias):
    B, S, _ = h.shape
    H, hd, blk = MOBA_HEADS, HEAD_DIM, MOBA_BLOCK
    qkv = (h @ w_qkv).reshape(B, S, 3, H, hd)
    q = rmsnorm(qkv[:, :, 0], q_gain).transpose(0, 2, 1, 3)
    k = rmsnorm(qkv[:, :, 1], k_gain).transpose(0, 2, 1, 3)
    v = qkv[:, :, 2].transpose(0, 2, 1, 3)
    nblk = -(-S // blk)
    Sp = nblk * blk
    pad = ((0, 0), (0, 0), (0, Sp - S), (0, 0))
    q, k, v = jnp.pad(q, pad), jnp.pad(k, pad), jnp.pad(v, pad)
    qb = q.reshape(B, H, nblk, blk, hd)
    kb = k.reshape(B, H, nblk, blk, hd)
    vb = v.reshape(B, H, nblk, blk, hd)
    pos = jnp.arange(Sp)
    qblk = pos // blk
    table_h = rel_bias[:, :H].T

    ii = jnp.arange(blk)
    own_bias = table_h[:, t5_bucket(ii[:, None] - ii[None, :])]
    lo = jnp.einsum('bhnid,bhnjd->bhnij', qb, kb, preferred_element_type=jnp.float32) * SCALE
    lo = jnp.where(ii[:, None] >= ii[None, :], lo + own_bias[None, :, None], NEG)
    o_own, lse_own = softmax_attend(lo, vb, 'bhnij,bhnjd->bhnid')
    o_own = o_own.reshape(B, H, Sp, hd)
    lse_own = lse_own.reshape(B, H, Sp)

    kmean = jnp.mean(kb.astype(jnp.float32), axis=3)
    gate = jnp.einsum('bhsd,bhnd->bhsn', q.astype(jnp.float32), kmean)
    past = jnp.arange(nblk)[None, :] < qblk[:, None]
    gate = jnp.where(past, gate, -jnp.inf)
    topk = min(MOBA_TOPK, nblk)
    _, sel = lax.top_k(gate, topk)
    sel_valid = sel < qblk[:, None]

    C = MOBA_Q_CHUNK
    nC = Sp // C

    def chunk(t):
        return jnp.moveaxis(t.reshape(B, H, nC, C, *t.shape[3:]), 2, 0)

    jj = jnp.arange(blk)
    head_ix = jnp.arange(H)[None, :, None, None]

    def attend(args):
        qc, selc, validc, posc = args
        flat = selc.reshape(B, H, C * topk)
        kg = gather_blocks(kb, flat).reshape(B, H, C, topk * blk, hd)
        vg = gather_blocks(vb, flat).reshape(B, H, C, topk * blk, hd)
        logits = jnp.einsum('bhcd,bhckd->bhck', qc, kg, preferred_element_type=jnp.float32) * SCALE
        kpos = (selc[..., None] * blk + jj).reshape(B, H, C, topk * blk)
        bias = table_h[head_ix, t5_bucket(posc[None, None, :, None] - kpos)]
        valid = jnp.repeat(validc, blk, axis=-1)
        logits = jnp.where(valid, logits + bias, NEG)
        return softmax_attend(logits, vg, 'bhck,bhckd->bhcd')

    o_sel, lse_sel = lax.map(attend, (chunk(q), chunk(sel), chunk(sel_valid), pos.reshape(nC, C)))
    o_sel = jnp.moveaxis(o_sel, 0, 2).reshape(B, H, Sp, hd)
    lse_sel = jnp.moveaxis(lse_sel, 0, 2).reshape(B, H, Sp)

    lse = jnp.logaddexp(lse_own, lse_sel)
    o = jnp.exp(lse_own - lse)[..., None] * o_own + jnp.exp(lse_sel - lse)[..., None] * o_sel
    o = o[:, :, :S].transpose(0, 2, 1, 3).reshape(B, S, H * hd)
    return o.astype(h.dtype) @ w_o


def sq_relu_mlp(h, w1, w2):
    return jnp.square(jax.nn.relu(h @ w1)) @ w2


def setup_inputs(seed: int = 0) -> dict:
    key = jax.random.key(seed)
    ks = jax.random.split(key, 16)
    D = D_MODEL
    nrm = lambda k, shape, s: jax.random.normal(k, shape, jnp.float32) * s
    a_qkv_cols = 3 * DSW_HEADS * HEAD_DIM
    b_qkv_cols = 3 * MOBA_HEADS * HEAD_DIM
    return {
        "x": nrm(ks[0], (BATCH, SEQ, D), 1.0),
        "rel_bias": nrm(ks[1], (REL_BUCKETS, BIAS_HEADS), 0.3),
        "norm_mix": 1.0 + nrm(ks[2], (DEPTH, D), 0.02),
        "norm_ffn": 1.0 + nrm(ks[3], (DEPTH, D), 0.02),
        "a_w_qkv": nrm(ks[4], (N_A_LAYERS, D, a_qkv_cols), D ** -0.5),
        "a_q_gain": 1.0 + nrm(ks[5], (N_A_LAYERS, HEAD_DIM), 0.02),
        "a_k_gain": 1.0 + nrm(ks[6], (N_A_LAYERS, HEAD_DIM), 0.02),
        "a_w_o": nrm(ks[7], (N_A_LAYERS, DSW_OUT_WIDTH, D), DSW_OUT_WIDTH ** -0.5),
        "b_w_qkv": nrm(ks[8], (N_B_LAYERS, D, b_qkv_cols), D ** -0.5),
        "b_q_gain": 1.0 + nrm(ks[9], (N_B_LAYERS, HEAD_DIM), 0.02),
        "b_k_gain": 1.0 + nrm(ks[10], (N_B_LAYERS, HEAD_DIM), 0.02),
        "b_w_o": nrm(ks[11], (N_B_LAYERS, MOBA_HEADS * HEAD_DIM, D), (MOBA_HEADS * HEAD_DIM) ** -0.5),
        "ffn_w1": nrm(ks[12], (DEPTH, D, D_FF), D ** -0.5),
        "ffn_w2": nrm(ks[13], (DEPTH, D_FF, D), 0.5 * D_FF ** -0.5),
    }


def reference(x, rel_bias, norm_mix, norm_ffn, a_w_qkv, a_q_gain, a_k_gain, a_w_o,
              b_w_qkv, b_q_gain, b_k_gain, b_w_o, ffn_w1, ffn_w2):
    h = x
    for i in range(DEPTH):
        u = rmsnorm(h, norm_mix[i])
        li = i // N_MIXERS
        if i % N_MIXERS == 0:
            h = h + dsw_mixer(u, a_w_qkv[li], a_q_gain[li], a_k_gain[li], a_w_o[li], rel_bias)
        else:
            h = h + moba_mixer(u, b_w_qkv[li], b_q_gain[li], b_k_gain[li], b_w_o[li], rel_bias)
        h = h + sq_relu_mlp(rmsnorm(h, norm_ffn[i]), ffn_w1[i], ffn_w2[i])
    return h
```

```python
import contextlib
import numpy as np
import ml_dtypes
import concourse.bass as bass
import concourse.mybir as mybir
from concourse.bass_utils import run_bass_kernel_spmd

F32 = mybir.dt.float32
BF16 = mybir.dt.bfloat16
ALU = mybir.AluOpType
AF = mybir.ActivationFunctionType
AX = mybir.AxisListType

D = 1024
SEQ = 4096
BATCH = 4
HD = 64
EPS = 1e-6
SCALE = HD ** -0.5
NEGB = -30000.0
SAME_ENGINE_SYNC = True


class Buf:
    __slots__ = ("w", "r", "name")

    def __init__(self, name=""):
        self.w = None
        self.r = {}
        self.name = name


class Prog:
    NDQ = 6

    def __init__(self, nc, es):
        self.nc = nc
        self.es = es
        self.eng = {"pe": nc.tensor, "act": nc.scalar, "dve": nc.vector,
                    "pool": nc.gpsimd, "sp": nc.sync}
        self.sem = {}
        self.cnt = {}
        for k in ("pe", "act", "dve", "pool"):
            self.sem[k] = es.enter_context(nc.semaphore(f"s_{k}"))
            self.cnt[k] = 0
        self.waited = {k: {} for k in self.eng}
        self.dq = {}
        for q in ("sp", "pool", "act"):
            sems = [es.enter_context(nc.semaphore(f"d_{q}{i}")) for i in range(self.NDQ)]
            self.dq[q] = {"sems": sems, "val": [0] * self.NDQ, "idx": 0}
        self.pe_open = False
        self.out_tickets = []

    def need(self, e, tk):
        if tk is None:
            return
        sem, val = tk
        key = sem.num
        if self.waited[e].get(key, 0) >= val:
            return
        self.eng[e].wait_ge(sem, val)
        self.waited[e][key] = val

    def _deps(self, e, reads, writes):
        own = self.sem.get(e)
        tks = []
        for b in reads:
            if b.w is not None:
                tks.append(b.w)
        for b in writes:
            if b.w is not None:
                tks.append(b.w)
            tks.extend(b.r.values())
        for tk in tks:
            if own is not None and tk[0].num == own.num:
                if e == "pe" or not SAME_ENGINE_SYNC:
                    continue
            self.need(e, tk)

    def _record(self, tk, reads, writes):
        key = tk[0].num
        for b in reads:
            b.r[key] = tk
        for b in writes:
            b.w = tk
            b.r = {}

    def op(self, e, fn, reads=(), writes=(), inc=True):
        self._deps(e, reads, writes)
        ins = fn()
        if inc:
            self.cnt[e] += 1
            ins.then_inc(self.sem[e], 1)
            tk = (self.sem[e], self.cnt[e])
            if e == "pe":
                self.pe_open = False
        else:
            assert e == "pe"
            tk = (self.sem[e], self.cnt[e] + 1)
            self.pe_open = True
        self._record(tk, reads, writes)
        return tk

    def dma(self, q, out, in_, reads=(), writes=(), is_output=False, **kw):
        d = self.dq[q]
        i = d["idx"] % self.NDQ
        d["idx"] += 1
        if d["val"][i] > 0:
            self.need(q, (d["sems"][i], d["val"][i]))
        self._deps(q, reads, writes)
        ins = self.eng[q].dma_start(out=out, in_=in_, **kw)
        d["val"][i] += 16
        ins.then_inc(d["sems"][i], 16)
        tk = (d["sems"][i], d["val"][i])
        self._record(tk, reads, writes)
        if is_output:
            self.out_tickets.append(tk)
        return tk

    def finish(self):
        assert not self.pe_open
        for q in ("sp", "pool", "act"):
            d = self.dq[q]
            for i in range(self.NDQ):
                if d["val"][i] > 0:
                    self.need("sp", (d["sems"][i], d["val"][i]))
        for e in ("pe", "act", "dve", "pool"):
            if self.cnt[e] > 0:
                self.need("sp", (self.sem[e], self.cnt[e]))

    def mm(self, out, lhsT, rhs, start, stop, reads, writes, last=None):
        if last is None:
            last = stop
        return self.op("pe", lambda: self.nc.tensor.matmul(out, lhsT, rhs, start=start, stop=stop),
                       reads, writes, inc=last)


def sstep(start, n, step):
    return slice(start, start + (n - 1) * step + 1, step)


def sb(nc, es, name, shape, dt):
    return es.enter_context(nc.sbuf_tensor(name, shape, dt))


def ps(nc, es, name, shape, dt=F32):
    return es.enter_context(nc.psum_tensor(name, shape, dt))


def emit_consts(P, nc, es):
    c = {}
    c["avg1024"] = sb(nc, es, "avg1024", [128, 128], BF16)
    c["avg64"] = sb(nc, es, "avg64", [128, 128], BF16)
    c["b_avg1024"] = Buf()
    c["b_avg64"] = Buf()
    P.op("dve", lambda: nc.vector.memset(c["avg1024"][:], 1.0 / 1024.0), (), (c["b_avg1024"],))
    P.op("dve", lambda: nc.vector.memset(c["avg64"][:], 0.0), (), (c["b_avg64"],))
    P.op("dve", lambda: nc.vector.memset(c["avg64"][0:64, 0:64], 1.0 / 64.0), (), (c["b_avg64"],))
    P.op("dve", lambda: nc.vector.memset(c["avg64"][64:128, 64:128], 1.0 / 64.0), (), (c["b_avg64"],))
    c["eps"] = sb(nc, es, "eps_c", [128, 1], F32)
    c["b_eps"] = Buf()
    P.op("dve", lambda: nc.vector.memset(c["eps"][:], EPS), (), (c["b_eps"],))
    return c


def emit_rsqrt(P, nc, C, rstd, b_rstd, pt, b_pt):
    P.op("act", lambda: nc.scalar.activation(out=rstd[:], in_=pt[:], func=AF.Sqrt, bias=C["eps"][:, 0:1]),
         (b_pt, C["b_eps"]), (b_rstd,))
    P.op("dve", lambda: nc.vector.reciprocal(out=rstd[:], in_=rstd[:]), (b_rstd,), (b_rstd,))


def emit_rmsnorm(P, nc, C, hT, b_h, g_sb, b_g, uT, b_u, TT, psums, tmp):
    NT = TT // 512
    sq, b_sq, rstd, b_rstd = tmp["sq"], tmp["b_sq"], tmp["rstd"], tmp["b_rstd"]
    for t in range(NT):
        ts = slice(t * 512, (t + 1) * 512)
        pt, b_pt = psums[t % 2]
        for c in range(8):
            i = c % 2
            P.op("act", lambda: nc.scalar.activation(out=sq[i][:], in_=hT[:, c, ts], func=AF.Square),
                 (b_h[c][t],), (b_sq[i],))
            P.mm(pt[:], C["avg1024"][:], sq[i][:], c == 0, c == 7,
                 (C["b_avg1024"], b_sq[i]), (b_pt,), last=True)
        emit_rsqrt(P, nc, C, rstd, b_rstd, pt, b_pt)
        for c in range(8):
            e = "dve"
            eng = nc.vector
            P.op(e, lambda: eng.scalar_tensor_tensor(out=uT[:, c, ts], in0=hT[:, c, ts],
                                                     scalar=g_sb[:, c:c + 1], in1=rstd[:],
                                                     op0=ALU.mult, op1=ALU.mult),
                 (b_h[c][t], b_g, b_rstd), (b_u[c][t],))


def build_lin_in(TT, CQK, CV):
    nc = bass.Bass("TRN2", target_bir_lowering=False)
    CW = CQK + CV
    hT_d = nc.dram_tensor("hT", [D, TT], F32, kind="ExternalInput").ap()
    g_d = nc.dram_tensor("g", [128, 8], F32, kind="ExternalInput").ap()
    w_d = nc.dram_tensor("w", [D, CW], F32, kind="ExternalInput").ap()
    gain_d = nc.dram_tensor("gain", [128, 2], F32, kind="ExternalInput").ap()
    qk_d = nc.dram_tensor("qkT", [CQK, TT], BF16, kind="ExternalOutput").ap()
    v_d = nc.dram_tensor("v", [TT, CV], BF16, kind="ExternalOutput").ap()
    NT = TT // 512
    with contextlib.ExitStack() as es:
        P = Prog(nc, es)
        C = emit_consts(P, nc, es)
        hT = sb(nc, es, "hT_sb", [128, 8, TT], F32)
        uT = sb(nc, es, "uT_sb", [128, 8, TT], BF16)
        g_sb = sb(nc, es, "g_sb", [128, 8], F32)
        gain_sb = sb(nc, es, "gain_sb", [128, 2], F32)
        b_h = [[Buf() for _ in range(NT)] for _ in range(8)]
        b_u = [[Buf() for _ in range(NT)] for _ in range(8)]
        b_g, b_gain = Buf(), Buf()
        tmp = {"sq": [sb(nc, es, f"sq{i}", [128, 512], BF16) for i in range(2)],
               "b_sq": [Buf(), Buf()],
               "rstd": sb(nc, es, "rstd", [128, 512], F32), "b_rstd": Buf()}
        psums = [(ps(nc, es, f"ps{i}", [128, 512]), Buf()) for i in range(6)]
        P.dma("sp", g_sb[:], g_d, (), (b_g,))
        P.dma("sp", gain_sb[:], gain_d, (), (b_gain,))
        for c in range(8):
            for t in range(NT):
                P.dma("sp", hT[:, c, t * 512:(t + 1) * 512], hT_d[c * 128:(c + 1) * 128, t * 512:(t + 1) * 512],
                      (), (b_h[c][t],))
        emit_rmsnorm(P, nc, C, hT, b_h, g_sb, b_g, uT, b_u, TT, psums[0:2], tmp)

        NG = CW // 512
        wb = [sb(nc, es, f"wb{i}", [128, 8, 512], BF16) for i in range(2)]
        b_wb = [Buf(), Buf()]
        osb = [sb(nc, es, f"osb{i}", [128, 512], BF16) for i in range(3)]
        b_osb = [Buf() for _ in range(3)]
        oi = 0
        w_v = w_d.rearrange("(c p) n -> p c n", p=128)

        def load_w(gi):
            for c in range(8):
                P.dma("pool", wb[gi % 2][:, c, :], w_v[:, c, gi * 512:(gi + 1) * 512], (), (b_wb[gi % 2],))

        load_w(0)
        pi = 2
        for gi in range(NG):
            if gi + 1 < NG:
                load_w(gi + 1)
            w = wb[gi % 2]
            bw = b_wb[gi % 2]
            if gi * 512 < CQK:
                isq = 0 if gi * 512 < CQK // 2 else 1
                for j in range(4):
                    col0 = gi * 512 + j * 128
                    for t in range(NT):
                        ts = slice(t * 512, (t + 1) * 512)
                        pa, b_pa = psums[2 + (pi % 2)]
                        pb, b_pb = psums[4 + (pi % 2)]
                        pi += 1
                        for c in range(8):
                            P.mm(pa[:], w[:, c, j * 128:(j + 1) * 128], uT[:, c, ts], c == 0, c == 7,
                                 (bw, b_u[c][t]), (b_pa,))
                        i = pi % 2
                        sq, b_sq = tmp["sq"][i], tmp["b_sq"][i]
                        P.op("act", lambda: nc.scalar.activation(out=sq[:], in_=pa[:], func=AF.Square),
                             (b_pa,), (b_sq,))
                        P.mm(pb[:], C["avg64"][:], sq[:], True, True, (C["b_avg64"], b_sq), (b_pb,))
                        rstd, b_rstd = tmp["rstd"], tmp["b_rstd"]
                        emit_rsqrt(P, nc, C, rstd, b_rstd, pb, b_pb)
                        o, b_o = osb[oi % 3], b_osb[oi % 3]
                        oi += 1
                        P.op("dve", lambda: nc.vector.scalar_tensor_tensor(out=o[:], in0=pa[:],
                                                                           scalar=gain_sb[:, isq:isq + 1], in1=rstd[:],
                                                                           op0=ALU.mult, op1=ALU.mult),
                             (b_pa, b_gain, b_rstd), (b_o,))
                        P.dma("sp", qk_d[col0:col0 + 128, ts], o[:], (b_o,), (), is_output=True)
            else:
                vc0 = gi * 512 - CQK
                for tt in range(TT // 128):
                    t = tt // 4
                    tsl = slice(tt * 128, (tt + 1) * 128)
                    pa, b_pa = psums[2 + (pi % 2)]
                    pi += 1
                    for c in range(8):
                        P.mm(pa[:], uT[:, c, tsl], w[:, c, :], c == 0, c == 7, (bw, b_u[c][t]), (b_pa,))
                    o, b_o = osb[oi % 3], b_osb[oi % 3]
                    oi += 1
                    P.op("act", lambda: nc.scalar.copy(out=o[:], in_=pa[:]), (b_pa,), (b_o,))
                    P.dma("sp", v_d[tsl, vc0:vc0 + 512], o[:], (b_o,), (), is_output=True)
        P.finish()
    return nc


def build_lin_out(TT, KO, DFF=4096):
    nc = bass.Bass("TRN2", target_bir_lowering=False)
    hT_d = nc.dram_tensor("hT", [D, TT], F32, kind="ExternalInput").ap()
    oT_d = nc.dram_tensor("oT", [KO, TT], BF16, kind="ExternalInput").ap()
    g_d = nc.dram_tensor("g", [128, 8], F32, kind="ExternalInput").ap()
    wo_d = nc.dram_tensor("wo", [KO, D], F32, kind="ExternalInput").ap()
    w1_d = nc.dram_tensor("w1", [D, DFF], F32, kind="ExternalInput").ap()
    w2_d = nc.dram_tensor("w2", [DFF, D], F32, kind="ExternalInput").ap()
    out_d = nc.dram_tensor("hT_out", [D, TT], F32, kind="ExternalOutput").ap()
    NT = TT // 512
    KC = KO // 128
    NFG = DFF // 512
    with contextlib.ExitStack() as es:
        P = Prog(nc, es)
        C = emit_consts(P, nc, es)
        hT = sb(nc, es, "hT_sb", [128, 8, TT], F32)
        uT = sb(nc, es, "uT_sb", [128, 8, TT], BF16)
        aT = sb(nc, es, "aT_sb", [128, 4, TT], BF16)
        g_sb = sb(nc, es, "g_sb", [128, 8], F32)
        b_h = [[Buf() for _ in range(NT)] for _ in range(8)]
        b_u = [[Buf() for _ in range(NT)] for _ in range(8)]
        b_a = [[Buf() for _ in range(NT)] for _ in range(4)]
        b_g = Buf()
        tmp = {"sq": [sb(nc, es, f"sq{i}", [128, 512], BF16) for i in range(2)],
               "b_sq": [Buf(), Buf()],
               "rstd": sb(nc, es, "rstd", [128, 512], F32), "b_rstd": Buf()}
        rl = [sb(nc, es, f"rl{i}", [128, 512], F32) for i in range(2)]
        b_rl = [Buf(), Buf()]
        psums = [(ps(nc, es, f"ps{i}", [128, 512]), Buf()) for i in range(6)]
        wb = [sb(nc, es, f"wb{i}", [128, 8, 512], BF16) for i in range(2)]
        b_wb = [Buf(), Buf()]
        w2b = [sb(nc, es, f"w2b{i}", [128, 4, D], BF16) for i in range(2)]
        b_w2b = [Buf(), Buf()]
        P.dma("sp", g_sb[:], g_d, (), (b_g,))
        for c in range(8):
            for t in range(NT):
                P.dma("sp", hT[:, c, t * 512:(t + 1) * 512], hT_d[c * 128:(c + 1) * 128, t * 512:(t + 1) * 512],
                      (), (b_h[c][t],))
        for c in range(KC):
            for t in range(NT):
                P.dma("act", uT[:, c, t * 512:(t + 1) * 512], oT_d[c * 128:(c + 1) * 128, t * 512:(t + 1) * 512],
                      (), (b_u[c][t],))
        wo_v = wo_d.rearrange("(c p) n -> p c n", p=128)
        w1_v = w1_d.rearrange("(c p) n -> p c n", p=128)
        w2_v = w2_d.rearrange("(f p) n -> p f n", p=128)
        wi = 0

        def load_wo(gi, slot):
            for c in range(KC):
                P.dma("pool", wb[slot][:, c, :], wo_v[:, c, gi * 512:(gi + 1) * 512], (), (b_wb[slot],))

        def load_w1(fg, slot):
            for c in range(8):
                P.dma("pool", wb[slot][:, c, :], w1_v[:, c, fg * 512:(fg + 1) * 512], (), (b_wb[slot],))

        def load_w2(fg, slot):
            for f in range(4):
                P.dma("pool", w2b[slot][:, f, :], w2_v[:, fg * 4 + f, :], (), (b_w2b[slot],))

        load_wo(0, 0)
        load_wo(1, 1)
        pi = 0
        for gi in range(2):
            w, bw = wb[gi], b_wb[gi]
            for j in range(4):
                cj = gi * 4 + j
                for t in range(NT):
                    ts = slice(t * 512, (t + 1) * 512)
                    pa, b_pa = psums[2 + (pi % 4)]
                    pi += 1
                    for c in range(KC):
                        P.mm(pa[:], w[:, c, j * 128:(j + 1) * 128], uT[:, c, ts], c == 0, c == KC - 1,
                             (bw, b_u[c][t]), (b_pa,))
                    P.op("dve", lambda: nc.vector.tensor_tensor(out=hT[:, cj, ts], in0=pa[:], in1=hT[:, cj, ts],
                                                                op=ALU.add),
                         (b_pa, b_h[cj][t]), (b_h[cj][t],))
        load_w1(0, 0)
        load_w2(0, 0)
        emit_rmsnorm(P, nc, C, hT, b_h, g_sb, b_g, uT, b_u, TT, psums[0:2], tmp)
        ri = 0
        for fg in range(NFG):
            slot = fg % 2
            if fg + 1 < NFG:
                load_w1(fg + 1, 1 - slot)
                load_w2(fg + 1, 1 - slot)
            w, bw = wb[slot], b_wb[slot]
            w2, bw2 = w2b[slot], b_w2b[slot]
            for f in range(4):
                for t in range(NT):
                    ts = slice(t * 512, (t + 1) * 512)
                    pa, b_pa = psums[2 + (pi % 4)]
                    pi += 1
                    for c in range(8):
                        P.mm(pa[:], w[:, c, f * 128:(f + 1) * 128], uT[:, c, ts], c == 0, c == 7,
                             (bw, b_u[c][t]), (b_pa,))
                    r, b_r = rl[ri % 2], b_rl[ri % 2]
                    ri += 1
                    P.op("act", lambda: nc.scalar.activation(out=r[:], in_=pa[:], func=AF.Relu), (b_pa,), (b_r,))
                    P.op("pool", lambda: nc.gpsimd.tensor_tensor(out=aT[:, f, ts], in0=r[:], in1=r[:], op=ALU.mult),
                         (b_r,), (b_a[f][t],))
            for j in range(8):
                for t in range(NT):
                    ts = slice(t * 512, (t + 1) * 512)
                    pa, b_pa = psums[2 + (pi % 4)]
                    pi += 1
                    for f in range(4):
                        P.mm(pa[:], w2[:, f, j * 128:(j + 1) * 128], aT[:, f, ts], f == 0, f == 3,
                             (bw2, b_a[f][t]), (b_pa,))
                    P.op("dve", lambda: nc.vector.tensor_tensor(out=hT[:, j, ts], in0=pa[:], in1=hT[:, j, ts],
                                                                op=ALU.add),
                         (b_pa, b_h[j][t]), (b_h[j][t],))
                    if fg == NFG - 1:
                        P.dma("sp", out_d[j * 128:(j + 1) * 128, ts], hT[:, j, ts], (b_h[j][t],), (), is_output=True)
        P.finish()
    return nc


def t5_bucket_np(dist):
    n = np.maximum(dist, 0)
    nf = np.maximum(n, 1).astype(np.float32)
    large = 16 + (np.log(nf / np.float32(16)) / np.float32(np.log(2048 / 16)) * np.float32(16)).astype(np.int32)
    large = np.minimum(large, 31)
    return np.where(n < 16, n, large)


def onehot_dsw(r):
    m = np.arange(384) - 127
    valid = (m >= 0) & (m <= 128)
    b = np.where(valid, t5_bucket_np(m * r), 32)
    oh = np.zeros((33, 384), np.float32)
    oh[b, np.arange(384)] = 1.0
    return oh.astype(ml_dtypes.bfloat16)


MOBA_W = 2304
MOBA_L = MOBA_W + 128


def onehot_moba():
    dd = np.arange(MOBA_L) - 255
    b = np.where(dd >= 0, t5_bucket_np(dd), 32)
    oh = np.zeros((33, MOBA_L), np.float32)
    oh[b, np.arange(MOBA_L)] = 1.0
    return oh.astype(ml_dtypes.bfloat16)


def emit_bias_rows(P, nc, es, rb_d, ncols):
    Bsb = sb(nc, es, "Bsb", [33, ncols], F32)
    b_B = Buf()
    P.op("dve", lambda: nc.vector.memset(Bsb[:], NEGB / 8.0), (), (b_B,))
    P.dma("sp", Bsb[0:32, :], rb_d, (), (b_B,))
    ones33 = sb(nc, es, "ones33", [33, 128], F32)
    b_o = Buf()
    P.op("dve", lambda: nc.vector.memset(ones33[:], 1.0), (), (b_o,))
    return Bsb, b_B, ones33, b_o


def emit_toeplitz(P, nc, Bsb, b_B, ones33, b_o, col, oh_sb, b_oh, L, brep, b_brep, frow, b_frow,
                  scr_d, pst, b_pst, out_ap, W, b_out):
    P.op("dve", lambda: nc.vector.tensor_scalar(out=brep[:], in0=ones33[:], scalar1=Bsb[:, col:col + 1], scalar2=8.0,
                                                op0=ALU.mult, op1=ALU.mult),
         (b_B, b_o), (b_brep,))
    for c0 in range(0, L, 512):
        n = min(512, L - c0)
        P.mm(pst[:, 0:n], brep[:], oh_sb[:, c0:c0 + n], True, True, (b_brep, b_oh), (b_pst,))
        P.op("dve", lambda: nc.vector.tensor_copy(out=frow[:, c0:c0 + n], in_=pst[:, 0:n]), (b_pst,), (b_frow,))
    b_scr = Buf()
    P.dma("sp", scr_d[:, 0:L], frow[:, 0:L], (b_frow,), (b_scr,))
    src = bass.AP(scr_d.tensor, scr_d.offset + 127, [[scr_d.ap[0][0] - 1, 128], [1, W]])
    P.dma("sp", out_ap, src, (b_scr,), (b_out,))


def emit_normalize(P, nc, acc_ap, b_acc, n, E, b_E, R, b_R, psB, b_psB, o_sb, b_osb, out_dram_ap):
    P.op("dve", lambda: nc.vector.reciprocal(out=R[64:65, 0:n], in_=acc_ap[64:65, :]), (b_acc,), (b_R,))
    P.mm(psB[0:64, 0:n], E[:], R[:, 0:n], True, True, (b_E, b_R), (b_psB,))
    P.op("dve", lambda: nc.vector.tensor_tensor(out=o_sb[0:64, 0:n], in0=acc_ap[0:64, :], in1=psB[0:64, 0:n],
                                                op=ALU.mult),
         (b_acc, b_psB), (b_osb,))
    P.dma("sp", out_dram_ap, o_sb[0:64, 0:n], (b_osb,), (), is_output=True)


def emit_attn_consts(P, nc, es):
    A = {}
    A["ident"] = sb(nc, es, "ident", [128, 128], BF16)
    A["b_ident"] = Buf()
    P.op("pool", lambda: nc.gpsimd.memset(A["ident"][:], 1.0), (), (A["b_ident"],))
    P.op("pool", lambda: nc.gpsimd.affine_select(out=A["ident"][:], in_=A["ident"][:], pattern=[[-1, 128]],
                                                 compare_op=ALU.is_equal, fill=0.0, base=0, channel_multiplier=1),
         (A["b_ident"],), (A["b_ident"],))
    A["E"] = sb(nc, es, "Esel", [65, 64], F32)
    A["b_E"] = Buf()
    P.op("dve", lambda: nc.vector.memset(A["E"][:], 0.0), (), (A["b_E"],))
    P.op("dve", lambda: nc.vector.memset(A["E"][64:65, :], 1.0), (), (A["b_E"],))
    A["R"] = sb(nc, es, "Rrec", [65, 512], F32)
    A["b_R"] = Buf()
    P.op("dve", lambda: nc.vector.memset(A["R"][:], 0.0), (), (A["b_R"],))
    return A


DSW_GROUPS = ((128, 1), (512, 4), (2048, 16))


def build_dsw(S, NHM):
    nc = bass.Bass("TRN2", target_bir_lowering=False)
    q_d = nc.dram_tensor("qT", [3, NHM * 64, S], BF16, kind="ExternalInput").ap()
    k_d = nc.dram_tensor("kT", [3, NHM * 64, S], BF16, kind="ExternalInput").ap()
    v_d = nc.dram_tensor("v", [S, 3, NHM * 64], BF16, kind="ExternalInput").ap()
    rb_d = nc.dram_tensor("rb", [32, 3 * NHM], F32, kind="ExternalInput").ap()
    oh_d = nc.dram_tensor("oh", [3, 33, 384], BF16, kind="ExternalInput").ap()
    o_d = nc.dram_tensor("oT", [NHM * 64, S], BF16, kind="ExternalOutput").ap()
    scr_d = nc.dram_tensor("scr", [3 * NHM, 128, 384], BF16, kind="Internal").ap()
    NB = S // 128
    with contextlib.ExitStack() as es:
        P = Prog(nc, es)
        A = emit_attn_consts(P, nc, es)
        Bsb, b_B, ones33, b_o = emit_bias_rows(P, nc, es, rb_d, 3 * NHM)
        oh_sb = sb(nc, es, "oh_sb", [33, 3, 384], BF16)
        b_oh = Buf()
        for g in range(3):
            P.dma("sp", oh_sb[:, g, :], oh_d[g], (), (b_oh,))
        brep = sb(nc, es, "brep", [33, 128], BF16)
        b_brep = Buf()
        frow = sb(nc, es, "frow", [128, 384], BF16)
        b_frow = Buf()
        T = sb(nc, es, "Ttab", [128, 3 * NHM, 256], BF16)
        b_T = [Buf() for _ in range(3 * NHM)]
        psS = [(ps(nc, es, f"psS{i}", [128, 512]), Buf()) for i in range(2)]
        psO = [(ps(nc, es, f"psO{i}", [128, 512]), Buf()) for i in range(2)]
        psB, b_psB = ps(nc, es, "psB", [128, 512]), Buf()
        pst, b_pst = ps(nc, es, "pst", [128, 512]), Buf()
        for g in range(3):
            for hh in range(NHM):
                col = g * NHM + hh
                emit_toeplitz(P, nc, Bsb, b_B, ones33, b_o, col, oh_sb[:, g, :], b_oh, 384, brep, b_brep, frow, b_frow,
                              scr_d[col], pst, b_pst, T[:, col, :], 256, b_T[col])
        Vg = sb(nc, es, "Vg", [128, 3, NB, NHM, 65], BF16)
        b_V = [Buf() for _ in range(3)]
        for g, (win, r) in enumerate(DSW_GROUPS):
            P.op("pool", lambda: nc.gpsimd.memset(Vg[:, g], 1.0), (), (b_V[g],))
            nb = NB // r
            for c in range(r):
                for hh in range(NHM):
                    src = bass.AP(v_d.tensor, v_d.offset + c * 3 * NHM * 64 + g * NHM * 64 + hh * 64,
                                  [[r * 3 * NHM * 64, 128], [128 * r * 3 * NHM * 64, nb], [1, 64]])
                    P.dma("act", Vg[:, g, c * nb:(c + 1) * nb, hh, 0:64], src, (), (b_V[g],))
        QT = [sb(nc, es, f"QT{i}", [128, S], BF16) for i in range(2)]
        KT = [sb(nc, es, f"KT{i}", [128, S], BF16) for i in range(2)]
        b_QT = [Buf(), Buf()]
        b_KT = [Buf(), Buf()]
        acc = [sb(nc, es, f"acc{i}", [65, S], F32) for i in range(2)]
        b_acc = [Buf(), Buf()]
        PT = [sb(nc, es, f"PT{i}", [128, 256], BF16) for i in range(3)]
        b_PT = [Buf() for _ in range(3)]
        o_sb = [sb(nc, es, f"o_sb{i}", [64, 512], BF16) for i in range(2)]
        b_osb = [Buf(), Buf()]
        li = 0
        ti = 0
        oi = 0
        for pair in range(NHM // 2):
            for g, (win, r) in enumerate(DSW_GROUPS):
                nb = NB // r
                slot = li % 2
                li += 1
                qt, kt = QT[slot], KT[slot]
                P.dma("sp", qt[:], q_d[g, pair * 128:(pair + 1) * 128, :], (), (b_QT[slot],))
                P.dma("sp", kt[:], k_d[g, pair * 128:(pair + 1) * 128, :], (), (b_KT[slot],))
                tiles = [(hl, c, j) for hl in range(2) for c in range(r) for j in range(nb)]

                def emit_S(tl, k):
                    hl, c, j = tl
                    hh = pair * 2 + hl
                    rows = slice(hl * 64, (hl + 1) * 64)
                    nqb = 2 if j + 1 < nb else 1
                    k0 = c + 128 * j * r
                    pS, b_pS = psS[k % 2]
                    P.mm(pS[:, 0:128 * nqb], kt[rows, sstep(k0, 128, r)], qt[rows, sstep(k0, 128 * nqb, r)],
                         True, False, (b_KT[slot], b_QT[slot]), (b_pS,))
                    P.mm(pS[:, 0:128 * nqb], A["ident"][:], T[:, g * NHM + hh, 0:128 * nqb], False, True,
                         (A["b_ident"], b_T[g * NHM + hh]), (b_pS,))
                    pt, b_pt = PT[k % 3], b_PT[k % 3]
                    P.op("act", lambda: nc.scalar.activation(out=pt[:, 0:128 * nqb], in_=pS[:, 0:128 * nqb],
                                                             func=AF.Exp, scale=SCALE),
                         (b_pS,), (b_pt,))

                def emit_PV(tl, k):
                    hl, c, j = tl
                    hh = pair * 2 + hl
                    nqb = 2 if j + 1 < nb else 1
                    pt, b_pt = PT[k % 3], b_PT[k % 3]
                    for qi in range(nqb):
                        i = j + qi
                        pO, b_pO = psO[i % 2]
                        start = (i == 0) or (j == i - 1)
                        stop = (j == i)
                        P.mm(pO[0:65, 0:128], Vg[:, g, c * nb + j, hh, :], pt[:, qi * 128:(qi + 1) * 128],
                             start, stop, (b_V[g], b_pt), (b_pO,), last=True)
                        if stop:
                            p0 = c + 128 * i * r
                            dst = acc[hl][:, sstep(p0, 128, r)]
                            if g == 0:
                                P.op("dve", lambda: nc.vector.tensor_copy(out=dst, in_=pO[0:65, 0:128]),
                                     (b_pO,), (b_acc[hl],))
                            else:
                                P.op("dve", lambda: nc.vector.tensor_tensor(out=dst, in0=pO[0:65, 0:128], in1=dst,
                                                                            op=ALU.add),
                                     (b_pO, b_acc[hl]), (b_acc[hl],))

                n = len(tiles)
                for k in range(n + 1):
                    if k < n:
                        emit_S(tiles[k], ti + k)
                    if k > 0:
                        emit_PV(tiles[k - 1], ti + k - 1)
                ti += n
            for hl in range(2):
                hh = pair * 2 + hl
                for ch in range(S // 512):
                    cs = slice(ch * 512, (ch + 1) * 512)
                    emit_normalize(P, nc, acc[hl][:, cs], b_acc[hl], 512, A["E"], A["b_E"], A["R"], A["b_R"],
                                   psB, b_psB, o_sb[oi % 2], b_osb[oi % 2], o_d[hh * 64:(hh + 1) * 64, cs])
                    oi += 1
        P.finish()
    return nc


def moba_consts(S):
    nblk = S // 256
    kind = np.zeros((16, S), np.float32)
    for n in range(nblk):
        kind[n, n * 256:(n + 1) * 256] = 1.0
    nqt = S // 128
    cm = np.full((nqt, 16), -1e30, np.float32)
    for qt in range(nqt):
        cm[qt, :qt // 2] = 0.0
        cm[qt, qt // 2] = 1e30
    cm = np.broadcast_to(cm.reshape(1, nqt * 16), (128, nqt * 16)).copy()
    return kind.astype(ml_dtypes.bfloat16), cm


def build_moba(S, NH):
    nc = bass.Bass("TRN2", target_bir_lowering=False)
    NQT = S // 128
    NQC = S // 512
    q_d = nc.dram_tensor("qT", [NH * 64, S], BF16, kind="ExternalInput").ap()
    k_d = nc.dram_tensor("kT", [NH * 64, S], BF16, kind="ExternalInput").ap()
    v_d = nc.dram_tensor("v", [S, NH * 64], BF16, kind="ExternalInput").ap()
    rb_d = nc.dram_tensor("rb", [32, NH], F32, kind="ExternalInput").ap()
    oh_d = nc.dram_tensor("oh", [33, MOBA_L], BF16, kind="ExternalInput").ap()
    kind_d = nc.dram_tensor("kind", [16, S], BF16, kind="ExternalInput").ap()
    cm_d = nc.dram_tensor("cm", [128, NQT * 16], F32, kind="ExternalInput").ap()
    o_d = nc.dram_tensor("oT", [NH * 64, S], BF16, kind="ExternalOutput").ap()
    scr_d = nc.dram_tensor("scr", [NH, 128, MOBA_L], BF16, kind="Internal").ap()
    with contextlib.ExitStack() as es:
        P = Prog(nc, es)
        A = emit_attn_consts(P, nc, es)
        Bsb, b_B, ones33, b_o = emit_bias_rows(P, nc, es, rb_d, NH)
        oh_sb = sb(nc, es, "oh_sb", [33, MOBA_L], BF16)
        b_oh = Buf()
        P.dma("sp", oh_sb[:], oh_d, (), (b_oh,))
        cm_sb = sb(nc, es, "cm_sb", [128, NQT * 16], F32)
        b_cm = Buf()
        P.dma("sp", cm_sb[:], cm_d, (), (b_cm,))
        brep = sb(nc, es, "brep", [33, 128], BF16)
        b_brep = Buf()
        frow = sb(nc, es, "frow", [128, MOBA_L], BF16)
        b_frow = Buf()
        TB = [sb(nc, es, f"TB{i}", [128, MOBA_W], BF16) for i in range(2)]
        b_TB = [Buf(), Buf()]
        psS = [(ps(nc, es, f"psS{i}", [128, 512]), Buf()) for i in range(2)]
        psO = [(ps(nc, es, f"psO{i}", [128, 512]), Buf()) for i in range(2)]
        psB, b_psB = ps(nc, es, "psB", [128, 512]), Buf()
        pst, b_pst = ps(nc, es, "pst", [128, 512]), Buf()
        psG, b_psG = ps(nc, es, "psG", [128, 512]), Buf()
        psT, b_psT = ps(nc, es, "psT", [128, 1024], BF16), Buf()
        Vh = sb(nc, es, "Vh", [128, NQT, NH, 65], BF16)
        b_V = Buf()
        P.op("pool", lambda: nc.gpsimd.memset(Vh[:], 1.0), (), (b_V,))
        for h in range(NH):
            src = bass.AP(v_d.tensor, v_d.offset + h * 64, [[NH * 64, 128], [128 * NH * 64, NQT], [1, 64]])
            P.dma("act", Vh[:, :, h, 0:64], src, (), (b_V,))
        QTa = [sb(nc, es, f"QTa{i}", [128, S], BF16) for i in range(2)]
        KTa = [sb(nc, es, f"KTa{i}", [128, S], BF16) for i in range(2)]
        b_Q = [Buf(), Buf()]
        b_K = [Buf(), Buf()]
        for i in range(2):
            P.op("pool", lambda: nc.gpsimd.memset(QTa[i][0:64, :], 0.0), (), (b_Q[i],))
            P.op("pool", lambda: nc.gpsimd.memset(KTa[i][0:64, :], 0.0), (), (b_K[i],))
            P.dma("sp", KTa[i][0:16, :], kind_d, (), (b_K[i],))
        km = sb(nc, es, "km", [128, 16], F32)
        kmh = sb(nc, es, "kmh", [128, 16], BF16)
        kml = sb(nc, es, "kml", [128, 16], BF16)
        kmr = sb(nc, es, "kmr", [128, 16], F32)
        b_km, b_kmh, b_kml, b_kmr = Buf(), Buf(), Buf(), Buf()
        gm = sb(nc, es, "gm", [128, NQT * 16], F32)
        b_gm = Buf()
        top8 = sb(nc, es, "top8", [128, NQT * 8], F32)
        b_top8 = Buf()
        thr = sb(nc, es, "thr", [128, NQT], F32)
        b_thr = Buf()
        mb = sb(nc, es, "mb", [128, NQT * 16], BF16)
        b_mb = Buf()
        PT = [sb(nc, es, f"PT{i}", [128, 512], BF16) for i in range(3)]
        b_PT = [Buf() for _ in range(3)]
        accs = [sb(nc, es, f"accs{i}", [65, 512], F32) for i in range(2)]
        b_accs = [Buf(), Buf()]
        o_sb = [sb(nc, es, f"o_sb{i}", [64, 512], BF16) for i in range(2)]
        b_osb = [Buf(), Buf()]
        ti = 0
        oi = 0
        for h in range(NH):
            sl = h % 2
            qa, ka = QTa[sl], KTa[sl]
            P.dma("sp", qa[64:128, :], q_d[h * 64:(h + 1) * 64, :], (), (b_Q[sl],))
            P.dma("sp", ka[64:128, :], k_d[h * 64:(h + 1) * 64, :], (), (b_K[sl],))
            emit_toeplitz(P, nc, Bsb, b_B, ones33, b_o, h, oh_sb, b_oh, MOBA_L, brep, b_brep, frow, b_frow,
                          scr_d[h], pst, b_pst, TB[sl][:], MOBA_W, b_TB[sl])
            P.op("dve", lambda: nc.vector.tensor_reduce(out=km[64:128, 0:S // 256],
                                                        in_=ka[64:128, :].rearrange("p (n k) -> p n k", k=256),
                                                        axis=AX.X, op=ALU.add),
                 (b_K[sl],), (b_km,))
            if S // 256 < 16:
                P.op("dve", lambda: nc.vector.memset(km[64:128, S // 256:16], 0.0), (), (b_km,))
            P.op("dve", lambda: nc.vector.tensor_scalar(out=kmh[64:128, :], in0=km[64:128, :], scalar1=1.0 / 256.0,
                                                        scalar2=None, op0=ALU.mult),
                 (b_km,), (b_kmh,))
            P.op("dve", lambda: nc.vector.scalar_tensor_tensor(out=kmr[64:128, :], in0=km[64:128, :], scalar=1.0 / 256.0,
                                                               in1=kmh[64:128, :], op0=ALU.mult, op1=ALU.subtract),
                 (b_km, b_kmh), (b_kmr,))
            P.op("dve", lambda: nc.vector.tensor_copy(out=kml[64:128, :], in_=kmr[64:128, :]), (b_kmr,), (b_kml,))
            for qt in range(NQT):
                P.mm(psG[:, qt * 16:(qt + 1) * 16], qa[64:128, qt * 128:(qt + 1) * 128], kmh[64:128, :], True, False,
                     (b_Q[sl], b_kmh), (b_psG,))
                P.mm(psG[:, qt * 16:(qt + 1) * 16], qa[64:128, qt * 128:(qt + 1) * 128], kml[64:128, :], False, True,
                     (b_Q[sl], b_kml), (b_psG,), last=(qt == NQT - 1))
            P.op("dve", lambda: nc.vector.tensor_tensor(out=gm[:], in0=psG[:, 0:NQT * 16], in1=cm_sb[:], op=ALU.add),
                 (b_psG, b_cm), (b_gm,))
            for qt in range(NQT):
                P.op("dve", lambda: nc.vector.max(out=top8[:, qt * 8:(qt + 1) * 8], in_=gm[:, qt * 16:(qt + 1) * 16]),
                     (b_gm,), (b_top8,))
            P.op("dve", lambda: nc.vector.tensor_scalar(out=thr[:], in0=top8[:, 3:NQT * 8:8], scalar1=-1e29,
                                                        scalar2=None, op0=ALU.max),
                 (b_top8,), (b_thr,))
            for qt in range(NQT):
                P.op("dve", lambda: nc.vector.tensor_scalar(out=mb[:, qt * 16:(qt + 1) * 16],
                                                            in0=gm[:, qt * 16:(qt + 1) * 16],
                                                            scalar1=thr[:, qt:qt + 1], scalar2=NEGB,
                                                            op0=ALU.is_lt, op1=ALU.mult),
                     (b_gm, b_thr), (b_mb,))
            for q8 in range(0, NQT, 8):
                n8 = min(8, NQT - q8)
                for qq in range(n8):
                    qt = q8 + qq
                    P.op("pe", lambda: nc.tensor.transpose(out=psT[0:16, qq * 128:(qq + 1) * 128],
                                                           in_=mb[:, qt * 16:(qt + 1) * 16], identity=A["ident"][:]),
                         (b_mb, A["b_ident"]), (b_psT,), inc=(qq == n8 - 1))
                P.op("act", lambda: nc.scalar.copy(out=qa[0:16, q8 * 128:(q8 + n8) * 128], in_=psT[0:16, 0:n8 * 128]),
                     (b_psT,), (b_Q[sl],))
            for qc in range(NQC):
                q0 = qc * 512
                kts = list(range(4 * qc + 4))
                pO, b_pO = psO[qc % 2]

                def emit_S(kt_, k):
                    half = kt_ >= 4 * qc + 2
                    qs = q0 + 256 if half else q0
                    nq = 256 if half else 512
                    pS, b_pS = psS[k % 2]
                    P.mm(pS[:, 0:nq], ka[0:128, kt_ * 128:(kt_ + 1) * 128], qa[0:128, qs:qs + nq], True, False,
                         (b_K[sl], b_Q[sl]), (b_pS,))
                    z0 = min(qs - 128 * kt_ + 128, MOBA_W - nq)
                    P.mm(pS[:, 0:nq], A["ident"][:], TB[sl][:, z0:z0 + nq], False, True,
                         (A["b_ident"], b_TB[sl]), (b_pS,))
                    pt, b_pt = PT[k % 3], b_PT[k % 3]
                    P.op("act", lambda: nc.scalar.activation(out=pt[:, 0:nq], in_=pS[:, 0:nq], func=AF.Exp, scale=SCALE),
                         (b_pS,), (b_pt,))

                def emit_PV(kt_, k):
                    half = kt_ >= 4 * qc + 2
                    c0 = 256 if half else 0
                    nq = 256 if half else 512
                    pt, b_pt = PT[k % 3], b_PT[k % 3]
                    P.mm(pO[0:65, c0:c0 + nq], Vh[:, kt_, h, :], pt[:, 0:nq], kt_ == 0, kt_ == kts[-1],
                         (b_V, b_pt), (b_pO,), last=True)

                n = len(kts)
                for k in range(n + 1):
                    if k < n:
                        emit_S(kts[k], ti + k)
                    if k > 0:
                        emit_PV(kts[k - 1], ti + k - 1)
                ti += n
                ac, b_ac = accs[oi % 2], b_accs[oi % 2]
                P.op("dve", lambda: nc.vector.tensor_copy(out=ac[:], in_=pO[0:65, :]), (b_pO,), (b_ac,))
                emit_normalize(P, nc, ac[:, :], b_ac, 512, A["E"], A["b_E"], A["R"], A["b_R"],
                               psB, b_psB, o_sb[oi % 2], b_osb[oi % 2], o_d[h * 64:(h + 1) * 64, q0:q0 + 512])
                oi += 1
        P.finish()
    return nc


def _g_layout(g):
    return np.ascontiguousarray(np.asarray(g, np.float32).reshape(8, 128).T)


def _gain_layout(qg, kg):
    return np.ascontiguousarray(np.stack([np.tile(np.asarray(qg, np.float32), 2),
                                          np.tile(np.asarray(kg, np.float32), 2)], axis=1))


def _run(nc, in_maps):
    res = run_bass_kernel_spmd(nc, in_maps, core_ids=list(range(8)))
    return res.results


def kernel(x, rel_bias, norm_mix, norm_ffn, a_w_qkv, a_q_gain, a_k_gain, a_w_o,
           b_w_qkv, b_q_gain, b_k_gain, b_w_o, ffn_w1, ffn_w2):
    x = np.asarray(x, np.float32)
    rel_bias = np.ascontiguousarray(np.asarray(rel_bias, np.float32))
    TT = SEQ // 2
    hT = [np.ascontiguousarray(x[c // 2, (c % 2) * TT:(c % 2 + 1) * TT].T) for c in range(8)]
    oh_a = np.stack([onehot_dsw(r) for _, r in DSW_GROUPS])
    oh_b = onehot_moba()
    kind, cm = moba_consts(SEQ)
    for layer in range(2):
        is_a = layer == 0
        w_qkv = np.ascontiguousarray(np.asarray(a_w_qkv[0] if is_a else b_w_qkv[0], np.float32))
        CQK, CV = (3072, 1536) if is_a else (2048, 1024)
        gain = _gain_layout(a_q_gain[0], a_k_gain[0]) if is_a else _gain_layout(b_q_gain[0], b_k_gain[0])
        g_mix = _g_layout(norm_mix[layer])
        nc1 = build_lin_in(TT, CQK, CV)
        r1 = _run(nc1, [{"hT": hT[c], "g": g_mix, "w": w_qkv, "gain": gain} for c in range(8)])
        in2 = []
        for c in range(8):
            b, r2 = c // 2, c % 2
            qk = np.concatenate([r1[2 * b]["qkT"], r1[2 * b + 1]["qkT"]], axis=1)
            v = np.concatenate([r1[2 * b]["v"], r1[2 * b + 1]["v"]], axis=0)
            if is_a:
                qT = np.stack([qk[g * 512 + r2 * 256:g * 512 + (r2 + 1) * 256] for g in range(3)])
                kT = np.stack([qk[1536 + g * 512 + r2 * 256:1536 + g * 512 + (r2 + 1) * 256] for g in range(3)])
                vv = np.stack([v[:, g * 512 + r2 * 256:g * 512 + (r2 + 1) * 256] for g in range(3)], axis=1)
                rb = np.concatenate([rel_bias[:, g * 8 + r2 * 4:g * 8 + r2 * 4 + 4] for g in range(3)], axis=1)
                in2.append({"qT": np.ascontiguousarray(qT), "kT": np.ascontiguousarray(kT),
                            "v": np.ascontiguousarray(vv), "rb": np.ascontiguousarray(rb), "oh": oh_a})
            else:
                in2.append({"qT": np.ascontiguousarray(qk[r2 * 512:(r2 + 1) * 512]),
                            "kT": np.ascontiguousarray(qk[1024 + r2 * 512:1024 + (r2 + 1) * 512]),
                            "v": np.ascontiguousarray(v[:, r2 * 512:(r2 + 1) * 512]),
                            "rb": np.ascontiguousarray(rel_bias[:, r2 * 8:(r2 + 1) * 8]),
                            "oh": oh_b, "kind": kind, "cm": cm})
        nc2 = build_dsw(SEQ, 4) if is_a else build_moba(SEQ, 8)
        r2_ = _run(nc2, in2)
        KO = 512 if is_a else 1024
        w_o = np.ascontiguousarray(np.asarray(a_w_o[0] if is_a else b_w_o[0], np.float32))
        w1 = np.ascontiguousarray(np.asarray(ffn_w1[layer], np.float32))
        w2 = np.ascontiguousarray(np.asarray(ffn_w2[layer], np.float32))
        g_ffn = _g_layout(norm_ffn[layer])
        in3 = []
        for c in range(8):
            b, r = c // 2, c % 2
            oT = np.concatenate([r2_[2 * b]["oT"], r2_[2 * b + 1]["oT"]], axis=0)
            in3.append({"hT": hT[c], "oT": np.ascontiguousarray(oT[:, r * TT:(r + 1) * TT]), "g": g_ffn,
                        "wo": w_o, "w1": w1, "w2": w2})
        nc3 = build_lin_out(TT, KO)
        r3 = _run(nc3, in3)
        hT = [np.asarray(r3[c]["hT_out"], np.float32) for c in range(8)]
    out = np.empty((BATCH, SEQ, D), np.float32)
    for c in range(8):
        out[c // 2, (c % 2) * TT:(c % 2 + 1) * TT] = hT[c].T
    return out
```

```python
import contextlib
import numpy as np
import ml_dtypes
import concourse.bass as bass
import concourse.mybir as mybir
from concourse.bass_utils import run_bass_kernel_spmd

F32 = mybir.dt.float32
BF16 = mybir.dt.bfloat16
ALU = mybir.AluOpType
AF = mybir.ActivationFunctionType
AX = mybir.AxisListType

D = 1024
SEQ = 4096
BATCH = 4
HD = 64
EPS = 1e-6
SCALE = HD ** -0.5
NEGB = -30000.0
SAME_ENGINE_SYNC = True


class Buf:
    __slots__ = ("w", "r", "name")

    def __init__(self, name=""):
        self.w = None
        self.r = {}
        self.name = name


class Prog:
    NDQ = 6

    def __init__(self, nc, es):
        self.nc = nc
        self.es = es
        self.eng = {"pe": nc.tensor, "act": nc.scalar, "dve": nc.vector,
                    "pool": nc.gpsimd, "sp": nc.sync}
        self.sem = {}
        self.cnt = {}
        for k in ("pe", "act", "dve", "pool"):
            self.sem[k] = es.enter_context(nc.semaphore(f"s_{k}"))
            self.cnt[k] = 0
        self.waited = {k: {} for k in self.eng}
        self.dq = {}
        for q in ("sp", "pool", "act"):
            sems = [es.enter_context(nc.semaphore(f"d_{q}{i}")) for i in range(self.NDQ)]
            self.dq[q] = {"sems": sems, "val": [0] * self.NDQ, "idx": 0}
        self.pe_open = False
        self.out_tickets = []

    def need(self, e, tk):
        if tk is None:
            return
        sem, val = tk
        key = sem.num
        if self.waited[e].get(key, 0) >= val:
            return
        self.eng[e].wait_ge(sem, val)
        self.waited[e][key] = val

    def _deps(self, e, reads, writes):
        own = self.sem.get(e)
        tks = []
        for b in reads:
            if b.w is not None:
                tks.append(b.w)
        for b in writes:
            if b.w is not None:
                tks.append(b.w)
            tks.extend(b.r.values())
        for tk in tks:
            if own is not None and tk[0].num == own.num:
                if e == "pe" or not SAME_ENGINE_SYNC:
                    continue
            self.need(e, tk)

    def _record(self, tk, reads, writes):
        key = tk[0].num
        for b in reads:
            b.r[key] = tk
        for b in writes:
            b.w = tk
            b.r = {}

    def op(self, e, fn, reads=(), writes=(), inc=True):
        self._deps(e, reads, writes)
        ins = fn()
        if inc:
            self.cnt[e] += 1
            ins.then_inc(self.sem[e], 1)
            tk = (self.sem[e], self.cnt[e])
            if e == "pe":
                self.pe_open = False
        else:
            assert e == "pe"
            tk = (self.sem[e], self.cnt[e] + 1)
            self.pe_open = True
        self._record(tk, reads, writes)
        return tk

    def dma(self, q, out, in_, reads=(), writes=(), is_output=False, **kw):
        d = self.dq[q]
        i = d["idx"] % self.NDQ
        d["idx"] += 1
        if d["val"][i] > 0:
            self.need(q, (d["sems"][i], d["val"][i]))
        self._deps(q, reads, writes)
        ins = self.eng[q].dma_start(out=out, in_=in_, **kw)
        d["val"][i] += 16
        ins.then_inc(d["sems"][i], 16)
        tk = (d["sems"][i], d["val"][i])
        self._record(tk, reads, writes)
        if is_output:
            self.out_tickets.append(tk)
        return tk

    def finish(self):
        assert not self.pe_open
        for q in ("sp", "pool", "act"):
            d = self.dq[q]
            for i in range(self.NDQ):
                if d["val"][i] > 0:
                    self.need("sp", (d["sems"][i], d["val"][i]))
        for e in ("pe", "act", "dve", "pool"):
            if self.cnt[e] > 0:
                self.need("sp", (self.sem[e], self.cnt[e]))

    def barrier(self):
        assert not self.pe_open
        for e in self.eng:
            for q in ("sp", "pool", "act"):
                d = self.dq[q]
                for i in range(self.NDQ):
                    if d["val"][i] > 0:
                        self.need(e, (d["sems"][i], d["val"][i]))
            for x in ("pe", "act", "dve", "pool"):
                if x != e and self.cnt[x] > 0:
                    self.need(e, (self.sem[x], self.cnt[x]))

    def mm(self, out, lhsT, rhs, start, stop, reads, writes, last=None):
        if last is None:
            last = stop
        return self.op("pe", lambda: self.nc.tensor.matmul(out, lhsT, rhs, start=start, stop=stop),
                       reads, writes, inc=last)


def sstep(start, n, step):
    return slice(start, start + (n - 1) * step + 1, step)


_UID = [0]


def _uname(name):
    _UID[0] += 1
    return f"{name}_{_UID[0]}"


def sb(nc, es, name, shape, dt):
    return es.enter_context(nc.sbuf_tensor(_uname(name), shape, dt))


def ps(nc, es, name, shape, dt=F32):
    return es.enter_context(nc.psum_tensor(_uname(name), shape, dt))


def emit_consts(P, nc, es):
    c = {}
    c["avg1024"] = sb(nc, es, "avg1024", [128, 128], BF16)
    c["avg64"] = sb(nc, es, "avg64", [128, 128], BF16)
    c["b_avg1024"] = Buf()
    c["b_avg64"] = Buf()
    P.op("dve", lambda: nc.vector.memset(c["avg1024"][:], 1.0 / 1024.0), (), (c["b_avg1024"],))
    P.op("dve", lambda: nc.vector.memset(c["avg64"][:], 0.0), (), (c["b_avg64"],))
    P.op("dve", lambda: nc.vector.memset(c["avg64"][0:64, 0:64], 1.0 / 64.0), (), (c["b_avg64"],))
    P.op("dve", lambda: nc.vector.memset(c["avg64"][64:128, 64:128], 1.0 / 64.0), (), (c["b_avg64"],))
    c["eps"] = sb(nc, es, "eps_c", [128, 1], F32)
    c["b_eps"] = Buf()
    P.op("dve", lambda: nc.vector.memset(c["eps"][:], EPS), (), (c["b_eps"],))
    return c


def emit_rsqrt(P, nc, C, rstd, b_rstd, pt, b_pt):
    P.op("act", lambda: nc.scalar.activation(out=rstd[:], in_=pt[:], func=AF.Sqrt, bias=C["eps"][:, 0:1]),
         (b_pt, C["b_eps"]), (b_rstd,))
    P.op("dve", lambda: nc.vector.reciprocal(out=rstd[:], in_=rstd[:]), (b_rstd,), (b_rstd,))


def emit_rmsnorm(P, nc, C, hT, b_h, g_sb, b_g, uT, b_u, TT, psums, tmp):
    NT = TT // 512
    sq, b_sq, rstd, b_rstd = tmp["sq"], tmp["b_sq"], tmp["rstd"], tmp["b_rstd"]
    for t in range(NT):
        ts = slice(t * 512, (t + 1) * 512)
        pt, b_pt = psums[t % 2]
        for c in range(8):
            i = c % 2
            P.op("act", lambda: nc.scalar.activation(out=sq[i][:], in_=hT[:, c, ts], func=AF.Square),
                 (b_h[c][t],), (b_sq[i],))
            P.mm(pt[:], C["avg1024"][:], sq[i][:], c == 0, c == 7,
                 (C["b_avg1024"], b_sq[i]), (b_pt,), last=True)
        emit_rsqrt(P, nc, C, rstd, b_rstd, pt, b_pt)
        for c in range(8):
            e = "dve"
            eng = nc.vector
            P.op(e, lambda: eng.scalar_tensor_tensor(out=uT[:, c, ts], in0=hT[:, c, ts],
                                                     scalar=g_sb[:, c:c + 1], in1=rstd[:],
                                                     op0=ALU.mult, op1=ALU.mult),
                 (b_h[c][t], b_g, b_rstd), (b_u[c][t],))


def build_lin_in(TT, CQK, CV):
    nc = bass.Bass("TRN2", target_bir_lowering=False)
    CW = CQK + CV
    hT_d = nc.dram_tensor("hT", [D, TT], F32, kind="ExternalInput").ap()
    g_d = nc.dram_tensor("g", [128, 8], F32, kind="ExternalInput").ap()
    w_d = nc.dram_tensor("w", [D, CW], F32, kind="ExternalInput").ap()
    gain_d = nc.dram_tensor("gain", [128, 2], F32, kind="ExternalInput").ap()
    qk_d = nc.dram_tensor("qkT", [CQK, TT], BF16, kind="ExternalOutput").ap()
    v_d = nc.dram_tensor("v", [TT, CV], BF16, kind="ExternalOutput").ap()
    with contextlib.ExitStack() as es0:
        P = Prog(nc, es0)
        emit_lin_in(P, nc, TT, CQK, CV, hT_d, g_d, w_d, gain_d, qk_d, v_d)
        P.finish()
    return nc


def emit_lin_in(P, nc, TT, CQK, CV, hT_d, g_d, w_d, gain_d, qk_d, v_d):
    CW = CQK + CV
    NT = TT // 512
    with contextlib.ExitStack() as es:
        C = emit_consts(P, nc, es)
        hT = sb(nc, es, "hT_sb", [128, 8, TT], F32)
        uT = sb(nc, es, "uT_sb", [128, 8, TT], BF16)
        g_sb = sb(nc, es, "g_sb", [128, 8], F32)
        gain_sb = sb(nc, es, "gain_sb", [128, 2], F32)
        b_h = [[Buf() for _ in range(NT)] for _ in range(8)]
        b_u = [[Buf() for _ in range(NT)] for _ in range(8)]
        b_g, b_gain = Buf(), Buf()
        tmp = {"sq": [sb(nc, es, f"sq{i}", [128, 512], BF16) for i in range(2)],
               "b_sq": [Buf(), Buf()],
               "rstd": sb(nc, es, "rstd", [128, 512], F32), "b_rstd": Buf()}
        psums = [(ps(nc, es, f"ps{i}", [128, 512]), Buf()) for i in range(6)]
        P.dma("sp", g_sb[:], g_d, (), (b_g,))
        P.dma("sp", gain_sb[:], gain_d, (), (b_gain,))
        for c in range(8):
            for t in range(NT):
                P.dma("sp", hT[:, c, t * 512:(t + 1) * 512], hT_d[c * 128:(c + 1) * 128, t * 512:(t + 1) * 512],
                      (), (b_h[c][t],))
        emit_rmsnorm(P, nc, C, hT, b_h, g_sb, b_g, uT, b_u, TT, psums[0:2], tmp)

        NG = CW // 512
        wb = [sb(nc, es, f"wb{i}", [128, 8, 512], BF16) for i in range(2)]
        b_wb = [Buf(), Buf()]
        osb = [sb(nc, es, f"osb{i}", [128, 512], BF16) for i in range(3)]
        b_osb = [Buf() for _ in range(3)]
        oi = 0
        w_v = w_d.rearrange("(c p) n -> p c n", p=128)

        def load_w(gi):
            for c in range(8):
                P.dma("pool", wb[gi % 2][:, c, :], w_v[:, c, gi * 512:(gi + 1) * 512], (), (b_wb[gi % 2],))

        load_w(0)
        pi = 2
        for gi in range(NG):
            if gi + 1 < NG:
                load_w(gi + 1)
            w = wb[gi % 2]
            bw = b_wb[gi % 2]
            if gi * 512 < CQK:
                isq = 0 if gi * 512 < CQK // 2 else 1
                for j in range(4):
                    col0 = gi * 512 + j * 128
                    for t in range(NT):
                        ts = slice(t * 512, (t + 1) * 512)
                        pa, b_pa = psums[2 + (pi % 2)]
                        pb, b_pb = psums[4 + (pi % 2)]
                        pi += 1
                        for c in range(8):
                            P.mm(pa[:], w[:, c, j * 128:(j + 1) * 128], uT[:, c, ts], c == 0, c == 7,
                                 (bw, b_u[c][t]), (b_pa,))
                        i = pi % 2
                        sq, b_sq = tmp["sq"][i], tmp["b_sq"][i]
                        P.op("act", lambda: nc.scalar.activation(out=sq[:], in_=pa[:], func=AF.Square),
                             (b_pa,), (b_sq,))
                        P.mm(pb[:], C["avg64"][:], sq[:], True, True, (C["b_avg64"], b_sq), (b_pb,))
                        rstd, b_rstd = tmp["rstd"], tmp["b_rstd"]
                        emit_rsqrt(P, nc, C, rstd, b_rstd, pb, b_pb)
                        o, b_o = osb[oi % 3], b_osb[oi % 3]
                        oi += 1
                        P.op("dve", lambda: nc.vector.scalar_tensor_tensor(out=o[:], in0=pa[:],
                                                                           scalar=gain_sb[:, isq:isq + 1], in1=rstd[:],
                                                                           op0=ALU.mult, op1=ALU.mult),
                             (b_pa, b_gain, b_rstd), (b_o,))
                        P.dma("sp", qk_d[col0:col0 + 128, ts], o[:], (b_o,), (), is_output=True)
            else:
                vc0 = gi * 512 - CQK
                for tt in range(TT // 128):
                    t = tt // 4
                    tsl = slice(tt * 128, (tt + 1) * 128)
                    pa, b_pa = psums[2 + (pi % 2)]
                    pi += 1
                    for c in range(8):
                        P.mm(pa[:], uT[:, c, tsl], w[:, c, :], c == 0, c == 7, (bw, b_u[c][t]), (b_pa,))
                    o, b_o = osb[oi % 3], b_osb[oi % 3]
                    oi += 1
                    P.op("act", lambda: nc.scalar.copy(out=o[:], in_=pa[:]), (b_pa,), (b_o,))
                    P.dma("sp", v_d[tsl, vc0:vc0 + 512], o[:], (b_o,), (), is_output=True)
        P.barrier()


def build_lin_out(TT, KO, DFF=4096):
    nc = bass.Bass("TRN2", target_bir_lowering=False)
    hT_d = nc.dram_tensor("hT", [D, TT], F32, kind="ExternalInput").ap()
    oT_d = nc.dram_tensor("oT", [KO, TT], BF16, kind="ExternalInput").ap()
    g_d = nc.dram_tensor("g", [128, 8], F32, kind="ExternalInput").ap()
    wo_d = nc.dram_tensor("wo", [KO, D], F32, kind="ExternalInput").ap()
    w1_d = nc.dram_tensor("w1", [D, DFF], F32, kind="ExternalInput").ap()
    w2_d = nc.dram_tensor("w2", [DFF, D], F32, kind="ExternalInput").ap()
    out_d = nc.dram_tensor("hT_out", [D, TT], F32, kind="ExternalOutput").ap()
    with contextlib.ExitStack() as es0:
        P = Prog(nc, es0)
        emit_lin_out(P, nc, TT, KO, DFF, hT_d, oT_d, g_d, wo_d, w1_d, w2_d, out_d)
        P.finish()
    return nc


def emit_lin_out(P, nc, TT, KO, DFF, hT_d, oT_d, g_d, wo_d, w1_d, w2_d, out_d):
    NT = TT // 512
    KC = KO // 128
    NFG = DFF // 512
    with contextlib.ExitStack() as es:
        C = emit_consts(P, nc, es)
        hT = sb(nc, es, "hT_sb", [128, 8, TT], F32)
        uT = sb(nc, es, "uT_sb", [128, 8, TT], BF16)
        aT = sb(nc, es, "aT_sb", [128, 4, TT], BF16)
        g_sb = sb(nc, es, "g_sb", [128, 8], F32)
        b_h = [[Buf() for _ in range(NT)] for _ in range(8)]
        b_u = [[Buf() for _ in range(NT)] for _ in range(8)]
        b_a = [[Buf() for _ in range(NT)] for _ in range(4)]
        b_g = Buf()
        tmp = {"sq": [sb(nc, es, f"sq{i}", [128, 512], BF16) for i in range(2)],
               "b_sq": [Buf(), Buf()],
               "rstd": sb(nc, es, "rstd", [128, 512], F32), "b_rstd": Buf()}
        rl = [sb(nc, es, f"rl{i}", [128, 512], F32) for i in range(2)]
        b_rl = [Buf(), Buf()]
        psums = [(ps(nc, es, f"ps{i}", [128, 512]), Buf()) for i in range(6)]
        wb = [sb(nc, es, f"wb{i}", [128, 8, 512], BF16) for i in range(2)]
        b_wb = [Buf(), Buf()]
        w2b = [sb(nc, es, f"w2b{i}", [128, 4, D], BF16) for i in range(2)]
        b_w2b = [Buf(), Buf()]
        P.dma("sp", g_sb[:], g_d, (), (b_g,))
        for c in range(8):
            for t in range(NT):
                P.dma("sp", hT[:, c, t * 512:(t + 1) * 512], hT_d[c * 128:(c + 1) * 128, t * 512:(t + 1) * 512],
                      (), (b_h[c][t],))
        for c in range(KC):
            for t in range(NT):
                P.dma("act", uT[:, c, t * 512:(t + 1) * 512], oT_d[c * 128:(c + 1) * 128, t * 512:(t + 1) * 512],
                      (), (b_u[c][t],))
        wo_v = wo_d.rearrange("(c p) n -> p c n", p=128)
        w1_v = w1_d.rearrange("(c p) n -> p c n", p=128)
        w2_v = w2_d.rearrange("(f p) n -> p f n", p=128)
        wi = 0

        def load_wo(gi, slot):
            for c in range(KC):
                P.dma("pool", wb[slot][:, c, :], wo_v[:, c, gi * 512:(gi + 1) * 512], (), (b_wb[slot],))

        def load_w1(fg, slot):
            for c in range(8):
                P.dma("pool", wb[slot][:, c, :], w1_v[:, c, fg * 512:(fg + 1) * 512], (), (b_wb[slot],))

        def load_w2(fg, slot):
            for f in range(4):
                P.dma("pool", w2b[slot][:, f, :], w2_v[:, fg * 4 + f, :], (), (b_w2b[slot],))

        load_wo(0, 0)
        load_wo(1, 1)
        pi = 0
        for gi in range(2):
            w, bw = wb[gi], b_wb[gi]
            for j in range(4):
                cj = gi * 4 + j
                for t in range(NT):
                    ts = slice(t * 512, (t + 1) * 512)
                    pa, b_pa = psums[2 + (pi % 4)]
                    pi += 1
                    for c in range(KC):
                        P.mm(pa[:], w[:, c, j * 128:(j + 1) * 128], uT[:, c, ts], c == 0, c == KC - 1,
                             (bw, b_u[c][t]), (b_pa,))
                    P.op("dve", lambda: nc.vector.tensor_tensor(out=hT[:, cj, ts], in0=pa[:], in1=hT[:, cj, ts],
                                                                op=ALU.add),
                         (b_pa, b_h[cj][t]), (b_h[cj][t],))
        load_w1(0, 0)
        load_w2(0, 0)
        emit_rmsnorm(P, nc, C, hT, b_h, g_sb, b_g, uT, b_u, TT, psums[0:2], tmp)
        ri = 0
        for fg in range(NFG):
            slot = fg % 2
            if fg + 1 < NFG:
                load_w1(fg + 1, 1 - slot)
                load_w2(fg + 1, 1 - slot)
            w, bw = wb[slot], b_wb[slot]
            w2, bw2 = w2b[slot], b_w2b[slot]
            for f in range(4):
                for t in range(NT):
                    ts = slice(t * 512, (t + 1) * 512)
                    pa, b_pa = psums[2 + (pi % 4)]
                    pi += 1
                    for c in range(8):
                        P.mm(pa[:], w[:, c, f * 128:(f + 1) * 128], uT[:, c, ts], c == 0, c == 7,
                             (bw, b_u[c][t]), (b_pa,))
                    r, b_r = rl[ri % 2], b_rl[ri % 2]
                    ri += 1
                    P.op("act", lambda: nc.scalar.activation(out=r[:], in_=pa[:], func=AF.Relu), (b_pa,), (b_r,))
                    P.op("pool", lambda: nc.gpsimd.tensor_tensor(out=aT[:, f, ts], in0=r[:], in1=r[:], op=ALU.mult),
                         (b_r,), (b_a[f][t],))
            for j in range(8):
                for t in range(NT):
                    ts = slice(t * 512, (t + 1) * 512)
                    pa, b_pa = psums[2 + (pi % 4)]
                    pi += 1
                    for f in range(4):
                        P.mm(pa[:], w2[:, f, j * 128:(j + 1) * 128], aT[:, f, ts], f == 0, f == 3,
                             (bw2, b_a[f][t]), (b_pa,))
                    P.op("dve", lambda: nc.vector.tensor_tensor(out=hT[:, j, ts], in0=pa[:], in1=hT[:, j, ts],
                                                                op=ALU.add),
                         (b_pa, b_h[j][t]), (b_h[j][t],))
                    if fg == NFG - 1:
                        P.dma("sp", out_d[j * 128:(j + 1) * 128, ts], hT[:, j, ts], (b_h[j][t],), (), is_output=True)
        P.barrier()


def t5_bucket_np(dist):
    n = np.maximum(dist, 0)
    nf = np.maximum(n, 1).astype(np.float32)
    large = 16 + (np.log(nf / np.float32(16)) / np.float32(np.log(2048 / 16)) * np.float32(16)).astype(np.int32)
    large = np.minimum(large, 31)
    return np.where(n < 16, n, large)


def onehot_dsw(r):
    m = np.arange(384) - 127
    valid = (m >= 0) & (m <= 128)
    b = np.where(valid, t5_bucket_np(m * r), 32)
    oh = np.zeros((33, 384), np.float32)
    oh[b, np.arange(384)] = 1.0
    return oh.astype(ml_dtypes.bfloat16)


MOBA_W = 2304
MOBA_L = MOBA_W + 128


def onehot_moba():
    dd = np.arange(MOBA_L) - 255
    b = np.where(dd >= 0, t5_bucket_np(dd), 32)
    oh = np.zeros((33, MOBA_L), np.float32)
    oh[b, np.arange(MOBA_L)] = 1.0
    return oh.astype(ml_dtypes.bfloat16)


def emit_bias_rows(P, nc, es, rb_d, ncols):
    Bsb = sb(nc, es, "Bsb", [33, ncols], F32)
    b_B = Buf()
    P.op("dve", lambda: nc.vector.memset(Bsb[:], NEGB / 8.0), (), (b_B,))
    P.dma("sp", Bsb[0:32, :], rb_d, (), (b_B,))
    ones33 = sb(nc, es, "ones33", [33, 128], F32)
    b_o = Buf()
    P.op("dve", lambda: nc.vector.memset(ones33[:], 1.0), (), (b_o,))
    return Bsb, b_B, ones33, b_o


def emit_toeplitz(P, nc, Bsb, b_B, ones33, b_o, col, oh_sb, b_oh, L, brep, b_brep, frow, b_frow,
                  scr_d, pst, b_pst, out_ap, W, b_out):
    P.op("dve", lambda: nc.vector.tensor_scalar(out=brep[:], in0=ones33[:], scalar1=Bsb[:, col:col + 1], scalar2=8.0,
                                                op0=ALU.mult, op1=ALU.mult),
         (b_B, b_o), (b_brep,))
    for c0 in range(0, L, 512):
        n = min(512, L - c0)
        P.mm(pst[:, 0:n], brep[:], oh_sb[:, c0:c0 + n], True, True, (b_brep, b_oh), (b_pst,))
        P.op("dve", lambda: nc.vector.tensor_copy(out=frow[:, c0:c0 + n], in_=pst[:, 0:n]), (b_pst,), (b_frow,))
    b_scr = Buf()
    P.dma("sp", scr_d[:, 0:L], frow[:, 0:L], (b_frow,), (b_scr,))
    src = bass.AP(scr_d.tensor, scr_d.offset + 127, [[scr_d.ap[0][0] - 1, 128], [1, W]])
    P.dma("sp", out_ap, src, (b_scr,), (b_out,))


def emit_normalize(P, nc, acc_ap, b_acc, n, E, b_E, R, b_R, psB, b_psB, o_sb, b_osb, out_dram_ap):
    P.op("dve", lambda: nc.vector.reciprocal(out=R[64:65, 0:n], in_=acc_ap[64:65, :]), (b_acc,), (b_R,))
    P.mm(psB[0:64, 0:n], E[:], R[:, 0:n], True, True, (b_E, b_R), (b_psB,))
    P.op("dve", lambda: nc.vector.tensor_tensor(out=o_sb[0:64, 0:n], in0=acc_ap[0:64, :], in1=psB[0:64, 0:n],
                                                op=ALU.mult),
         (b_acc, b_psB), (b_osb,))
    P.dma("sp", out_dram_ap, o_sb[0:64, 0:n], (b_osb,), (), is_output=True)


def emit_attn_consts(P, nc, es):
    A = {}
    A["ident"] = sb(nc, es, "ident", [128, 128], BF16)
    A["b_ident"] = Buf()
    P.op("pool", lambda: nc.gpsimd.memset(A["ident"][:], 1.0), (), (A["b_ident"],))
    P.op("pool", lambda: nc.gpsimd.affine_select(out=A["ident"][:], in_=A["ident"][:], pattern=[[-1, 128]],
                                                 compare_op=ALU.is_equal, fill=0.0, base=0, channel_multiplier=1),
         (A["b_ident"],), (A["b_ident"],))
    A["E"] = sb(nc, es, "Esel", [65, 64], F32)
    A["b_E"] = Buf()
    P.op("dve", lambda: nc.vector.memset(A["E"][:], 0.0), (), (A["b_E"],))
    P.op("dve", lambda: nc.vector.memset(A["E"][64:65, :], 1.0), (), (A["b_E"],))
    A["R"] = sb(nc, es, "Rrec", [65, 512], F32)
    A["b_R"] = Buf()
    P.op("dve", lambda: nc.vector.memset(A["R"][:], 0.0), (), (A["b_R"],))
    return A


DSW_GROUPS = ((128, 1), (512, 4), (2048, 16))


def build_dsw(S, NHM):
    nc = bass.Bass("TRN2", target_bir_lowering=False)
    q_d = nc.dram_tensor("qT", [3, NHM * 64, S], BF16, kind="ExternalInput").ap()
    k_d = nc.dram_tensor("kT", [3, NHM * 64, S], BF16, kind="ExternalInput").ap()
    v_d = nc.dram_tensor("v", [S, 3, NHM * 64], BF16, kind="ExternalInput").ap()
    rb_d = nc.dram_tensor("rb", [32, 3 * NHM], F32, kind="ExternalInput").ap()
    oh_d = nc.dram_tensor("oh", [3, 33, 384], BF16, kind="ExternalInput").ap()
    o_d = nc.dram_tensor("oT", [NHM * 64, S], BF16, kind="ExternalOutput").ap()
    scr_d = nc.dram_tensor("scr", [3 * NHM, 128, 384], BF16, kind="Internal").ap()
    with contextlib.ExitStack() as es0:
        P = Prog(nc, es0)
        emit_dsw(P, nc, S, NHM, q_d, k_d, v_d, rb_d, oh_d, o_d, scr_d)
        P.finish()
    return nc


def emit_dsw(P, nc, S, NHM, q_d, k_d, v_d, rb_d, oh_d, o_d, scr_d):
    NB = S // 128
    vpitch = v_d.ap[0][0]
    gpitch = v_d.ap[1][0]
    with contextlib.ExitStack() as es:
        A = emit_attn_consts(P, nc, es)
        Bsb, b_B, ones33, b_o = emit_bias_rows(P, nc, es, rb_d, 3 * NHM)
        oh_sb = sb(nc, es, "oh_sb", [33, 3, 384], BF16)
        b_oh = Buf()
        for g in range(3):
            P.dma("sp", oh_sb[:, g, :], oh_d[g], (), (b_oh,))
        brep = sb(nc, es, "brep", [33, 128], BF16)
        b_brep = Buf()
        frow = sb(nc, es, "frow", [128, 384], BF16)
        b_frow = Buf()
        T = sb(nc, es, "Ttab", [128, 3 * NHM, 256], BF16)
        b_T = [Buf() for _ in range(3 * NHM)]
        psS = [(ps(nc, es, f"psS{i}", [128, 512]), Buf()) for i in range(2)]
        psO = [(ps(nc, es, f"psO{i}", [128, 512]), Buf()) for i in range(2)]
        psB, b_psB = ps(nc, es, "psB", [128, 512]), Buf()
        pst, b_pst = ps(nc, es, "pst", [128, 512]), Buf()
        for g in range(3):
            for hh in range(NHM):
                col = g * NHM + hh
                emit_toeplitz(P, nc, Bsb, b_B, ones33, b_o, col, oh_sb[:, g, :], b_oh, 384, brep, b_brep, frow, b_frow,
                              scr_d[col], pst, b_pst, T[:, col, :], 256, b_T[col])
        Vgs = [sb(nc, es, f"Vg{i}", [128, 3, NB, 2, 65], BF16) for i in range(2)]
        b_Vs = [Buf(), Buf()]
        for i in range(2):
            P.op("pool", lambda: nc.gpsimd.memset(Vgs[i][:], 1.0), (), (b_Vs[i],))

        def load_v(pair):
            Vg_, b_V_ = Vgs[pair % 2], b_Vs[pair % 2]
            for g, (win, r) in enumerate(DSW_GROUPS):
                nb = NB // r
                for c in range(r):
                    for hl in range(2):
                        hh = pair * 2 + hl
                        src = bass.AP(v_d.tensor, v_d.offset + c * vpitch + g * gpitch + hh * 64,
                                      [[r * vpitch, 128], [128 * r * vpitch, nb], [1, 64]])
                        P.dma("pool", Vg_[:, g, c * nb:(c + 1) * nb, hl, 0:64], src, (), (b_V_,))

        load_v(0)
        QT = [sb(nc, es, f"QT{i}", [128, S], BF16) for i in range(2)]
        KT = [sb(nc, es, f"KT{i}", [128, S], BF16) for i in range(2)]
        b_QT = [Buf(), Buf()]
        b_KT = [Buf(), Buf()]
        acc = [sb(nc, es, f"acc{i}", [65, S], F32) for i in range(2)]
        b_acc = [Buf(), Buf()]
        PT = [sb(nc, es, f"PT{i}", [128, 256], BF16) for i in range(3)]
        b_PT = [Buf() for _ in range(3)]
        o_sb = [sb(nc, es, f"o_sb{i}", [64, 512], BF16) for i in range(2)]
        b_osb = [Buf(), Buf()]
        li = 0
        ti = 0
        oi = 0
        for pair in range(NHM // 2):
            if pair + 1 < NHM // 2:
                load_v(pair + 1)
            Vg, b_Vp = Vgs[pair % 2], b_Vs[pair % 2]
            for g, (win, r) in enumerate(DSW_GROUPS):
                nb = NB // r
                slot = li % 2
                li += 1
                qt, kt = QT[slot], KT[slot]
                P.dma("sp", qt[:], q_d[g, pair * 128:(pair + 1) * 128, :], (), (b_QT[slot],))
                P.dma("sp", kt[:], k_d[g, pair * 128:(pair + 1) * 128, :], (), (b_KT[slot],))
                tiles = [(hl, c, j) for hl in range(2) for c in range(r) for j in range(nb)]

                def emit_S(tl, k):
                    hl, c, j = tl
                    hh = pair * 2 + hl
                    rows = slice(hl * 64, (hl + 1) * 64)
                    nqb = 2 if j + 1 < nb else 1
                    k0 = c + 128 * j * r
                    pS, b_pS = psS[k % 2]
                    P.mm(pS[:, 0:128 * nqb], kt[rows, sstep(k0, 128, r)], qt[rows, sstep(k0, 128 * nqb, r)],
                         True, False, (b_KT[slot], b_QT[slot]), (b_pS,))
                    P.mm(pS[:, 0:128 * nqb], A["ident"][:], T[:, g * NHM + hh, 0:128 * nqb], False, True,
                         (A["b_ident"], b_T[g * NHM + hh]), (b_pS,))
                    pt, b_pt = PT[k % 3], b_PT[k % 3]
                    P.op("act", lambda: nc.scalar.activation(out=pt[:, 0:128 * nqb], in_=pS[:, 0:128 * nqb],
                                                             func=AF.Exp, scale=SCALE),
                         (b_pS,), (b_pt,))

                def emit_PV(tl, k):
                    hl, c, j = tl
                    hh = pair * 2 + hl
                    nqb = 2 if j + 1 < nb else 1
                    pt, b_pt = PT[k % 3], b_PT[k % 3]
                    for qi in range(nqb):
                        i = j + qi
                        pO, b_pO = psO[i % 2]
                        start = (i == 0) or (j == i - 1)
                        stop = (j == i)
                        P.mm(pO[0:65, 0:128], Vg[:, g, c * nb + j, hl, :], pt[:, qi * 128:(qi + 1) * 128],
                             start, stop, (b_Vp, b_pt), (b_pO,), last=True)
                        if stop:
                            p0 = c + 128 * i * r
                            dst = acc[hl][:, sstep(p0, 128, r)]
                            if g == 0:
                                P.op("dve", lambda: nc.vector.tensor_copy(out=dst, in_=pO[0:65, 0:128]),
                                     (b_pO,), (b_acc[hl],))
                            else:
                                P.op("dve", lambda: nc.vector.tensor_tensor(out=dst, in0=pO[0:65, 0:128], in1=dst,
                                                                            op=ALU.add),
                                     (b_pO, b_acc[hl]), (b_acc[hl],))

                n = len(tiles)
                for k in range(n + 1):
                    if k < n:
                        emit_S(tiles[k], ti + k)
                    if k > 0:
                        emit_PV(tiles[k - 1], ti + k - 1)
                ti += n
            for hl in range(2):
                hh = pair * 2 + hl
                for ch in range(S // 512):
                    cs = slice(ch * 512, (ch + 1) * 512)
                    emit_normalize(P, nc, acc[hl][:, cs], b_acc[hl], 512, A["E"], A["b_E"], A["R"], A["b_R"],
                                   psB, b_psB, o_sb[oi % 2], b_osb[oi % 2], o_d[hh * 64:(hh + 1) * 64, cs])
                    oi += 1
        P.barrier()


def moba_consts(S):
    nblk = S // 256
    kind = np.zeros((16, S), np.float32)
    for n in range(nblk):
        kind[n, n * 256:(n + 1) * 256] = 1.0
    nqt = S // 128
    cm = np.full((nqt, 16), -1e30, np.float32)
    for qt in range(nqt):
        cm[qt, :qt // 2] = 0.0
        cm[qt, qt // 2] = 1e30
    cm = np.broadcast_to(cm.reshape(1, nqt * 16), (128, nqt * 16)).copy()
    return kind.astype(ml_dtypes.bfloat16), cm


def build_moba(S, NH):
    nc = bass.Bass("TRN2", target_bir_lowering=False)
    NQT = S // 128
    NQC = S // 512
    q_d = nc.dram_tensor("qT", [NH * 64, S], BF16, kind="ExternalInput").ap()
    k_d = nc.dram_tensor("kT", [NH * 64, S], BF16, kind="ExternalInput").ap()
    v_d = nc.dram_tensor("v", [S, NH * 64], BF16, kind="ExternalInput").ap()
    rb_d = nc.dram_tensor("rb", [32, NH], F32, kind="ExternalInput").ap()
    oh_d = nc.dram_tensor("oh", [33, MOBA_L], BF16, kind="ExternalInput").ap()
    kind_d = nc.dram_tensor("kind", [16, S], BF16, kind="ExternalInput").ap()
    cm_d = nc.dram_tensor("cm", [128, NQT * 16], F32, kind="ExternalInput").ap()
    o_d = nc.dram_tensor("oT", [NH * 64, S], BF16, kind="ExternalOutput").ap()
    scr_d = nc.dram_tensor("scr", [NH, 128, MOBA_L], BF16, kind="Internal").ap()
    with contextlib.ExitStack() as es0:
        P = Prog(nc, es0)
        emit_moba(P, nc, S, NH, q_d, k_d, v_d, rb_d, oh_d, kind_d, cm_d, o_d, scr_d)
        P.finish()
    return nc


def emit_moba(P, nc, S, NH, q_d, k_d, v_d, rb_d, oh_d, kind_d, cm_d, o_d, scr_d):
    NQT = S // 128
    NQC = S // 512
    vpitch = v_d.ap[0][0]
    with contextlib.ExitStack() as es:
        A = emit_attn_consts(P, nc, es)
        Bsb, b_B, ones33, b_o = emit_bias_rows(P, nc, es, rb_d, NH)
        oh_sb = sb(nc, es, "oh_sb", [33, MOBA_L], BF16)
        b_oh = Buf()
        P.dma("sp", oh_sb[:], oh_d, (), (b_oh,))
        cm_sb = sb(nc, es, "cm_sb", [128, NQT * 16], F32)
        b_cm = Buf()
        P.dma("sp", cm_sb[:], cm_d, (), (b_cm,))
        brep = sb(nc, es, "brep", [33, 128], BF16)
        b_brep = Buf()
        frow = sb(nc, es, "frow", [128, MOBA_L], BF16)
        b_frow = Buf()
        TB = [sb(nc, es, f"TB{i}", [128, MOBA_W], BF16) for i in range(2)]
        b_TB = [Buf(), Buf()]
        psS = [(ps(nc, es, f"psS{i}", [128, 512]), Buf()) for i in range(2)]
        psO = [(ps(nc, es, f"psO{i}", [128, 512]), Buf()) for i in range(2)]
        psB, b_psB = ps(nc, es, "psB", [128, 512]), Buf()
        pst, b_pst = ps(nc, es, "pst", [128, 512]), Buf()
        psG, b_psG = ps(nc, es, "psG", [128, 512]), Buf()
        psT, b_psT = ps(nc, es, "psT", [128, 1024], BF16), Buf()
        Vh = sb(nc, es, "Vh", [128, NQT, NH, 65], BF16)
        b_V = Buf()
        P.op("pool", lambda: nc.gpsimd.memset(Vh[:], 1.0), (), (b_V,))
        for h in range(NH):
            src = bass.AP(v_d.tensor, v_d.offset + h * 64, [[vpitch, 128], [128 * vpitch, NQT], [1, 64]])
            P.dma("pool", Vh[:, :, h, 0:64], src, (), (b_V,))
        QTa = [sb(nc, es, f"QTa{i}", [128, S], BF16) for i in range(2)]
        KTa = [sb(nc, es, f"KTa{i}", [128, S], BF16) for i in range(2)]
        b_Q = [Buf(), Buf()]
        b_K = [Buf(), Buf()]
        for i in range(2):
            P.op("pool", lambda: nc.gpsimd.memset(QTa[i][0:64, :], 0.0), (), (b_Q[i],))
            P.op("pool", lambda: nc.gpsimd.memset(KTa[i][0:64, :], 0.0), (), (b_K[i],))
            P.dma("sp", KTa[i][0:16, :], kind_d, (), (b_K[i],))
        km = sb(nc, es, "km", [128, 16], F32)
        kmh = sb(nc, es, "kmh", [128, 16], BF16)
        kml = sb(nc, es, "kml", [128, 16], BF16)
        kmr = sb(nc, es, "kmr", [128, 16], F32)
        b_km, b_kmh, b_kml, b_kmr = Buf(), Buf(), Buf(), Buf()
        gm = sb(nc, es, "gm", [128, NQT * 16], F32)
        b_gm = Buf()
        top8 = sb(nc, es, "top8", [128, NQT * 8], F32)
        b_top8 = Buf()
        thr = sb(nc, es, "thr", [128, NQT], F32)
        b_thr = Buf()
        mb = sb(nc, es, "mb", [128, NQT * 16], BF16)
        b_mb = Buf()
        PT = [sb(nc, es, f"PT{i}", [128, 512], BF16) for i in range(3)]
        b_PT = [Buf() for _ in range(3)]
        accs = [sb(nc, es, f"accs{i}", [65, 512], F32) for i in range(2)]
        b_accs = [Buf(), Buf()]
        o_sb = [sb(nc, es, f"o_sb{i}", [64, 512], BF16) for i in range(2)]
        b_osb = [Buf(), Buf()]
        ti = 0
        oi = 0
        for h in range(NH):
            sl = h % 2
            qa, ka = QTa[sl], KTa[sl]
            P.dma("sp", qa[64:128, :], q_d[h * 64:(h + 1) * 64, :], (), (b_Q[sl],))
            P.dma("sp", ka[64:128, :], k_d[h * 64:(h + 1) * 64, :], (), (b_K[sl],))
            emit_toeplitz(P, nc, Bsb, b_B, ones33, b_o, h, oh_sb, b_oh, MOBA_L, brep, b_brep, frow, b_frow,
                          scr_d[h], pst, b_pst, TB[sl][:], MOBA_W, b_TB[sl])
            P.op("dve", lambda: nc.vector.tensor_reduce(out=km[64:128, 0:S // 256],
                                                        in_=ka[64:128, :].rearrange("p (n k) -> p n k", k=256),
                                                        axis=AX.X, op=ALU.add),
                 (b_K[sl],), (b_km,))
            if S // 256 < 16:
                P.op("dve", lambda: nc.vector.memset(km[64:128, S // 256:16], 0.0), (), (b_km,))
            P.op("dve", lambda: nc.vector.tensor_scalar(out=kmh[64:128, :], in0=km[64:128, :], scalar1=1.0 / 256.0,
                                                        scalar2=None, op0=ALU.mult),
                 (b_km,), (b_kmh,))
            P.op("dve", lambda: nc.vector.scalar_tensor_tensor(out=kmr[64:128, :], in0=km[64:128, :], scalar=1.0 / 256.0,
                                                               in1=kmh[64:128, :], op0=ALU.mult, op1=ALU.subtract),
                 (b_km, b_kmh), (b_kmr,))
            P.op("dve", lambda: nc.vector.tensor_copy(out=kml[64:128, :], in_=kmr[64:128, :]), (b_kmr,), (b_kml,))
            for qt in range(NQT):
                P.mm(psG[:, qt * 16:(qt + 1) * 16], qa[64:128, qt * 128:(qt + 1) * 128], kmh[64:128, :], True, False,
                     (b_Q[sl], b_kmh), (b_psG,))
                P.mm(psG[:, qt * 16:(qt + 1) * 16], qa[64:128, qt * 128:(qt + 1) * 128], kml[64:128, :], False, True,
                     (b_Q[sl], b_kml), (b_psG,), last=(qt == NQT - 1))
            P.op("dve", lambda: nc.vector.tensor_tensor(out=gm[:], in0=psG[:, 0:NQT * 16], in1=cm_sb[:], op=ALU.add),
                 (b_psG, b_cm), (b_gm,))
            for qt in range(NQT):
                P.op("dve", lambda: nc.vector.max(out=top8[:, qt * 8:(qt + 1) * 8], in_=gm[:, qt * 16:(qt + 1) * 16]),
                     (b_gm,), (b_top8,))
            P.op("dve", lambda: nc.vector.tensor_scalar(out=thr[:], in0=top8[:, 3:NQT * 8:8], scalar1=-1e29,
                                                        scalar2=None, op0=ALU.max),
                 (b_top8,), (b_thr,))
            for qt in range(NQT):
                P.op("dve", lambda: nc.vector.tensor_scalar(out=mb[:, qt * 16:(qt + 1) * 16],
                                                            in0=gm[:, qt * 16:(qt + 1) * 16],
                                                            scalar1=thr[:, qt:qt + 1], scalar2=NEGB,
                                                            op0=ALU.is_lt, op1=ALU.mult),
                     (b_gm, b_thr), (b_mb,))
            for q8 in range(0, NQT, 8):
                n8 = min(8, NQT - q8)
                for qq in range(n8):
                    qt = q8 + qq
                    P.op("pe", lambda: nc.tensor.transpose(out=psT[0:16, qq * 128:(qq + 1) * 128],
                                                           in_=mb[:, qt * 16:(qt + 1) * 16], identity=A["ident"][:]),
                         (b_mb, A["b_ident"]), (b_psT,), inc=(qq == n8 - 1))
                P.op("act", lambda: nc.scalar.copy(out=qa[0:16, q8 * 128:(q8 + n8) * 128], in_=psT[0:16, 0:n8 * 128]),
                     (b_psT,), (b_Q[sl],))
            for qc in range(NQC):
                q0 = qc * 512
                kts = list(range(4 * qc + 4))
                pO, b_pO = psO[qc % 2]

                def emit_S(kt_, k):
                    half = kt_ >= 4 * qc + 2
                    qs = q0 + 256 if half else q0
                    nq = 256 if half else 512
                    pS, b_pS = psS[k % 2]
                    P.mm(pS[:, 0:nq], ka[0:128, kt_ * 128:(kt_ + 1) * 128], qa[0:128, qs:qs + nq], True, False,
                         (b_K[sl], b_Q[sl]), (b_pS,))
                    z0 = min(qs - 128 * kt_ + 128, MOBA_W - nq)
                    P.mm(pS[:, 0:nq], A["ident"][:], TB[sl][:, z0:z0 + nq], False, True,
                         (A["b_ident"], b_TB[sl]), (b_pS,))
                    pt, b_pt = PT[k % 3], b_PT[k % 3]
                    P.op("act", lambda: nc.scalar.activation(out=pt[:, 0:nq], in_=pS[:, 0:nq], func=AF.Exp, scale=SCALE),
                         (b_pS,), (b_pt,))

                def emit_PV(kt_, k):
                    half = kt_ >= 4 * qc + 2
                    c0 = 256 if half else 0
                    nq = 256 if half else 512
                    pt, b_pt = PT[k % 3], b_PT[k % 3]
                    P.mm(pO[0:65, c0:c0 + nq], Vh[:, kt_, h, :], pt[:, 0:nq], kt_ == 0, kt_ == kts[-1],
                         (b_V, b_pt), (b_pO,), last=True)

                n = len(kts)
                for k in range(n + 1):
                    if k < n:
                        emit_S(kts[k], ti + k)
                    if k > 0:
                        emit_PV(kts[k - 1], ti + k - 1)
                ti += n
                ac, b_ac = accs[oi % 2], b_accs[oi % 2]
                P.op("dve", lambda: nc.vector.tensor_copy(out=ac[:], in_=pO[0:65, :]), (b_pO,), (b_ac,))
                emit_normalize(P, nc, ac[:, :], b_ac, 512, A["E"], A["b_E"], A["R"], A["b_R"],
                               psB, b_psB, o_sb[oi % 2], b_osb[oi % 2], o_d[h * 64:(h + 1) * 64, q0:q0 + 512])
                oi += 1
        P.barrier()


def _g_layout(g):
    return np.ascontiguousarray(np.asarray(g, np.float32).reshape(8, 128).T)


def _gain_layout(qg, kg):
    return np.ascontiguousarray(np.stack([np.tile(np.asarray(qg, np.float32), 2),
                                          np.tile(np.asarray(kg, np.float32), 2)], axis=1))


def _run(nc, in_maps):
    res = run_bass_kernel_spmd(nc, in_maps, core_ids=list(range(8)))
    return res.results


def build_fused():
    nc = bass.Bass("TRN2", target_bir_lowering=False)
    S = SEQ
    xT = nc.dram_tensor("xT", [D, S], F32, kind="ExternalInput").ap()
    rb = nc.dram_tensor("rel_bias", [32, 24], F32, kind="ExternalInput").ap()
    g_mix = [nc.dram_tensor(f"g_mix{i}", [128, 8], F32, kind="ExternalInput").ap() for i in range(2)]
    g_ffn = [nc.dram_tensor(f"g_ffn{i}", [128, 8], F32, kind="ExternalInput").ap() for i in range(2)]
    gain = [nc.dram_tensor(f"gain{i}", [128, 2], F32, kind="ExternalInput").ap() for i in range(2)]
    wqkv = [nc.dram_tensor("a_w_qkv", [D, 4608], F32, kind="ExternalInput").ap(),
            nc.dram_tensor("b_w_qkv", [D, 3072], F32, kind="ExternalInput").ap()]
    wo = [nc.dram_tensor("a_w_o", [512, D], F32, kind="ExternalInput").ap(),
          nc.dram_tensor("b_w_o", [1024, D], F32, kind="ExternalInput").ap()]
    w1 = [nc.dram_tensor(f"w1_{i}", [D, 4096], F32, kind="ExternalInput").ap() for i in range(2)]
    w2 = [nc.dram_tensor(f"w2_{i}", [4096, D], F32, kind="ExternalInput").ap() for i in range(2)]
    oh_a = nc.dram_tensor("oh_a", [3, 33, 384], BF16, kind="ExternalInput").ap()
    oh_b = nc.dram_tensor("oh_b", [33, MOBA_L], BF16, kind="ExternalInput").ap()
    kind_d = nc.dram_tensor("kind", [16, S], BF16, kind="ExternalInput").ap()
    cm_d = nc.dram_tensor("cm", [128, (S // 128) * 16], F32, kind="ExternalInput").ap()
    outT = nc.dram_tensor("outT", [D, S], F32, kind="ExternalOutput").ap()
    qk_s = nc.dram_tensor("qk_s", [3072, S], BF16, kind="Internal").ap()
    v_s = nc.dram_tensor("v_s", [S, 1536], BF16, kind="Internal").ap()
    o_s = nc.dram_tensor("o_s", [1024, S], BF16, kind="Internal").ap()
    h1 = nc.dram_tensor("h1_s", [D, S], F32, kind="Internal").ap()
    scr_a = nc.dram_tensor("scr_a", [24, 128, 384], BF16, kind="Internal").ap()
    scr_b = nc.dram_tensor("scr_b", [16, 128, MOBA_L], BF16, kind="Internal").ap()
    TT = S // 2
    with contextlib.ExitStack() as es0:
        P = Prog(nc, es0)
        for layer in range(2):
            h_src = xT if layer == 0 else h1
            h_dst = h1 if layer == 0 else outT
            CQK, CV = (3072, 1536) if layer == 0 else (2048, 1024)
            KO = 512 if layer == 0 else 1024
            for half in range(2):
                tk = slice(half * TT, (half + 1) * TT)
                emit_lin_in(P, nc, TT, CQK, CV, h_src[:, tk], g_mix[layer], wqkv[layer], gain[layer],
                            qk_s[0:CQK, tk], v_s[tk, 0:CV])
            if layer == 0:
                emit_dsw(P, nc, S, 8,
                         qk_s[0:1536, :].rearrange("(g r) s -> g r s", g=3),
                         qk_s[1536:3072, :].rearrange("(g r) s -> g r s", g=3),
                         v_s.rearrange("s (g c) -> s g c", g=3),
                         rb, oh_a, o_s[0:512, :], scr_a)
            else:
                emit_moba(P, nc, S, 16, qk_s[0:1024, :], qk_s[1024:2048, :], v_s[:, 0:1024],
                          rb[:, 0:16], oh_b, kind_d, cm_d, o_s[0:1024, :], scr_b)
            for half in range(2):
                tk = slice(half * TT, (half + 1) * TT)
                emit_lin_out(P, nc, TT, KO, 4096, h_src[:, tk], o_s[0:KO, tk], g_ffn[layer], wo[layer],
                             w1[layer], w2[layer], h_dst[:, tk])
        P.finish()
    return nc


REAL_CORES = (0, 1, 4, 5)


def kernel(x, rel_bias, norm_mix, norm_ffn, a_w_qkv, a_q_gain, a_k_gain, a_w_o,
           b_w_qkv, b_q_gain, b_k_gain, b_w_o, ffn_w1, ffn_w2):
    f32 = lambda a: np.ascontiguousarray(np.asarray(a, np.float32))
    x = np.asarray(x, np.float32)
    kind, cm = moba_consts(SEQ)
    common = {
        "rel_bias": f32(rel_bias),
        "g_mix0": _g_layout(norm_mix[0]), "g_mix1": _g_layout(norm_mix[1]),
        "g_ffn0": _g_layout(norm_ffn[0]), "g_ffn1": _g_layout(norm_ffn[1]),
        "gain0": _gain_layout(a_q_gain[0], a_k_gain[0]), "gain1": _gain_layout(b_q_gain[0], b_k_gain[0]),
        "a_w_qkv": f32(a_w_qkv[0]), "b_w_qkv": f32(b_w_qkv[0]),
        "a_w_o": f32(a_w_o[0]), "b_w_o": f32(b_w_o[0]),
        "w1_0": f32(ffn_w1[0]), "w1_1": f32(ffn_w1[1]), "w2_0": f32(ffn_w2[0]), "w2_1": f32(ffn_w2[1]),
        "oh_a": np.stack([onehot_dsw(r) for _, r in DSW_GROUPS]), "oh_b": onehot_moba(),
        "kind": kind, "cm": cm,
    }
    zeros = np.zeros((D, SEQ), np.float32)
    in_maps = []
    for c in range(8):
        m = dict(common)
        m["xT"] = np.ascontiguousarray(x[REAL_CORES.index(c)].T) if c in REAL_CORES else zeros
        in_maps.append(m)
    nc = build_fused()
    res = run_bass_kernel_spmd(nc, in_maps, core_ids=list(range(8))).results
    out = np.empty((BATCH, SEQ, D), np.float32)
    for b, c in enumerate(REAL_CORES):
        out[b] = np.asarray(res[c]["outT"], np.float32).T
    return out


def kernel_unfused(x, rel_bias, norm_mix, norm_ffn, a_w_qkv, a_q_gain, a_k_gain, a_w_o,
           b_w_qkv, b_q_gain, b_k_gain, b_w_o, ffn_w1, ffn_w2):
    x = np.asarray(x, np.float32)
    rel_bias = np.ascontiguousarray(np.asarray(rel_bias, np.float32))
    TT = SEQ // 2
    hT = [np.ascontiguousarray(x[c // 2, (c % 2) * TT:(c % 2 + 1) * TT].T) for c in range(8)]
    oh_a = np.stack([onehot_dsw(r) for _, r in DSW_GROUPS])
    oh_b = onehot_moba()
    kind, cm = moba_consts(SEQ)
    for layer in range(2):
        is_a = layer == 0
        w_qkv = np.ascontiguousarray(np.asarray(a_w_qkv[0] if is_a else b_w_qkv[0], np.float32))
        CQK, CV = (3072, 1536) if is_a else (2048, 1024)
        gain = _gain_layout(a_q_gain[0], a_k_gain[0]) if is_a else _gain_layout(b_q_gain[0], b_k_gain[0])
        g_mix = _g_layout(norm_mix[layer])
        nc1 = build_lin_in(TT, CQK, CV)
        r1 = _run(nc1, [{"hT": hT[c], "g": g_mix, "w": w_qkv, "gain": gain} for c in range(8)])
        in2 = []
        for c in range(8):
            b, r2 = c // 2, c % 2
            qk = np.concatenate([r1[2 * b]["qkT"], r1[2 * b + 1]["qkT"]], axis=1)
            v = np.concatenate([r1[2 * b]["v"], r1[2 * b + 1]["v"]], axis=0)
            if is_a:
                qT = np.stack([qk[g * 512 + r2 * 256:g * 512 + (r2 + 1) * 256] for g in range(3)])
                kT = np.stack([qk[1536 + g * 512 + r2 * 256:1536 + g * 512 + (r2 + 1) * 256] for g in range(3)])
                vv = np.stack([v[:, g * 512 + r2 * 256:g * 512 + (r2 + 1) * 256] for g in range(3)], axis=1)
                rb = np.concatenate([rel_bias[:, g * 8 + r2 * 4:g * 8 + r2 * 4 + 4] for g in range(3)], axis=1)
                in2.append({"qT": np.ascontiguousarray(qT), "kT": np.ascontiguousarray(kT),
                            "v": np.ascontiguousarray(vv), "rb": np.ascontiguousarray(rb), "oh": oh_a})
            else:
                in2.append({"qT": np.ascontiguousarray(qk[r2 * 512:(r2 + 1) * 512]),
                            "kT": np.ascontiguousarray(qk[1024 + r2 * 512:1024 + (r2 + 1) * 512]),
                            "v": np.ascontiguousarray(v[:, r2 * 512:(r2 + 1) * 512]),
                            "rb": np.ascontiguousarray(rel_bias[:, r2 * 8:(r2 + 1) * 8]),
                            "oh": oh_b, "kind": kind, "cm": cm})
        nc2 = build_dsw(SEQ, 4) if is_a else build_moba(SEQ, 8)
        r2_ = _run(nc2, in2)
        KO = 512 if is_a else 1024
        w_o = np.ascontiguousarray(np.asarray(a_w_o[0] if is_a else b_w_o[0], np.float32))
        w1 = np.ascontiguousarray(np.asarray(ffn_w1[layer], np.float32))
        w2 = np.ascontiguousarray(np.asarray(ffn_w2[layer], np.float32))
        g_ffn = _g_layout(norm_ffn[layer])
        in3 = []
        for c in range(8):
            b, r = c // 2, c % 2
            oT = np.concatenate([r2_[2 * b]["oT"], r2_[2 * b + 1]["oT"]], axis=0)
            in3.append({"hT": hT[c], "oT": np.ascontiguousarray(oT[:, r * TT:(r + 1) * TT]), "g": g_ffn,
                        "wo": w_o, "w1": w1, "w2": w2})
        nc3 = build_lin_out(TT, KO)
        r3 = _run(nc3, in3)
        hT = [np.asarray(r3[c]["hT_out"], np.float32) for c in range(8)]
    out = np.empty((BATCH, SEQ, D), np.float32)
    for c in range(8):
        out[c // 2, (c % 2) * TT:(c % 2 + 1) * TT] = hT[c].T
    return out
```

```python
import contextlib
import numpy as np
import ml_dtypes
import concourse.bass as bass
import concourse.mybir as mybir
from concourse.bass_utils import run_bass_kernel_spmd

F32 = mybir.dt.float32
BF16 = mybir.dt.bfloat16
ALU = mybir.AluOpType
AF = mybir.ActivationFunctionType
AX = mybir.AxisListType

D = 1024
SEQ = 4096
BATCH = 4
HD = 64
EPS = 1e-6
SCALE = HD ** -0.5
NEGB = -30000.0
SAME_ENGINE_SYNC = True


class Buf:
    __slots__ = ("w", "r", "name")

    def __init__(self, name=""):
        self.w = None
        self.r = {}
        self.name = name


class Prog:
    NDQ = 6

    def __init__(self, nc, es):
        self.nc = nc
        self.es = es
        self.eng = {"pe": nc.tensor, "act": nc.scalar, "dve": nc.vector,
                    "pool": nc.gpsimd, "sp": nc.sync}
        self.sem = {}
        self.cnt = {}
        for k in ("pe", "act", "dve", "pool"):
            self.sem[k] = es.enter_context(nc.semaphore(f"s_{k}"))
            self.cnt[k] = 0
        self.waited = {k: {} for k in self.eng}
        self.dq = {}
        for q in ("sp", "pool", "act"):
            sems = [es.enter_context(nc.semaphore(f"d_{q}{i}")) for i in range(self.NDQ)]
            self.dq[q] = {"sems": sems, "val": [0] * self.NDQ, "idx": 0}
        self.pe_open = False
        self.out_tickets = []

    def need(self, e, tk):
        if tk is None:
            return
        sem, val = tk
        key = sem.num
        if self.waited[e].get(key, 0) >= val:
            return
        self.eng[e].wait_ge(sem, val)
        self.waited[e][key] = val

    def _deps(self, e, reads, writes):
        own = self.sem.get(e)
        tks = []
        for b in reads:
            if b.w is not None:
                tks.append(b.w)
        for b in writes:
            if b.w is not None:
                tks.append(b.w)
            tks.extend(b.r.values())
        for tk in tks:
            if own is not None and tk[0].num == own.num:
                if e == "pe" or not SAME_ENGINE_SYNC:
                    continue
            self.need(e, tk)

    def _record(self, tk, reads, writes):
        key = tk[0].num
        for b in reads:
            b.r[key] = tk
        for b in writes:
            b.w = tk
            b.r = {}

    def op(self, e, fn, reads=(), writes=(), inc=True):
        self._deps(e, reads, writes)
        ins = fn()
        if inc:
            self.cnt[e] += 1
            ins.then_inc(self.sem[e], 1)
            tk = (self.sem[e], self.cnt[e])
            if e == "pe":
                self.pe_open = False
        else:
            assert e == "pe"
            tk = (self.sem[e], self.cnt[e] + 1)
            self.pe_open = True
        self._record(tk, reads, writes)
        return tk

    def dma(self, q, out, in_, reads=(), writes=(), is_output=False, **kw):
        d = self.dq[q]
        i = d["idx"] % self.NDQ
        d["idx"] += 1
        if d["val"][i] > 0:
            self.need(q, (d["sems"][i], d["val"][i]))
        self._deps(q, reads, writes)
        ins = self.eng[q].dma_start(out=out, in_=in_, **kw)
        d["val"][i] += 16
        ins.then_inc(d["sems"][i], 16)
        tk = (d["sems"][i], d["val"][i])
        self._record(tk, reads, writes)
        if is_output:
            self.out_tickets.append(tk)
        return tk

    def finish(self):
        assert not self.pe_open
        for q in ("sp", "pool", "act"):
            d = self.dq[q]
            for i in range(self.NDQ):
                if d["val"][i] > 0:
                    self.need("sp", (d["sems"][i], d["val"][i]))
        for e in ("pe", "act", "dve", "pool"):
            if self.cnt[e] > 0:
                self.need("sp", (self.sem[e], self.cnt[e]))

    def barrier(self):
        assert not self.pe_open
        for e in self.eng:
            for q in ("sp", "pool", "act"):
                d = self.dq[q]
                for i in range(self.NDQ):
                    if d["val"][i] > 0:
                        self.need(e, (d["sems"][i], d["val"][i]))
            for x in ("pe", "act", "dve", "pool"):
                if x != e and self.cnt[x] > 0:
                    self.need(e, (self.sem[x], self.cnt[x]))

    def mm(self, out, lhsT, rhs, start, stop, reads, writes, last=None):
        if last is None:
            last = stop
        return self.op("pe", lambda: self.nc.tensor.matmul(out, lhsT, rhs, start=start, stop=stop),
                       reads, writes, inc=last)


def sstep(start, n, step):
    return slice(start, start + (n - 1) * step + 1, step)


_UID = [0]


def _uname(name):
    _UID[0] += 1
    return f"{name}_{_UID[0]}"


def sb(nc, es, name, shape, dt):
    return es.enter_context(nc.sbuf_tensor(_uname(name), shape, dt))


def ps(nc, es, name, shape, dt=F32):
    return es.enter_context(nc.psum_tensor(_uname(name), shape, dt))


def emit_consts(P, nc, es):
    c = {}
    c["avg1024"] = sb(nc, es, "avg1024", [128, 128], BF16)
    c["avg64"] = sb(nc, es, "avg64", [128, 128], BF16)
    c["b_avg1024"] = Buf()
    c["b_avg64"] = Buf()
    P.op("dve", lambda: nc.vector.memset(c["avg1024"][:], 1.0 / 1024.0), (), (c["b_avg1024"],))
    P.op("dve", lambda: nc.vector.memset(c["avg64"][:], 0.0), (), (c["b_avg64"],))
    P.op("dve", lambda: nc.vector.memset(c["avg64"][0:64, 0:64], 1.0 / 64.0), (), (c["b_avg64"],))
    P.op("dve", lambda: nc.vector.memset(c["avg64"][64:128, 64:128], 1.0 / 64.0), (), (c["b_avg64"],))
    c["eps"] = sb(nc, es, "eps_c", [128, 1], F32)
    c["b_eps"] = Buf()
    P.op("dve", lambda: nc.vector.memset(c["eps"][:], EPS), (), (c["b_eps"],))
    return c


def emit_rsqrt(P, nc, C, rstd, b_rstd, pt, b_pt):
    P.op("act", lambda: nc.scalar.activation(out=rstd[:], in_=pt[:], func=AF.Sqrt, bias=C["eps"][:, 0:1]),
         (b_pt, C["b_eps"]), (b_rstd,))
    P.op("dve", lambda: nc.vector.reciprocal(out=rstd[:], in_=rstd[:]), (b_rstd,), (b_rstd,))


def emit_rmsnorm(P, nc, C, hT, b_h, g_sb, b_g, uT, b_u, TT, psums, tmp):
    NT = TT // 512
    sq, b_sq, rstd, b_rstd = tmp["sq"], tmp["b_sq"], tmp["rstd"], tmp["b_rstd"]
    for t in range(NT):
        ts = slice(t * 512, (t + 1) * 512)
        pt, b_pt = psums[t % 2]
        for c in range(8):
            i = c % 2
            P.op("act", lambda: nc.scalar.activation(out=sq[i][:], in_=hT[:, c, ts], func=AF.Square),
                 (b_h[c][t],), (b_sq[i],))
            P.mm(pt[:], C["avg1024"][:], sq[i][:], c == 0, c == 7,
                 (C["b_avg1024"], b_sq[i]), (b_pt,), last=True)
        emit_rsqrt(P, nc, C, rstd, b_rstd, pt, b_pt)
        for c in range(8):
            e = "dve"
            eng = nc.vector
            P.op(e, lambda: eng.scalar_tensor_tensor(out=uT[:, c, ts], in0=hT[:, c, ts],
                                                     scalar=g_sb[:, c:c + 1], in1=rstd[:],
                                                     op0=ALU.mult, op1=ALU.mult),
                 (b_h[c][t], b_g, b_rstd), (b_u[c][t],))


def build_lin_in(TT, CQK, CV):
    nc = bass.Bass("TRN2", target_bir_lowering=False)
    CW = CQK + CV
    hT_d = nc.dram_tensor("hT", [D, TT], F32, kind="ExternalInput").ap()
    g_d = nc.dram_tensor("g", [128, 8], F32, kind="ExternalInput").ap()
    w_d = nc.dram_tensor("w", [D, CW], F32, kind="ExternalInput").ap()
    gain_d = nc.dram_tensor("gain", [128, 2], F32, kind="ExternalInput").ap()
    qk_d = nc.dram_tensor("qkT", [CQK, TT], BF16, kind="ExternalOutput").ap()
    v_d = nc.dram_tensor("v", [TT, CV], BF16, kind="ExternalOutput").ap()
    with contextlib.ExitStack() as es0:
        P = Prog(nc, es0)
        emit_lin_in(P, nc, TT, CQK, CV, hT_d, g_d, w_d, gain_d, qk_d, v_d)
        P.finish()
    return nc


def emit_lin_in(P, nc, TT, CQK, CV, hT_d, g_d, w_d, gain_d, qk_d, v_d):
    CW = CQK + CV
    NT = TT // 512
    with contextlib.ExitStack() as es:
        C = emit_consts(P, nc, es)
        hT = sb(nc, es, "hT_sb", [128, 8, TT], F32)
        uT = sb(nc, es, "uT_sb", [128, 8, TT], BF16)
        g_sb = sb(nc, es, "g_sb", [128, 8], F32)
        gain_sb = sb(nc, es, "gain_sb", [128, 2], F32)
        b_h = [[Buf() for _ in range(NT)] for _ in range(8)]
        b_u = [[Buf() for _ in range(NT)] for _ in range(8)]
        b_g, b_gain = Buf(), Buf()
        tmp = {"sq": [sb(nc, es, f"sq{i}", [128, 512], BF16) for i in range(2)],
               "b_sq": [Buf(), Buf()],
               "rstd": sb(nc, es, "rstd", [128, 512], F32), "b_rstd": Buf()}
        psums = [(ps(nc, es, f"ps{i}", [128, 512]), Buf()) for i in range(6)]
        P.dma("sp", g_sb[:], g_d, (), (b_g,))
        P.dma("sp", gain_sb[:], gain_d, (), (b_gain,))
        for c in range(8):
            for t in range(NT):
                P.dma("sp", hT[:, c, t * 512:(t + 1) * 512], hT_d[c * 128:(c + 1) * 128, t * 512:(t + 1) * 512],
                      (), (b_h[c][t],))
        emit_rmsnorm(P, nc, C, hT, b_h, g_sb, b_g, uT, b_u, TT, psums[0:2], tmp)

        NG = CW // 512
        wb = [sb(nc, es, f"wb{i}", [128, 8, 512], BF16) for i in range(2)]
        b_wb = [Buf(), Buf()]
        osb = [sb(nc, es, f"osb{i}", [128, 512], BF16) for i in range(3)]
        b_osb = [Buf() for _ in range(3)]
        oi = 0
        w_v = w_d.rearrange("(c p) n -> p c n", p=128)

        def load_w(gi):
            for c in range(8):
                P.dma("pool", wb[gi % 2][:, c, :], w_v[:, c, gi * 512:(gi + 1) * 512], (), (b_wb[gi % 2],))

        load_w(0)
        pi = 2
        pending = []
        rstd2 = [tmp["rstd"], sb(nc, es, "rstd_b", [128, 512], F32)]
        b_rstd2 = [tmp["b_rstd"], Buf()]
        for gi in range(NG):
            if gi + 1 < NG:
                load_w(gi + 1)
            w = wb[gi % 2]
            bw = b_wb[gi % 2]
            if gi * 512 < CQK:
                isq = 0 if gi * 512 < CQK // 2 else 1
                for j in range(4):
                    col0 = gi * 512 + j * 128
                    for t in range(NT):
                        ts = slice(t * 512, (t + 1) * 512)
                        pa, b_pa = psums[pi % 4]
                        pb, b_pb = psums[4 + (pi % 2)]
                        i = pi % 2
                        pi += 1
                        for c in range(8):
                            P.mm(pa[:], w[:, c, j * 128:(j + 1) * 128], uT[:, c, ts], c == 0, c == 7,
                                 (bw, b_u[c][t]), (b_pa,))
                        sq, b_sq = tmp["sq"][i], tmp["b_sq"][i]
                        P.op("act", lambda: nc.scalar.activation(out=sq[:], in_=pa[:], func=AF.Square),
                             (b_pa,), (b_sq,))

                        def stage2(pa=pa, b_pa=b_pa, pb=pb, b_pb=b_pb, sq=sq, b_sq=b_sq, i=i, isq=isq,
                                   col0=col0, ts=ts):
                            nonlocal oi
                            P.mm(pb[:], C["avg64"][:], sq[:], True, True, (C["b_avg64"], b_sq), (b_pb,))
                            rstd, b_rstd = rstd2[i], b_rstd2[i]
                            emit_rsqrt(P, nc, C, rstd, b_rstd, pb, b_pb)
                            o, b_o = osb[oi % 3], b_osb[oi % 3]
                            oi += 1
                            P.op("dve", lambda: nc.vector.scalar_tensor_tensor(out=o[:], in0=pa[:],
                                                                               scalar=gain_sb[:, isq:isq + 1],
                                                                               in1=rstd[:],
                                                                               op0=ALU.mult, op1=ALU.mult),
                                 (b_pa, b_gain, b_rstd), (b_o,))
                            P.dma("sp", qk_d[col0:col0 + 128, ts], o[:], (b_o,), (), is_output=True)

                        if pending:
                            pending.pop()()
                        pending.append(stage2)
            else:
                if pending:
                    pending.pop()()
                vc0 = gi * 512 - CQK
                for tt in range(TT // 128):
                    t = tt // 4
                    tsl = slice(tt * 128, (tt + 1) * 128)
                    pa, b_pa = psums[2 + (pi % 2)]
                    pi += 1
                    for c in range(8):
                        P.mm(pa[:], uT[:, c, tsl], w[:, c, :], c == 0, c == 7, (bw, b_u[c][t]), (b_pa,))
                    o, b_o = osb[oi % 3], b_osb[oi % 3]
                    oi += 1
                    P.op("act", lambda: nc.scalar.copy(out=o[:], in_=pa[:]), (b_pa,), (b_o,))
                    P.dma("sp", v_d[tsl, vc0:vc0 + 512], o[:], (b_o,), (), is_output=True)
        P.barrier()


def build_lin_out(TT, KO, DFF=4096):
    nc = bass.Bass("TRN2", target_bir_lowering=False)
    hT_d = nc.dram_tensor("hT", [D, TT], F32, kind="ExternalInput").ap()
    oT_d = nc.dram_tensor("oT", [KO, TT], BF16, kind="ExternalInput").ap()
    g_d = nc.dram_tensor("g", [128, 8], F32, kind="ExternalInput").ap()
    wo_d = nc.dram_tensor("wo", [KO, D], F32, kind="ExternalInput").ap()
    w1_d = nc.dram_tensor("w1", [D, DFF], F32, kind="ExternalInput").ap()
    w2_d = nc.dram_tensor("w2", [DFF, D], F32, kind="ExternalInput").ap()
    out_d = nc.dram_tensor("hT_out", [D, TT], F32, kind="ExternalOutput").ap()
    with contextlib.ExitStack() as es0:
        P = Prog(nc, es0)
        emit_lin_out(P, nc, TT, KO, DFF, hT_d, oT_d, g_d, wo_d, w1_d, w2_d, out_d)
        P.finish()
    return nc


def emit_lin_out(P, nc, TT, KO, DFF, hT_d, oT_d, g_d, wo_d, w1_d, w2_d, out_d):
    NT = TT // 512
    KC = KO // 128
    NFG = DFF // 512
    with contextlib.ExitStack() as es:
        C = emit_consts(P, nc, es)
        hT = sb(nc, es, "hT_sb", [128, 8, TT], F32)
        uT = sb(nc, es, "uT_sb", [128, 8, TT], BF16)
        aT = sb(nc, es, "aT_sb", [128, 4, TT], BF16)
        g_sb = sb(nc, es, "g_sb", [128, 8], F32)
        b_h = [[Buf() for _ in range(NT)] for _ in range(8)]
        b_u = [[Buf() for _ in range(NT)] for _ in range(8)]
        b_a = [[Buf() for _ in range(NT)] for _ in range(4)]
        b_g = Buf()
        tmp = {"sq": [sb(nc, es, f"sq{i}", [128, 512], BF16) for i in range(2)],
               "b_sq": [Buf(), Buf()],
               "rstd": sb(nc, es, "rstd", [128, 512], F32), "b_rstd": Buf()}
        rl = [sb(nc, es, f"rl{i}", [128, 512], F32) for i in range(2)]
        b_rl = [Buf(), Buf()]
        psums = [(ps(nc, es, f"ps{i}", [128, 512]), Buf()) for i in range(6)]
        wb = [sb(nc, es, f"wb{i}", [128, 8, 512], BF16) for i in range(2)]
        b_wb = [Buf(), Buf()]
        w2b = [sb(nc, es, f"w2b{i}", [128, 4, D], BF16) for i in range(2)]
        b_w2b = [Buf(), Buf()]
        P.dma("sp", g_sb[:], g_d, (), (b_g,))
        for c in range(8):
            for t in range(NT):
                P.dma("sp", hT[:, c, t * 512:(t + 1) * 512], hT_d[c * 128:(c + 1) * 128, t * 512:(t + 1) * 512],
                      (), (b_h[c][t],))
        for c in range(KC):
            for t in range(NT):
                P.dma("act", uT[:, c, t * 512:(t + 1) * 512], oT_d[c * 128:(c + 1) * 128, t * 512:(t + 1) * 512],
                      (), (b_u[c][t],))
        wo_v = wo_d.rearrange("(c p) n -> p c n", p=128)
        w1_v = w1_d.rearrange("(c p) n -> p c n", p=128)
        w2_v = w2_d.rearrange("(f p) n -> p f n", p=128)
        wi = 0

        def load_wo(gi, slot):
            for c in range(KC):
                P.dma("pool", wb[slot][:, c, :], wo_v[:, c, gi * 512:(gi + 1) * 512], (), (b_wb[slot],))

        def load_w1(fg, slot):
            for c in range(8):
                P.dma("pool", wb[slot][:, c, :], w1_v[:, c, fg * 512:(fg + 1) * 512], (), (b_wb[slot],))

        def load_w2(fg, slot):
            for f in range(4):
                P.dma("pool", w2b[slot][:, f, :], w2_v[:, fg * 4 + f, :], (), (b_w2b[slot],))

        load_wo(0, 0)
        load_wo(1, 1)
        pi = 0
        for gi in range(2):
            w, bw = wb[gi], b_wb[gi]
            for j in range(4):
                cj = gi * 4 + j
                for t in range(NT):
                    ts = slice(t * 512, (t + 1) * 512)
                    pa, b_pa = psums[2 + (pi % 4)]
                    pi += 1
                    for c in range(KC):
                        P.mm(pa[:], w[:, c, j * 128:(j + 1) * 128], uT[:, c, ts], c == 0, c == KC - 1,
                             (bw, b_u[c][t]), (b_pa,))
                    P.op("dve", lambda: nc.vector.tensor_tensor(out=hT[:, cj, ts], in0=pa[:], in1=hT[:, cj, ts],
                                                                op=ALU.add),
                         (b_pa, b_h[cj][t]), (b_h[cj][t],))
        load_w1(0, 0)
        load_w2(0, 0)
        emit_rmsnorm(P, nc, C, hT, b_h, g_sb, b_g, uT, b_u, TT, psums[0:2], tmp)
        ri = 0
        for fg in range(NFG):
            slot = fg % 2
            if fg + 1 < NFG:
                load_w1(fg + 1, 1 - slot)
                load_w2(fg + 1, 1 - slot)
            w, bw = wb[slot], b_wb[slot]
            w2, bw2 = w2b[slot], b_w2b[slot]
            for f in range(4):
                for t in range(NT):
                    ts = slice(t * 512, (t + 1) * 512)
                    pa, b_pa = psums[2 + (pi % 4)]
                    pi += 1
                    for c in range(8):
                        P.mm(pa[:], w[:, c, f * 128:(f + 1) * 128], uT[:, c, ts], c == 0, c == 7,
                             (bw, b_u[c][t]), (b_pa,))
                    r, b_r = rl[ri % 2], b_rl[ri % 2]
                    ri += 1
                    P.op("act", lambda: nc.scalar.activation(out=r[:], in_=pa[:], func=AF.Relu), (b_pa,), (b_r,))
                    P.op("pool", lambda: nc.gpsimd.tensor_tensor(out=aT[:, f, ts], in0=r[:], in1=r[:], op=ALU.mult),
                         (b_r,), (b_a[f][t],))
            for j in range(8):
                for t in range(NT):
                    ts = slice(t * 512, (t + 1) * 512)
                    pa, b_pa = psums[2 + (pi % 4)]
                    pi += 1
                    for f in range(4):
                        P.mm(pa[:], w2[:, f, j * 128:(j + 1) * 128], aT[:, f, ts], f == 0, f == 3,
                             (bw2, b_a[f][t]), (b_pa,))
                    P.op("dve", lambda: nc.vector.tensor_tensor(out=hT[:, j, ts], in0=pa[:], in1=hT[:, j, ts],
                                                                op=ALU.add),
                         (b_pa, b_h[j][t]), (b_h[j][t],))
                    if fg == NFG - 1:
                        P.dma("sp", out_d[j * 128:(j + 1) * 128, ts], hT[:, j, ts], (b_h[j][t],), (), is_output=True)
        P.barrier()


def t5_bucket_np(dist):
    n = np.maximum(dist, 0)
    nf = np.maximum(n, 1).astype(np.float32)
    large = 16 + (np.log(nf / np.float32(16)) / np.float32(np.log(2048 / 16)) * np.float32(16)).astype(np.int32)
    large = np.minimum(large, 31)
    return np.where(n < 16, n, large)


def onehot_dsw(r):
    m = np.arange(384) - 127
    valid = (m >= 0) & (m <= 128)
    b = np.where(valid, t5_bucket_np(m * r), 32)
    oh = np.zeros((33, 384), np.float32)
    oh[b, np.arange(384)] = 1.0
    return oh.astype(ml_dtypes.bfloat16)


MOBA_W = 2304
MOBA_L = MOBA_W + 128


def onehot_moba():
    dd = np.arange(MOBA_L) - 255
    b = np.where(dd >= 0, t5_bucket_np(dd), 32)
    oh = np.zeros((33, MOBA_L), np.float32)
    oh[b, np.arange(MOBA_L)] = 1.0
    return oh.astype(ml_dtypes.bfloat16)


def emit_bias_rows(P, nc, es, rb_d, ncols):
    Bsb = sb(nc, es, "Bsb", [33, ncols], F32)
    b_B = Buf()
    P.op("dve", lambda: nc.vector.memset(Bsb[:], NEGB / 8.0), (), (b_B,))
    P.dma("sp", Bsb[0:32, :], rb_d, (), (b_B,))
    ones33 = sb(nc, es, "ones33", [33, 128], F32)
    b_o = Buf()
    P.op("dve", lambda: nc.vector.memset(ones33[:], 1.0), (), (b_o,))
    return Bsb, b_B, ones33, b_o


def emit_toeplitz(P, nc, Bsb, b_B, ones33, b_o, col, oh_sb, b_oh, L, brep, b_brep, frow, b_frow,
                  scr_d, pst, b_pst, out_ap, W, b_out):
    P.op("dve", lambda: nc.vector.tensor_scalar(out=brep[:], in0=ones33[:], scalar1=Bsb[:, col:col + 1], scalar2=8.0,
                                                op0=ALU.mult, op1=ALU.mult),
         (b_B, b_o), (b_brep,))
    for c0 in range(0, L, 512):
        n = min(512, L - c0)
        P.mm(pst[:, 0:n], brep[:], oh_sb[:, c0:c0 + n], True, True, (b_brep, b_oh), (b_pst,))
        P.op("dve", lambda: nc.vector.tensor_copy(out=frow[:, c0:c0 + n], in_=pst[:, 0:n]), (b_pst,), (b_frow,))
    b_scr = Buf()
    P.dma("sp", scr_d[:, 0:L], frow[:, 0:L], (b_frow,), (b_scr,))
    src = bass.AP(scr_d.tensor, scr_d.offset + 127, [[scr_d.ap[0][0] - 1, 128], [1, W]])
    P.dma("sp", out_ap, src, (b_scr,), (b_out,))


def emit_normalize(P, nc, acc_ap, b_acc, n, E, b_E, R, b_R, psB, b_psB, o_sb, b_osb, out_dram_ap):
    P.op("dve", lambda: nc.vector.reciprocal(out=R[64:65, 0:n], in_=acc_ap[64:65, :]), (b_acc,), (b_R,))
    P.mm(psB[0:64, 0:n], E[:], R[:, 0:n], True, True, (b_E, b_R), (b_psB,))
    P.op("dve", lambda: nc.vector.tensor_tensor(out=o_sb[0:64, 0:n], in0=acc_ap[0:64, :], in1=psB[0:64, 0:n],
                                                op=ALU.mult),
         (b_acc, b_psB), (b_osb,))
    P.dma("sp", out_dram_ap, o_sb[0:64, 0:n], (b_osb,), (), is_output=True)


def emit_attn_consts(P, nc, es):
    A = {}
    A["ident"] = sb(nc, es, "ident", [128, 128], BF16)
    A["b_ident"] = Buf()
    P.op("pool", lambda: nc.gpsimd.memset(A["ident"][:], 1.0), (), (A["b_ident"],))
    P.op("pool", lambda: nc.gpsimd.affine_select(out=A["ident"][:], in_=A["ident"][:], pattern=[[-1, 128]],
                                                 compare_op=ALU.is_equal, fill=0.0, base=0, channel_multiplier=1),
         (A["b_ident"],), (A["b_ident"],))
    A["E"] = sb(nc, es, "Esel", [65, 64], F32)
    A["b_E"] = Buf()
    P.op("dve", lambda: nc.vector.memset(A["E"][:], 0.0), (), (A["b_E"],))
    P.op("dve", lambda: nc.vector.memset(A["E"][64:65, :], 1.0), (), (A["b_E"],))
    A["R"] = sb(nc, es, "Rrec", [65, 512], F32)
    A["b_R"] = Buf()
    P.op("dve", lambda: nc.vector.memset(A["R"][:], 0.0), (), (A["b_R"],))
    return A


DSW_GROUPS = ((128, 1), (512, 4), (2048, 16))


def build_dsw(S, NHM):
    nc = bass.Bass("TRN2", target_bir_lowering=False)
    q_d = nc.dram_tensor("qT", [3, NHM * 64, S], BF16, kind="ExternalInput").ap()
    k_d = nc.dram_tensor("kT", [3, NHM * 64, S], BF16, kind="ExternalInput").ap()
    v_d = nc.dram_tensor("v", [S, 3, NHM * 64], BF16, kind="ExternalInput").ap()
    rb_d = nc.dram_tensor("rb", [32, 3 * NHM], F32, kind="ExternalInput").ap()
    oh_d = nc.dram_tensor("oh", [3, 33, 384], BF16, kind="ExternalInput").ap()
    o_d = nc.dram_tensor("oT", [NHM * 64, S], BF16, kind="ExternalOutput").ap()
    scr_d = nc.dram_tensor("scr", [3 * NHM, 128, 384], BF16, kind="Internal").ap()
    with contextlib.ExitStack() as es0:
        P = Prog(nc, es0)
        emit_dsw(P, nc, S, NHM, q_d, k_d, v_d, rb_d, oh_d, o_d, scr_d)
        P.finish()
    return nc


def emit_dsw(P, nc, S, NHM, q_d, k_d, v_d, rb_d, oh_d, o_d, scr_d):
    NB = S // 128
    vpitch = v_d.ap[0][0]
    gpitch = v_d.ap[1][0]
    with contextlib.ExitStack() as es:
        A = emit_attn_consts(P, nc, es)
        Bsb, b_B, ones33, b_o = emit_bias_rows(P, nc, es, rb_d, 3 * NHM)
        oh_sb = sb(nc, es, "oh_sb", [33, 3, 384], BF16)
        b_oh = Buf()
        for g in range(3):
            P.dma("sp", oh_sb[:, g, :], oh_d[g], (), (b_oh,))
        brep = sb(nc, es, "brep", [33, 128], BF16)
        b_brep = Buf()
        frow = sb(nc, es, "frow", [128, 384], BF16)
        b_frow = Buf()
        T = sb(nc, es, "Ttab", [128, 3 * NHM, 256], BF16)
        b_T = [Buf() for _ in range(3 * NHM)]
        psS = [(ps(nc, es, f"psS{i}", [128, 512]), Buf()) for i in range(3)]
        psO = [(ps(nc, es, f"psO{i}", [128, 512]), Buf()) for i in range(4)]
        pst, b_pst = ps(nc, es, "pst", [128, 512]), Buf()
        psB, b_psB = pst, b_pst
        for g in range(3):
            for hh in range(NHM):
                col = g * NHM + hh
                emit_toeplitz(P, nc, Bsb, b_B, ones33, b_o, col, oh_sb[:, g, :], b_oh, 384, brep, b_brep, frow, b_frow,
                              scr_d[col], pst, b_pst, T[:, col, :], 256, b_T[col])
        Vgs = [sb(nc, es, f"Vg{i}", [128, 3, NB, 2, 65], BF16) for i in range(2)]
        b_Vs = [Buf(), Buf()]
        for i in range(2):
            P.op("pool", lambda: nc.gpsimd.memset(Vgs[i][:], 1.0), (), (b_Vs[i],))

        def load_v(pair):
            Vg_, b_V_ = Vgs[pair % 2], b_Vs[pair % 2]
            for g, (win, r) in enumerate(DSW_GROUPS):
                nb = NB // r
                for c in range(r):
                    for hl in range(2):
                        hh = pair * 2 + hl
                        src = bass.AP(v_d.tensor, v_d.offset + c * vpitch + g * gpitch + hh * 64,
                                      [[r * vpitch, 128], [128 * r * vpitch, nb], [1, 64]])
                        P.dma("pool", Vg_[:, g, c * nb:(c + 1) * nb, hl, 0:64], src, (), (b_V_,))

        load_v(0)
        QT = [sb(nc, es, f"QT{i}", [128, S], BF16) for i in range(2)]
        KT = [sb(nc, es, f"KT{i}", [128, S], BF16) for i in range(2)]
        b_QT = [Buf(), Buf()]
        b_KT = [Buf(), Buf()]
        acc = sb(nc, es, "acc", [65, 2, S], F32)
        b_acc = Buf()
        PT = [sb(nc, es, f"PT{i}", [128, 512], BF16) for i in range(4)]
        b_PT = [Buf() for _ in range(4)]
        o_sb = [sb(nc, es, f"o_sb{i}", [64, 512], BF16) for i in range(2)]
        b_osb = [Buf(), Buf()]
        oi = 0
        jobs = [(pair, g) for pair in range(NHM // 2) for g in range(3)]

        def load_qk(ji):
            pair, g = jobs[ji]
            slot = ji % 2
            P.dma("sp", QT[slot][:], q_d[g, pair * 128:(pair + 1) * 128, :], (), (b_QT[slot],))
            P.dma("sp", KT[slot][:], k_d[g, pair * 128:(pair + 1) * 128, :], (), (b_KT[slot],))

        tiles = []
        for ji, (pair, g) in enumerate(jobs):
            r = DSW_GROUPS[g][1]
            nb = NB // r
            for c in range(r):
                for j in range(nb):
                    tiles.append((ji, c, j))
        first_tile_of_job = {}
        last_tile_of_pair = {}
        for k, (ji, c, j) in enumerate(tiles):
            first_tile_of_job.setdefault(ji, k)
            last_tile_of_pair[jobs[ji][0]] = k

        def emit_S(tl, k):
            ji, c, j = tl
            pair, g = jobs[ji]
            slot = ji % 2
            r = DSW_GROUPS[g][1]
            nb = NB // r
            qt, kt = QT[slot], KT[slot]
            nqb = 2 if j + 1 < nb else 1
            k0 = c + 128 * j * r
            pS, b_pS = psS[k % 3]
            for hl in range(2):
                hh = pair * 2 + hl
                rows = slice(hl * 64, (hl + 1) * 64)
                cols = slice(hl * 256, hl * 256 + 128 * nqb)
                P.mm(pS[:, cols], kt[rows, sstep(k0, 128, r)], qt[rows, sstep(k0, 128 * nqb, r)],
                     True, False, (b_KT[slot], b_QT[slot]), (b_pS,), last=False)
                P.mm(pS[:, cols], A["ident"][:], T[:, g * NHM + hh, 0:128 * nqb], False, True,
                     (A["b_ident"], b_T[g * NHM + hh]), (b_pS,), last=(hl == 1))
            pt, b_pt = PT[k % 4], b_PT[k % 4]
            if nqb == 2:
                src, dst = pS[:, 0:512], pt[:, 0:512]
            else:
                src = pS[:, 0:512].rearrange("p (h x) -> p h x", h=2)[:, :, 0:128]
                dst = pt[:, 0:512].rearrange("p (h x) -> p h x", h=2)[:, :, 0:128]
            P.op("act", lambda: nc.scalar.activation(out=dst, in_=src, func=AF.Exp, scale=SCALE),
                 (b_pS,), (b_pt,))

        def emit_PV(tl, k):
            nonlocal oi
            ji, c, j = tl
            pair, g = jobs[ji]
            r = DSW_GROUPS[g][1]
            nb = NB // r
            Vg, b_Vp = Vgs[pair % 2], b_Vs[pair % 2]
            nqb = 2 if j + 1 < nb else 1
            pt, b_pt = PT[k % 4], b_PT[k % 4]
            for qi in range(nqb):
                i = j + qi
                start = (i == 0) or (j == i - 1)
                stop = (j == i)
                for hl in range(2):
                    pO, b_pO = psO[(i % 2) * 2 + hl]
                    P.mm(pO[0:65, 0:128], Vg[:, g, c * nb + j, hl, :],
                         pt[:, hl * 256 + qi * 128:hl * 256 + (qi + 1) * 128],
                         start, stop, (b_Vp, b_pt), (b_pO,), last=True)
                    if stop:
                        p0 = c + 128 * i * r
                        dst = acc[:, hl, sstep(p0, 128, r)]
                        if g == 0:
                            P.op("dve", lambda: nc.vector.tensor_copy(out=dst, in_=pO[0:65, 0:128]), (b_pO,), (b_acc,))
                        else:
                            P.op("dve", lambda: nc.vector.tensor_tensor(out=dst, in0=pO[0:65, 0:128], in1=dst,
                                                                        op=ALU.add),
                                 (b_pO, b_acc), (b_acc,))
            if last_tile_of_pair[pair] == k:
                for hl in range(2):
                    hh = pair * 2 + hl
                    for ch in range(S // 512):
                        cs = slice(ch * 512, (ch + 1) * 512)
                        emit_normalize(P, nc, acc[:, hl, cs], b_acc, 512, A["E"], A["b_E"], A["R"], A["b_R"],
                                       psB, b_psB, o_sb[oi % 2], b_osb[oi % 2], o_d[hh * 64:(hh + 1) * 64, cs])
                        oi += 1

        DEPTH = 2
        load_qk(0)
        n = len(tiles)
        for k in range(n + DEPTH):
            if k < n:
                ji = tiles[k][0]
                if first_tile_of_job[ji] == k:
                    if ji + 1 < len(jobs):
                        load_qk(ji + 1)
                emit_S(tiles[k], k)
            if k >= DEPTH:
                emit_PV(tiles[k - DEPTH], k - DEPTH)
            if k < n:
                ji2 = tiles[k][0]
                pair2, g2 = jobs[ji2]
                if g2 == 0 and k == first_tile_of_job[ji2] + DEPTH and pair2 + 1 < NHM // 2:
                    load_v(pair2 + 1)
        P.barrier()


def moba_consts(S):
    nblk = S // 256
    kind = np.zeros((16, S), np.float32)
    for n in range(nblk):
        kind[n, n * 256:(n + 1) * 256] = 1.0
    nqt = S // 128
    cm = np.full((nqt, 16), -1e30, np.float32)
    for qt in range(nqt):
        cm[qt, :qt // 2] = 0.0
        cm[qt, qt // 2] = 1e30
    cm = np.broadcast_to(cm.reshape(1, nqt * 16), (128, nqt * 16)).copy()
    return kind.astype(ml_dtypes.bfloat16), cm


def build_moba(S, NH):
    nc = bass.Bass("TRN2", target_bir_lowering=False)
    NQT = S // 128
    NQC = S // 512
    q_d = nc.dram_tensor("qT", [NH * 64, S], BF16, kind="ExternalInput").ap()
    k_d = nc.dram_tensor("kT", [NH * 64, S], BF16, kind="ExternalInput").ap()
    v_d = nc.dram_tensor("v", [S, NH * 64], BF16, kind="ExternalInput").ap()
    rb_d = nc.dram_tensor("rb", [32, NH], F32, kind="ExternalInput").ap()
    oh_d = nc.dram_tensor("oh", [33, MOBA_L], BF16, kind="ExternalInput").ap()
    kind_d = nc.dram_tensor("kind", [16, S], BF16, kind="ExternalInput").ap()
    cm_d = nc.dram_tensor("cm", [128, NQT * 16], F32, kind="ExternalInput").ap()
    o_d = nc.dram_tensor("oT", [NH * 64, S], BF16, kind="ExternalOutput").ap()
    scr_d = nc.dram_tensor("scr", [NH, 128, MOBA_L], BF16, kind="Internal").ap()
    with contextlib.ExitStack() as es0:
        P = Prog(nc, es0)
        emit_moba(P, nc, S, NH, q_d, k_d, v_d, rb_d, oh_d, kind_d, cm_d, o_d, scr_d)
        P.finish()
    return nc


def emit_moba(P, nc, S, NH, q_d, k_d, v_d, rb_d, oh_d, kind_d, cm_d, o_d, scr_d):
    NQT = S // 128
    NQC = S // 512
    vpitch = v_d.ap[0][0]
    with contextlib.ExitStack() as es:
        A = emit_attn_consts(P, nc, es)
        Bsb, b_B, ones33, b_o = emit_bias_rows(P, nc, es, rb_d, NH)
        oh_sb = sb(nc, es, "oh_sb", [33, MOBA_L], BF16)
        b_oh = Buf()
        P.dma("sp", oh_sb[:], oh_d, (), (b_oh,))
        cm_sb = sb(nc, es, "cm_sb", [128, NQT * 16], F32)
        b_cm = Buf()
        P.dma("sp", cm_sb[:], cm_d, (), (b_cm,))
        brep = sb(nc, es, "brep", [33, 128], BF16)
        b_brep = Buf()
        frow = sb(nc, es, "frow", [128, MOBA_L], BF16)
        b_frow = Buf()
        TB = [sb(nc, es, f"TB{i}", [128, MOBA_W], BF16) for i in range(2)]
        b_TB = [Buf(), Buf()]
        psS = [(ps(nc, es, f"psS{i}", [128, 512]), Buf()) for i in range(3)]
        psO = [(ps(nc, es, f"psO{i}", [128, 512]), Buf()) for i in range(2)]
        psB, b_psB = ps(nc, es, "psB", [128, 512]), Buf()
        pst, b_pst = ps(nc, es, "pst", [128, 512]), Buf()
        psG, b_psG = pst, b_pst
        psT, b_psT = ps(nc, es, "psT", [128, 1024], BF16), Buf()
        Vh = sb(nc, es, "Vh", [128, NQT, NH, 65], BF16)
        b_V = Buf()
        P.op("pool", lambda: nc.gpsimd.memset(Vh[:], 1.0), (), (b_V,))
        for h in range(NH):
            src = bass.AP(v_d.tensor, v_d.offset + h * 64, [[vpitch, 128], [128 * vpitch, NQT], [1, 64]])
            P.dma("pool", Vh[:, :, h, 0:64], src, (), (b_V,))
        QTa = [sb(nc, es, f"QTa{i}", [128, S], BF16) for i in range(2)]
        KTa = [sb(nc, es, f"KTa{i}", [128, S], BF16) for i in range(2)]
        b_Q = [Buf(), Buf()]
        b_K = [Buf(), Buf()]
        for i in range(2):
            P.op("pool", lambda: nc.gpsimd.memset(QTa[i][0:64, :], 0.0), (), (b_Q[i],))
            P.op("pool", lambda: nc.gpsimd.memset(KTa[i][0:64, :], 0.0), (), (b_K[i],))
            P.dma("sp", KTa[i][0:16, :], kind_d, (), (b_K[i],))
        km = sb(nc, es, "km", [128, 16], F32)
        kmh = sb(nc, es, "kmh", [128, 16], BF16)
        kml = sb(nc, es, "kml", [128, 16], BF16)
        kmr = sb(nc, es, "kmr", [128, 16], F32)
        b_km, b_kmh, b_kml, b_kmr = Buf(), Buf(), Buf(), Buf()
        gm = sb(nc, es, "gm", [128, NQT * 16], F32)
        b_gm = Buf()
        top8 = sb(nc, es, "top8", [128, NQT * 8], F32)
        b_top8 = Buf()
        thr = sb(nc, es, "thr", [128, NQT], F32)
        b_thr = Buf()
        mb = sb(nc, es, "mb", [128, NQT * 16], BF16)
        b_mb = Buf()
        PT = [sb(nc, es, f"PT{i}", [128, 512], BF16) for i in range(4)]
        b_PT = [Buf() for _ in range(4)]
        accs = [sb(nc, es, f"accs{i}", [65, 512], F32) for i in range(2)]
        b_accs = [Buf(), Buf()]
        o_sb = [sb(nc, es, f"o_sb{i}", [64, 512], BF16) for i in range(2)]
        b_osb = [Buf(), Buf()]
        ti = 0
        oi = 0

        def preamble(h):
            sl = h % 2
            qa, ka = QTa[sl], KTa[sl]
            P.dma("sp", qa[64:128, :], q_d[h * 64:(h + 1) * 64, :], (), (b_Q[sl],))
            P.dma("sp", ka[64:128, :], k_d[h * 64:(h + 1) * 64, :], (), (b_K[sl],))
            emit_toeplitz(P, nc, Bsb, b_B, ones33, b_o, h, oh_sb, b_oh, MOBA_L, brep, b_brep, frow, b_frow,
                          scr_d[h], pst, b_pst, TB[sl][:], MOBA_W, b_TB[sl])
            P.op("dve", lambda: nc.vector.tensor_reduce(out=km[64:128, 0:S // 256],
                                                        in_=ka[64:128, :].rearrange("p (n k) -> p n k", k=256),
                                                        axis=AX.X, op=ALU.add),
                 (b_K[sl],), (b_km,))
            if S // 256 < 16:
                P.op("dve", lambda: nc.vector.memset(km[64:128, S // 256:16], 0.0), (), (b_km,))
            P.op("dve", lambda: nc.vector.tensor_scalar(out=kmh[64:128, :], in0=km[64:128, :], scalar1=1.0 / 256.0,
                                                        scalar2=None, op0=ALU.mult),
                 (b_km,), (b_kmh,))
            P.op("dve", lambda: nc.vector.scalar_tensor_tensor(out=kmr[64:128, :], in0=km[64:128, :], scalar=1.0 / 256.0,
                                                               in1=kmh[64:128, :], op0=ALU.mult, op1=ALU.subtract),
                 (b_km, b_kmh), (b_kmr,))
            P.op("dve", lambda: nc.vector.tensor_copy(out=kml[64:128, :], in_=kmr[64:128, :]), (b_kmr,), (b_kml,))
            for qt in range(NQT):
                P.mm(psG[:, qt * 16:(qt + 1) * 16], qa[64:128, qt * 128:(qt + 1) * 128], kmh[64:128, :], True, False,
                     (b_Q[sl], b_kmh), (b_psG,))
                P.mm(psG[:, qt * 16:(qt + 1) * 16], qa[64:128, qt * 128:(qt + 1) * 128], kml[64:128, :], False, True,
                     (b_Q[sl], b_kml), (b_psG,), last=(qt == NQT - 1))
            P.op("dve", lambda: nc.vector.tensor_tensor(out=gm[:], in0=psG[:, 0:NQT * 16], in1=cm_sb[:], op=ALU.add),
                 (b_psG, b_cm), (b_gm,))
            yield
            for qt in range(NQT):
                P.op("dve", lambda: nc.vector.max(out=top8[:, qt * 8:(qt + 1) * 8], in_=gm[:, qt * 16:(qt + 1) * 16]),
                     (b_gm,), (b_top8,))
            P.op("dve", lambda: nc.vector.tensor_scalar(out=thr[:], in0=top8[:, 3:NQT * 8:8], scalar1=-1e29,
                                                        scalar2=None, op0=ALU.max),
                 (b_top8,), (b_thr,))
            for qt in range(NQT):
                P.op("dve", lambda: nc.vector.tensor_scalar(out=mb[:, qt * 16:(qt + 1) * 16],
                                                            in0=gm[:, qt * 16:(qt + 1) * 16],
                                                            scalar1=thr[:, qt:qt + 1], scalar2=NEGB,
                                                            op0=ALU.is_lt, op1=ALU.mult),
                     (b_gm, b_thr), (b_mb,))
            yield
            for q8 in range(0, NQT, 8):
                n8 = min(8, NQT - q8)
                for qq in range(n8):
                    qt = q8 + qq
                    P.op("pe", lambda: nc.tensor.transpose(out=psT[0:16, qq * 128:(qq + 1) * 128],
                                                           in_=mb[:, qt * 16:(qt + 1) * 16], identity=A["ident"][:]),
                         (b_mb, A["b_ident"]), (b_psT,), inc=(qq == n8 - 1))
                P.op("act", lambda: nc.scalar.copy(out=qa[0:16, q8 * 128:(q8 + n8) * 128], in_=psT[0:16, 0:n8 * 128]),
                     (b_psT,), (b_Q[sl],))

        def run_all(gen):
            for _ in gen:
                pass

        run_all(preamble(0))
        DEPTH = 2
        for h in range(NH):
            sl = h % 2
            qa, ka = QTa[sl], KTa[sl]
            nxt = preamble(h + 1) if h + 1 < NH else iter(())
            tl = [(qc, kt_) for qc in range(NQC) for kt_ in range(4 * qc + 4)]
            inject = {8, len(tl) // 3, (2 * len(tl)) // 3}

            def emit_S(tile, k):
                qc, kt_ = tile
                q0 = qc * 512
                half = kt_ >= 4 * qc + 2
                qs = q0 + 256 if half else q0
                nq = 256 if half else 512
                pS, b_pS = psS[k % 3]
                P.mm(pS[:, 0:nq], ka[0:128, kt_ * 128:(kt_ + 1) * 128], qa[0:128, qs:qs + nq], True, False,
                     (b_K[sl], b_Q[sl]), (b_pS,))
                z0 = min(qs - 128 * kt_ + 128, MOBA_W - nq)
                P.mm(pS[:, 0:nq], A["ident"][:], TB[sl][:, z0:z0 + nq], False, True,
                     (A["b_ident"], b_TB[sl]), (b_pS,))
                pt, b_pt = PT[k % 4], b_PT[k % 4]
                P.op("act", lambda: nc.scalar.activation(out=pt[:, 0:nq], in_=pS[:, 0:nq], func=AF.Exp, scale=SCALE),
                     (b_pS,), (b_pt,))

            def emit_PV(tile, k):
                nonlocal oi
                qc, kt_ = tile
                q0 = qc * 512
                half = kt_ >= 4 * qc + 2
                c0 = 256 if half else 0
                nq = 256 if half else 512
                pO, b_pO = psO[qc % 2]
                pt, b_pt = PT[k % 4], b_PT[k % 4]
                lastk = 4 * qc + 3
                P.mm(pO[0:65, c0:c0 + nq], Vh[:, kt_, h, :], pt[:, 0:nq], kt_ == 0, kt_ == lastk,
                     (b_V, b_pt), (b_pO,), last=True)
                if kt_ == lastk:
                    ac, b_ac = accs[oi % 2], b_accs[oi % 2]
                    P.op("dve", lambda: nc.vector.tensor_copy(out=ac[:], in_=pO[0:65, :]), (b_pO,), (b_ac,))
                    emit_normalize(P, nc, ac[:, :], b_ac, 512, A["E"], A["b_E"], A["R"], A["b_R"],
                                   psB, b_psB, o_sb[oi % 2], b_osb[oi % 2], o_d[h * 64:(h + 1) * 64, q0:q0 + 512])
                    oi += 1

            n = len(tl)
            for k in range(n + DEPTH):
                if k < n:
                    emit_S(tl[k], ti + k)
                if k >= DEPTH:
                    emit_PV(tl[k - DEPTH], ti + k - DEPTH)
                if k in inject:
                    next(nxt, None)
            ti += n
            run_all(nxt)
        P.barrier()


def _g_layout(g):
    return np.ascontiguousarray(np.asarray(g, np.float32).reshape(8, 128).T)


def _gain_layout(qg, kg):
    return np.ascontiguousarray(np.stack([np.tile(np.asarray(qg, np.float32), 2),
                                          np.tile(np.asarray(kg, np.float32), 2)], axis=1))


def _run(nc, in_maps):
    res = run_bass_kernel_spmd(nc, in_maps, core_ids=list(range(8)))
    return res.results


def build_fused():
    nc = bass.Bass("TRN2", target_bir_lowering=False)
    S = SEQ
    xT = nc.dram_tensor("xT", [D, S], F32, kind="ExternalInput").ap()
    rb = nc.dram_tensor("rel_bias", [32, 24], F32, kind="ExternalInput").ap()
    g_mix = [nc.dram_tensor(f"g_mix{i}", [128, 8], F32, kind="ExternalInput").ap() for i in range(2)]
    g_ffn = [nc.dram_tensor(f"g_ffn{i}", [128, 8], F32, kind="ExternalInput").ap() for i in range(2)]
    gain = [nc.dram_tensor(f"gain{i}", [128, 2], F32, kind="ExternalInput").ap() for i in range(2)]
    wqkv = [nc.dram_tensor("a_w_qkv", [D, 4608], F32, kind="ExternalInput").ap(),
            nc.dram_tensor("b_w_qkv", [D, 3072], F32, kind="ExternalInput").ap()]
    wo = [nc.dram_tensor("a_w_o", [512, D], F32, kind="ExternalInput").ap(),
          nc.dram_tensor("b_w_o", [1024, D], F32, kind="ExternalInput").ap()]
    w1 = [nc.dram_tensor(f"w1_{i}", [D, 4096], F32, kind="ExternalInput").ap() for i in range(2)]
    w2 = [nc.dram_tensor(f"w2_{i}", [4096, D], F32, kind="ExternalInput").ap() for i in range(2)]
    oh_a = nc.dram_tensor("oh_a", [3, 33, 384], BF16, kind="ExternalInput").ap()
    oh_b = nc.dram_tensor("oh_b", [33, MOBA_L], BF16, kind="ExternalInput").ap()
    kind_d = nc.dram_tensor("kind", [16, S], BF16, kind="ExternalInput").ap()
    cm_d = nc.dram_tensor("cm", [128, (S // 128) * 16], F32, kind="ExternalInput").ap()
    outT = nc.dram_tensor("outT", [D, S], F32, kind="ExternalOutput").ap()
    qk_s = nc.dram_tensor("qk_s", [3072, S], BF16, kind="Internal").ap()
    v_s = nc.dram_tensor("v_s", [S, 1536], BF16, kind="Internal").ap()
    o_s = nc.dram_tensor("o_s", [1024, S], BF16, kind="Internal").ap()
    h1 = nc.dram_tensor("h1_s", [D, S], F32, kind="Internal").ap()
    scr_a = nc.dram_tensor("scr_a", [24, 128, 384], BF16, kind="Internal").ap()
    scr_b = nc.dram_tensor("scr_b", [16, 128, MOBA_L], BF16, kind="Internal").ap()
    TT = S // 2
    with contextlib.ExitStack() as es0:
        P = Prog(nc, es0)
        for layer in range(2):
            h_src = xT if layer == 0 else h1
            h_dst = h1 if layer == 0 else outT
            CQK, CV = (3072, 1536) if layer == 0 else (2048, 1024)
            KO = 512 if layer == 0 else 1024
            for half in range(2):
                tk = slice(half * TT, (half + 1) * TT)
                emit_lin_in(P, nc, TT, CQK, CV, h_src[:, tk], g_mix[layer], wqkv[layer], gain[layer],
                            qk_s[0:CQK, tk], v_s[tk, 0:CV])
            if layer == 0:
                emit_dsw(P, nc, S, 8,
                         qk_s[0:1536, :].rearrange("(g r) s -> g r s", g=3),
                         qk_s[1536:3072, :].rearrange("(g r) s -> g r s", g=3),
                         v_s.rearrange("s (g c) -> s g c", g=3),
                         rb, oh_a, o_s[0:512, :], scr_a)
            else:
                emit_moba(P, nc, S, 16, qk_s[0:1024, :], qk_s[1024:2048, :], v_s[:, 0:1024],
                          rb[:, 0:16], oh_b, kind_d, cm_d, o_s[0:1024, :], scr_b)
            for half in range(2):
                tk = slice(half * TT, (half + 1) * TT)
                emit_lin_out(P, nc, TT, KO, 4096, h_src[:, tk], o_s[0:KO, tk], g_ffn[layer], wo[layer],
                             w1[layer], w2[layer], h_dst[:, tk])
        P.finish()
    return nc


REAL_CORES = (0, 1, 4, 5)


def kernel(x, rel_bias, norm_mix, norm_ffn, a_w_qkv, a_q_gain, a_k_gain, a_w_o,
           b_w_qkv, b_q_gain, b_k_gain, b_w_o, ffn_w1, ffn_w2):
    f32 = lambda a: np.ascontiguousarray(np.asarray(a, np.float32))
    x = np.asarray(x, np.float32)
    kind, cm = moba_consts(SEQ)
    common = {
        "rel_bias": f32(rel_bias),
        "g_mix0": _g_layout(norm_mix[0]), "g_mix1": _g_layout(norm_mix[1]),
        "g_ffn0": _g_layout(norm_ffn[0]), "g_ffn1": _g_layout(norm_ffn[1]),
        "gain0": _gain_layout(a_q_gain[0], a_k_gain[0]), "gain1": _gain_layout(b_q_gain[0], b_k_gain[0]),
        "a_w_qkv": f32(a_w_qkv[0]), "b_w_qkv": f32(b_w_qkv[0]),
        "a_w_o": f32(a_w_o[0]), "b_w_o": f32(b_w_o[0]),
        "w1_0": f32(ffn_w1[0]), "w1_1": f32(ffn_w1[1]), "w2_0": f32(ffn_w2[0]), "w2_1": f32(ffn_w2[1]),
        "oh_a": np.stack([onehot_dsw(r) for _, r in DSW_GROUPS]), "oh_b": onehot_moba(),
        "kind": kind, "cm": cm,
    }
    zeros = np.zeros((D, SEQ), np.float32)
    in_maps = []
    for c in range(8):
        m = dict(common)
        m["xT"] = np.ascontiguousarray(x[REAL_CORES.index(c)].T) if c in REAL_CORES else zeros
        in_maps.append(m)
    nc = build_fused()
    res = run_bass_kernel_spmd(nc, in_maps, core_ids=list(range(8))).results
    out = np.empty((BATCH, SEQ, D), np.float32)
    for b, c in enumerate(REAL_CORES):
        out[b] = np.asarray(res[c]["outT"], np.float32).T
    return out


def kernel_unfused(x, rel_bias, norm_mix, norm_ffn, a_w_qkv, a_q_gain, a_k_gain, a_w_o,
           b_w_qkv, b_q_gain, b_k_gain, b_w_o, ffn_w1, ffn_w2):
    x = np.asarray(x, np.float32)
    rel_bias = np.ascontiguousarray(np.asarray(rel_bias, np.float32))
    TT = SEQ // 2
    hT = [np.ascontiguousarray(x[c // 2, (c % 2) * TT:(c % 2 + 1) * TT].T) for c in range(8)]
    oh_a = np.stack([onehot_dsw(r) for _, r in DSW_GROUPS])
    oh_b = onehot_moba()
    kind, cm = moba_consts(SEQ)
    for layer in range(2):
        is_a = layer == 0
        w_qkv = np.ascontiguousarray(np.asarray(a_w_qkv[0] if is_a else b_w_qkv[0], np.float32))
        CQK, CV = (3072, 1536) if is_a else (2048, 1024)
        gain = _gain_layout(a_q_gain[0], a_k_gain[0]) if is_a else _gain_layout(b_q_gain[0], b_k_gain[0])
        g_mix = _g_layout(norm_mix[layer])
        nc1 = build_lin_in(TT, CQK, CV)
        r1 = _run(nc1, [{"hT": hT[c], "g": g_mix, "w": w_qkv, "gain": gain} for c in range(8)])
        in2 = []
        for c in range(8):
            b, r2 = c // 2, c % 2
            qk = np.concatenate([r1[2 * b]["qkT"], r1[2 * b + 1]["qkT"]], axis=1)
            v = np.concatenate([r1[2 * b]["v"], r1[2 * b + 1]["v"]], axis=0)
            if is_a:
                qT = np.stack([qk[g * 512 + r2 * 256:g * 512 + (r2 + 1) * 256] for g in range(3)])
                kT = np.stack([qk[1536 + g * 512 + r2 * 256:1536 + g * 512 + (r2 + 1) * 256] for g in range(3)])
                vv = np.stack([v[:, g * 512 + r2 * 256:g * 512 + (r2 + 1) * 256] for g in range(3)], axis=1)
                rb = np.concatenate([rel_bias[:, g * 8 + r2 * 4:g * 8 + r2 * 4 + 4] for g in range(3)], axis=1)
                in2.append({"qT": np.ascontiguousarray(qT), "kT": np.ascontiguousarray(kT),
                            "v": np.ascontiguousarray(vv), "rb": np.ascontiguousarray(rb), "oh": oh_a})
            else:
                in2.append({"qT": np.ascontiguousarray(qk[r2 * 512:(r2 + 1) * 512]),
                            "kT": np.ascontiguousarray(qk[1024 + r2 * 512:1024 + (r2 + 1) * 512]),
                            "v": np.ascontiguousarray(v[:, r2 * 512:(r2 + 1) * 512]),
                            "rb": np.ascontiguousarray(rel_bias[:, r2 * 8:(r2 + 1) * 8]),
                            "oh": oh_b, "kind": kind, "cm": cm})
        nc2 = build_dsw(SEQ, 4) if is_a else build_moba(SEQ, 8)
        r2_ = _run(nc2, in2)
        KO = 512 if is_a else 1024
        w_o = np.ascontiguousarray(np.asarray(a_w_o[0] if is_a else b_w_o[0], np.float32))
        w1 = np.ascontiguousarray(np.asarray(ffn_w1[layer], np.float32))
        w2 = np.ascontiguousarray(np.asarray(ffn_w2[layer], np.float32))
        g_ffn = _g_layout(norm_ffn[layer])
        in3 = []
        for c in range(8):
            b, r = c // 2, c % 2
            oT = np.concatenate([r2_[2 * b]["oT"], r2_[2 * b + 1]["oT"]], axis=0)
            in3.append({"hT": hT[c], "oT": np.ascontiguousarray(oT[:, r * TT:(r + 1) * TT]), "g": g_ffn,
                        "wo": w_o, "w1": w1, "w2": w2})
        nc3 = build_lin_out(TT, KO)
        r3 = _run(nc3, in3)
        hT = [np.asarray(r3[c]["hT_out"], np.float32) for c in range(8)]
    out = np.empty((BATCH, SEQ, D), np.float32)
    for c in range(8):
        out[c // 2, (c % 2) * TT:(c % 2 + 1) * TT] = hT[c].T
    return out
```

```python
import contextlib
import numpy as np
import ml_dtypes
import concourse.bass as bass
import concourse.mybir as mybir
from concourse.bass_utils import run_bass_kernel_spmd

F32 = mybir.dt.float32
BF16 = mybir.dt.bfloat16
ALU = mybir.AluOpType
AF = mybir.ActivationFunctionType
AX = mybir.AxisListType

D = 1024
SEQ = 4096
BATCH = 4
HD = 64
EPS = 1e-6
SCALE = HD ** -0.5
NEGB = -30000.0
SAME_ENGINE_SYNC = True


class Buf:
    __slots__ = ("w", "r", "name")

    def __init__(self, name=""):
        self.w = None
        self.r = {}
        self.name = name


class Prog:
    NDQ = 6

    def __init__(self, nc, es):
        self.nc = nc
        self.es = es
        self.eng = {"pe": nc.tensor, "act": nc.scalar, "dve": nc.vector,
                    "pool": nc.gpsimd, "sp": nc.sync}
        self.sem = {}
        self.cnt = {}
        for k in ("pe", "act", "dve", "pool"):
            self.sem[k] = es.enter_context(nc.semaphore(f"s_{k}"))
            self.cnt[k] = 0
        self.waited = {k: {} for k in self.eng}
        self.dq = {}
        for q in ("sp", "pool", "act"):
            sems = [es.enter_context(nc.semaphore(f"d_{q}{i}")) for i in range(self.NDQ)]
            self.dq[q] = {"sems": sems, "val": [0] * self.NDQ, "idx": 0}
        self.pe_open = False
        self.out_tickets = []

    def need(self, e, tk):
        if tk is None:
            return
        sem, val = tk
        key = sem.num
        if self.waited[e].get(key, 0) >= val:
            return
        self.eng[e].wait_ge(sem, val)
        self.waited[e][key] = val

    def _deps(self, e, reads, writes):
        own = self.sem.get(e)
        tks = []
        for b in reads:
            if b.w is not None:
                tks.append(b.w)
        for b in writes:
            if b.w is not None:
                tks.append(b.w)
            tks.extend(b.r.values())
        for tk in tks:
            if own is not None and tk[0].num == own.num:
                if e == "pe" or not SAME_ENGINE_SYNC:
                    continue
            self.need(e, tk)

    def _record(self, tk, reads, writes):
        key = tk[0].num
        for b in reads:
            b.r[key] = tk
        for b in writes:
            b.w = tk
            b.r = {}

    def op(self, e, fn, reads=(), writes=(), inc=True):
        self._deps(e, reads, writes)
        ins = fn()
        if inc:
            self.cnt[e] += 1
            ins.then_inc(self.sem[e], 1)
            tk = (self.sem[e], self.cnt[e])
            if e == "pe":
                self.pe_open = False
        else:
            assert e == "pe"
            tk = (self.sem[e], self.cnt[e] + 1)
            self.pe_open = True
        self._record(tk, reads, writes)
        return tk

    def dma(self, q, out, in_, reads=(), writes=(), is_output=False, **kw):
        d = self.dq[q]
        i = d["idx"] % self.NDQ
        d["idx"] += 1
        if d["val"][i] > 0:
            self.need(q, (d["sems"][i], d["val"][i]))
        self._deps(q, reads, writes)
        ins = self.eng[q].dma_start(out=out, in_=in_, **kw)
        d["val"][i] += 16
        ins.then_inc(d["sems"][i], 16)
        tk = (d["sems"][i], d["val"][i])
        self._record(tk, reads, writes)
        if is_output:
            self.out_tickets.append(tk)
        return tk

    def finish(self):
        assert not self.pe_open
        for q in ("sp", "pool", "act"):
            d = self.dq[q]
            for i in range(self.NDQ):
                if d["val"][i] > 0:
                    self.need("sp", (d["sems"][i], d["val"][i]))
        for e in ("pe", "act", "dve", "pool"):
            if self.cnt[e] > 0:
                self.need("sp", (self.sem[e], self.cnt[e]))

    def barrier(self):
        assert not self.pe_open
        for e in self.eng:
            for q in ("sp", "pool", "act"):
                d = self.dq[q]
                for i in range(self.NDQ):
                    if d["val"][i] > 0:
                        self.need(e, (d["sems"][i], d["val"][i]))
            for x in ("pe", "act", "dve", "pool"):
                if x != e and self.cnt[x] > 0:
                    self.need(e, (self.sem[x], self.cnt[x]))

    def mm(self, out, lhsT, rhs, start, stop, reads, writes, last=None):
        if last is None:
            last = stop
        return self.op("pe", lambda: self.nc.tensor.matmul(out, lhsT, rhs, start=start, stop=stop),
                       reads, writes, inc=last)


def sstep(start, n, step):
    return slice(start, start + (n - 1) * step + 1, step)


_UID = [0]


def _uname(name):
    _UID[0] += 1
    return f"{name}_{_UID[0]}"


def sb(nc, es, name, shape, dt):
    return es.enter_context(nc.sbuf_tensor(_uname(name), shape, dt))


def ps(nc, es, name, shape, dt=F32):
    return es.enter_context(nc.psum_tensor(_uname(name), shape, dt))


def emit_consts(P, nc, es):
    c = {}
    c["avg1024"] = sb(nc, es, "avg1024", [128, 128], BF16)
    c["avg64"] = sb(nc, es, "avg64", [128, 128], BF16)
    c["b_avg1024"] = Buf()
    c["b_avg64"] = Buf()
    P.op("dve", lambda: nc.vector.memset(c["avg1024"][:], 1.0 / 1024.0), (), (c["b_avg1024"],))
    P.op("dve", lambda: nc.vector.memset(c["avg64"][:], 0.0), (), (c["b_avg64"],))
    P.op("dve", lambda: nc.vector.memset(c["avg64"][0:64, 0:64], 1.0 / 64.0), (), (c["b_avg64"],))
    P.op("dve", lambda: nc.vector.memset(c["avg64"][64:128, 64:128], 1.0 / 64.0), (), (c["b_avg64"],))
    c["eps"] = sb(nc, es, "eps_c", [128, 1], F32)
    c["b_eps"] = Buf()
    P.op("dve", lambda: nc.vector.memset(c["eps"][:], EPS), (), (c["b_eps"],))
    return c


def emit_rsqrt(P, nc, C, rstd, b_rstd, pt, b_pt):
    P.op("act", lambda: nc.scalar.activation(out=rstd[:], in_=pt[:], func=AF.Ln, bias=C["eps"][:, 0:1]),
         (b_pt, C["b_eps"]), (b_rstd,))
    P.op("act", lambda: nc.scalar.activation(out=rstd[:], in_=rstd[:], func=AF.Exp, scale=-0.5),
         (b_rstd,), (b_rstd,))


def emit_rmsnorm(P, nc, C, hT, b_h, g_sb, b_g, uT, b_u, TT, psums, tmp):
    NT = TT // 512
    sq, b_sq, rstd, b_rstd = tmp["sq"], tmp["b_sq"], tmp["rstd"], tmp["b_rstd"]
    for t in range(NT):
        ts = slice(t * 512, (t + 1) * 512)
        pt, b_pt = psums[t % 2]
        for c in range(8):
            i = c % 2
            P.op("act", lambda: nc.scalar.activation(out=sq[i][:], in_=hT[:, c, ts], func=AF.Square),
                 (b_h[c][t],), (b_sq[i],))
            P.mm(pt[:], C["avg1024"][:], sq[i][:], c == 0, c == 7,
                 (C["b_avg1024"], b_sq[i]), (b_pt,), last=True)
        emit_rsqrt(P, nc, C, rstd, b_rstd, pt, b_pt)
        for c in range(8):
            e = "dve"
            eng = nc.vector
            P.op(e, lambda: eng.scalar_tensor_tensor(out=uT[:, c, ts], in0=hT[:, c, ts],
                                                     scalar=g_sb[:, c:c + 1], in1=rstd[:],
                                                     op0=ALU.mult, op1=ALU.mult),
                 (b_h[c][t], b_g, b_rstd), (b_u[c][t],))


def build_lin_in(TT, CQK, CV):
    nc = bass.Bass("TRN2", target_bir_lowering=False)
    CW = CQK + CV
    hT_d = nc.dram_tensor("hT", [D, TT], F32, kind="ExternalInput").ap()
    g_d = nc.dram_tensor("g", [128, 8], F32, kind="ExternalInput").ap()
    w_d = nc.dram_tensor("w", [D, CW], F32, kind="ExternalInput").ap()
    gain_d = nc.dram_tensor("gain", [128, 2], F32, kind="ExternalInput").ap()
    qk_d = nc.dram_tensor("qkT", [CQK, TT], BF16, kind="ExternalOutput").ap()
    v_d = nc.dram_tensor("v", [TT, CV], BF16, kind="ExternalOutput").ap()
    with contextlib.ExitStack() as es0:
        P = Prog(nc, es0)
        emit_lin_in(P, nc, TT, CQK, CV, hT_d, g_d, w_d, gain_d, qk_d, v_d)
        P.finish()
    return nc


def emit_lin_in(P, nc, TT, CQK, CV, hT_d, g_d, w_d, gain_d, qk_d, v_d):
    CW = CQK + CV
    NT = TT // 512
    with contextlib.ExitStack() as es:
        C = emit_consts(P, nc, es)
        hT = sb(nc, es, "hT_sb", [128, 8, TT], F32)
        uT = sb(nc, es, "uT_sb", [128, 8, TT], BF16)
        g_sb = sb(nc, es, "g_sb", [128, 8], F32)
        gain_sb = sb(nc, es, "gain_sb", [128, 2], F32)
        b_h = [[Buf() for _ in range(NT)] for _ in range(8)]
        b_u = [[Buf() for _ in range(NT)] for _ in range(8)]
        b_g, b_gain = Buf(), Buf()
        tmp = {"sq": [sb(nc, es, f"sq{i}", [128, 512], BF16) for i in range(2)],
               "b_sq": [Buf(), Buf()],
               "rstd": sb(nc, es, "rstd", [128, 512], F32), "b_rstd": Buf()}
        psums = [(ps(nc, es, f"ps{i}", [128, 512]), Buf()) for i in range(6)]
        P.dma("sp", g_sb[:], g_d, (), (b_g,))
        P.dma("sp", gain_sb[:], gain_d, (), (b_gain,))
        for c in range(8):
            for t in range(NT):
                P.dma("sp", hT[:, c, t * 512:(t + 1) * 512], hT_d[c * 128:(c + 1) * 128, t * 512:(t + 1) * 512],
                      (), (b_h[c][t],))
        emit_rmsnorm(P, nc, C, hT, b_h, g_sb, b_g, uT, b_u, TT, psums[0:2], tmp)

        NG = CW // 512
        wb = [sb(nc, es, f"wb{i}", [128, 8, 512], BF16) for i in range(2)]
        b_wb = [Buf(), Buf()]
        osb = [sb(nc, es, f"osb{i}", [128, 512], BF16) for i in range(3)]
        b_osb = [Buf() for _ in range(3)]
        oi = 0
        w_v = w_d.rearrange("(c p) n -> p c n", p=128)

        def load_w(gi):
            for c in range(8):
                P.dma("pool", wb[gi % 2][:, c, :], w_v[:, c, gi * 512:(gi + 1) * 512], (), (b_wb[gi % 2],))

        load_w(0)
        pi = 2
        pending = []
        rstd2 = [tmp["rstd"], sb(nc, es, "rstd_b", [128, 512], F32)]
        b_rstd2 = [tmp["b_rstd"], Buf()]
        for gi in range(NG):
            if gi + 1 < NG:
                load_w(gi + 1)
            w = wb[gi % 2]
            bw = b_wb[gi % 2]
            if gi * 512 < CQK:
                isq = 0 if gi * 512 < CQK // 2 else 1
                for j in range(4):
                    col0 = gi * 512 + j * 128
                    for t in range(NT):
                        ts = slice(t * 512, (t + 1) * 512)
                        pa, b_pa = psums[pi % 4]
                        pb, b_pb = psums[4 + (pi % 2)]
                        i = pi % 2
                        pi += 1
                        for c in range(8):
                            P.mm(pa[:], w[:, c, j * 128:(j + 1) * 128], uT[:, c, ts], c == 0, c == 7,
                                 (bw, b_u[c][t]), (b_pa,))
                        sq, b_sq = tmp["sq"][i], tmp["b_sq"][i]
                        P.op("act", lambda: nc.scalar.activation(out=sq[:], in_=pa[:], func=AF.Square),
                             (b_pa,), (b_sq,))

                        def stage2(pa=pa, b_pa=b_pa, pb=pb, b_pb=b_pb, sq=sq, b_sq=b_sq, i=i, isq=isq,
                                   col0=col0, ts=ts):
                            nonlocal oi
                            P.mm(pb[:], C["avg64"][:], sq[:], True, True, (C["b_avg64"], b_sq), (b_pb,))
                            rstd, b_rstd = rstd2[i], b_rstd2[i]
                            emit_rsqrt(P, nc, C, rstd, b_rstd, pb, b_pb)
                            o, b_o = osb[oi % 3], b_osb[oi % 3]
                            oi += 1
                            P.op("dve", lambda: nc.vector.scalar_tensor_tensor(out=o[:], in0=pa[:],
                                                                               scalar=gain_sb[:, isq:isq + 1],
                                                                               in1=rstd[:],
                                                                               op0=ALU.mult, op1=ALU.mult),
                                 (b_pa, b_gain, b_rstd), (b_o,))
                            P.dma("sp", qk_d[col0:col0 + 128, ts], o[:], (b_o,), (), is_output=True)

                        if pending:
                            pending.pop()()
                        pending.append(stage2)
            else:
                if pending:
                    pending.pop()()
                vc0 = gi * 512 - CQK
                for tt in range(TT // 128):
                    t = tt // 4
                    tsl = slice(tt * 128, (tt + 1) * 128)
                    pa, b_pa = psums[2 + (pi % 2)]
                    pi += 1
                    for c in range(8):
                        P.mm(pa[:], uT[:, c, tsl], w[:, c, :], c == 0, c == 7, (bw, b_u[c][t]), (b_pa,))
                    o, b_o = osb[oi % 3], b_osb[oi % 3]
                    oi += 1
                    P.op("act", lambda: nc.scalar.copy(out=o[:], in_=pa[:]), (b_pa,), (b_o,))
                    P.dma("sp", v_d[tsl, vc0:vc0 + 512], o[:], (b_o,), (), is_output=True)
        P.barrier()


def build_lin_out(TT, KO, DFF=4096):
    nc = bass.Bass("TRN2", target_bir_lowering=False)
    hT_d = nc.dram_tensor("hT", [D, TT], F32, kind="ExternalInput").ap()
    oT_d = nc.dram_tensor("oT", [KO, TT], BF16, kind="ExternalInput").ap()
    g_d = nc.dram_tensor("g", [128, 8], F32, kind="ExternalInput").ap()
    wo_d = nc.dram_tensor("wo", [KO, D], F32, kind="ExternalInput").ap()
    w1_d = nc.dram_tensor("w1", [D, DFF], F32, kind="ExternalInput").ap()
    w2_d = nc.dram_tensor("w2", [DFF, D], F32, kind="ExternalInput").ap()
    out_d = nc.dram_tensor("hT_out", [D, TT], F32, kind="ExternalOutput").ap()
    with contextlib.ExitStack() as es0:
        P = Prog(nc, es0)
        emit_lin_out(P, nc, TT, KO, DFF, hT_d, oT_d, g_d, wo_d, w1_d, w2_d, out_d)
        P.finish()
    return nc


def emit_lin_out(P, nc, TT, KO, DFF, hT_d, oT_d, g_d, wo_d, w1_d, w2_d, out_d):
    NT = TT // 512
    KC = KO // 128
    NFG = DFF // 512
    with contextlib.ExitStack() as es:
        C = emit_consts(P, nc, es)
        hT = sb(nc, es, "hT_sb", [128, 8, TT], F32)
        uT = sb(nc, es, "uT_sb", [128, 8, TT], BF16)
        aT = sb(nc, es, "aT_sb", [128, 4, TT], BF16)
        g_sb = sb(nc, es, "g_sb", [128, 8], F32)
        b_h = [[Buf() for _ in range(NT)] for _ in range(8)]
        b_u = [[Buf() for _ in range(NT)] for _ in range(8)]
        b_a = [[Buf() for _ in range(NT)] for _ in range(4)]
        b_g = Buf()
        tmp = {"sq": [sb(nc, es, f"sq{i}", [128, 512], BF16) for i in range(2)],
               "b_sq": [Buf(), Buf()],
               "rstd": sb(nc, es, "rstd", [128, 512], F32), "b_rstd": Buf()}
        rl = [sb(nc, es, f"rl{i}", [128, 512], F32) for i in range(2)]
        b_rl = [Buf(), Buf()]
        psums = [(ps(nc, es, f"ps{i}", [128, 512]), Buf()) for i in range(6)]
        wb = [sb(nc, es, f"wb{i}", [128, 8, 512], BF16) for i in range(2)]
        b_wb = [Buf(), Buf()]
        w2b = [sb(nc, es, f"w2b{i}", [128, 4, D], BF16) for i in range(2)]
        b_w2b = [Buf(), Buf()]
        P.dma("sp", g_sb[:], g_d, (), (b_g,))
        for c in range(8):
            for t in range(NT):
                P.dma("sp", hT[:, c, t * 512:(t + 1) * 512], hT_d[c * 128:(c + 1) * 128, t * 512:(t + 1) * 512],
                      (), (b_h[c][t],))
        for c in range(KC):
            for t in range(NT):
                P.dma("act", uT[:, c, t * 512:(t + 1) * 512], oT_d[c * 128:(c + 1) * 128, t * 512:(t + 1) * 512],
                      (), (b_u[c][t],))
        wo_v = wo_d.rearrange("(c p) n -> p c n", p=128)
        w1_v = w1_d.rearrange("(c p) n -> p c n", p=128)
        w2_v = w2_d.rearrange("(f p) n -> p f n", p=128)
        wi = 0

        def load_wo(gi, slot):
            for c in range(KC):
                P.dma("pool", wb[slot][:, c, :], wo_v[:, c, gi * 512:(gi + 1) * 512], (), (b_wb[slot],))

        def load_w1(fg, slot):
            for c in range(8):
                P.dma("pool", wb[slot][:, c, :], w1_v[:, c, fg * 512:(fg + 1) * 512], (), (b_wb[slot],))

        def load_w2(fg, slot):
            for f in range(4):
                P.dma("pool", w2b[slot][:, f, :], w2_v[:, fg * 4 + f, :], (), (b_w2b[slot],))

        load_wo(0, 0)
        load_wo(1, 1)
        pi = 0
        for gi in range(2):
            w, bw = wb[gi], b_wb[gi]
            for j in range(4):
                cj = gi * 4 + j
                for t in range(NT):
                    ts = slice(t * 512, (t + 1) * 512)
                    pa, b_pa = psums[2 + (pi % 4)]
                    pi += 1
                    for c in range(KC):
                        P.mm(pa[:], w[:, c, j * 128:(j + 1) * 128], uT[:, c, ts], c == 0, c == KC - 1,
                             (bw, b_u[c][t]), (b_pa,))
                    P.op("dve", lambda: nc.vector.tensor_tensor(out=hT[:, cj, ts], in0=pa[:], in1=hT[:, cj, ts],
                                                                op=ALU.add),
                         (b_pa, b_h[cj][t]), (b_h[cj][t],))
        load_w1(0, 0)
        load_w2(0, 0)
        emit_rmsnorm(P, nc, C, hT, b_h, g_sb, b_g, uT, b_u, TT, psums[0:2], tmp)
        ri = 0
        for fg in range(NFG):
            slot = fg % 2
            if fg + 1 < NFG:
                load_w1(fg + 1, 1 - slot)
                load_w2(fg + 1, 1 - slot)
            w, bw = wb[slot], b_wb[slot]
            w2, bw2 = w2b[slot], b_w2b[slot]
            for f in range(4):
                for t in range(NT):
                    ts = slice(t * 512, (t + 1) * 512)
                    pa, b_pa = psums[2 + (pi % 4)]
                    pi += 1
                    for c in range(8):
                        P.mm(pa[:], w[:, c, f * 128:(f + 1) * 128], uT[:, c, ts], c == 0, c == 7,
                             (bw, b_u[c][t]), (b_pa,))
                    r, b_r = rl[ri % 2], b_rl[ri % 2]
                    ri += 1
                    P.op("act", lambda: nc.scalar.activation(out=r[:], in_=pa[:], func=AF.Relu), (b_pa,), (b_r,))
                    P.op("pool", lambda: nc.gpsimd.tensor_tensor(out=aT[:, f, ts], in0=r[:], in1=r[:], op=ALU.mult),
                         (b_r,), (b_a[f][t],))
            for j in range(8):
                for t in range(NT):
                    ts = slice(t * 512, (t + 1) * 512)
                    pa, b_pa = psums[2 + (pi % 4)]
                    pi += 1
                    for f in range(4):
                        P.mm(pa[:], w2[:, f, j * 128:(j + 1) * 128], aT[:, f, ts], f == 0, f == 3,
                             (bw2, b_a[f][t]), (b_pa,))
                    P.op("dve", lambda: nc.vector.tensor_tensor(out=hT[:, j, ts], in0=pa[:], in1=hT[:, j, ts],
                                                                op=ALU.add),
                         (b_pa, b_h[j][t]), (b_h[j][t],))
                    if fg == NFG - 1:
                        P.dma("sp", out_d[j * 128:(j + 1) * 128, ts], hT[:, j, ts], (b_h[j][t],), (), is_output=True)
        P.barrier()


def t5_bucket_np(dist):
    n = np.maximum(dist, 0)
    nf = np.maximum(n, 1).astype(np.float32)
    large = 16 + (np.log(nf / np.float32(16)) / np.float32(np.log(2048 / 16)) * np.float32(16)).astype(np.int32)
    large = np.minimum(large, 31)
    return np.where(n < 16, n, large)


def onehot_dsw(r):
    m = np.arange(384) - 127
    valid = (m >= 0) & (m <= 128)
    b = np.where(valid, t5_bucket_np(m * r), 32)
    oh = np.zeros((33, 384), np.float32)
    oh[b, np.arange(384)] = 1.0
    return oh.astype(ml_dtypes.bfloat16)


MOBA_W = 2304
MOBA_L = MOBA_W + 128


def onehot_moba():
    dd = np.arange(MOBA_L) - 255
    b = np.where(dd >= 0, t5_bucket_np(dd), 32)
    oh = np.zeros((33, MOBA_L), np.float32)
    oh[b, np.arange(MOBA_L)] = 1.0
    return oh.astype(ml_dtypes.bfloat16)


def emit_bias_rows(P, nc, es, rb_d, ncols):
    Bsb = sb(nc, es, "Bsb", [33, ncols], F32)
    b_B = Buf()
    P.op("dve", lambda: nc.vector.memset(Bsb[:], NEGB / 8.0), (), (b_B,))
    P.dma("sp", Bsb[0:32, :], rb_d, (), (b_B,))
    ones33 = sb(nc, es, "ones33", [33, 128], F32)
    b_o = Buf()
    P.op("dve", lambda: nc.vector.memset(ones33[:], 1.0), (), (b_o,))
    return Bsb, b_B, ones33, b_o


def emit_toeplitz(P, nc, Bsb, b_B, ones33, b_o, col, oh_sb, b_oh, L, brep, b_brep, frow, b_frow,
                  scr_d, pst, b_pst, out_ap, W, b_out):
    P.op("dve", lambda: nc.vector.tensor_scalar(out=brep[:], in0=ones33[:], scalar1=Bsb[:, col:col + 1], scalar2=8.0,
                                                op0=ALU.mult, op1=ALU.mult),
         (b_B, b_o), (b_brep,))
    for c0 in range(0, L, 512):
        n = min(512, L - c0)
        P.mm(pst[:, 0:n], brep[:], oh_sb[:, c0:c0 + n], True, True, (b_brep, b_oh), (b_pst,))
        P.op("dve", lambda: nc.vector.tensor_copy(out=frow[:, c0:c0 + n], in_=pst[:, 0:n]), (b_pst,), (b_frow,))
    b_scr = Buf()
    P.dma("sp", scr_d[:, 0:L], frow[:, 0:L], (b_frow,), (b_scr,))
    src = bass.AP(scr_d.tensor, scr_d.offset + 127, [[scr_d.ap[0][0] - 1, 128], [1, W]])
    P.dma("sp", out_ap, src, (b_scr,), (b_out,))


def emit_normalize(P, nc, acc_ap, b_acc, n, A, psB, b_psB, o_sb, b_osb, out_dram_ap, slot=0):
    R32, b_R32 = A["R32"][slot], A["b_R32"][slot]
    Rh, b_Rh = A["Rh"][slot], A["b_Rh"][slot]
    Rl, b_Rl = A["Rl"][slot], A["b_Rl"][slot]
    P.op("dve", lambda: nc.vector.reciprocal(out=R32[64:65, 0:n], in_=acc_ap[64:65, :]), (b_acc,), (b_R32,))
    P.op("dve", lambda: nc.vector.tensor_copy(out=Rh[64:65, 0:n], in_=R32[64:65, 0:n]), (b_R32,), (b_Rh,))
    P.op("dve", lambda: nc.vector.tensor_tensor(out=Rl[64:65, 0:n], in0=R32[64:65, 0:n], in1=Rh[64:65, 0:n],
                                                op=ALU.subtract),
         (b_R32, b_Rh), (b_Rl,))

    def part2():
        P.mm(psB[0:64, 0:n], A["E"][:], Rh[:, 0:n], True, False, (A["b_E"], b_Rh), (b_psB,), last=False)
        P.mm(psB[0:64, 0:n], A["E"][:], Rl[:, 0:n], False, True, (A["b_E"], b_Rl), (b_psB,))
        P.op("dve", lambda: nc.vector.tensor_tensor(out=o_sb[0:64, 0:n], in0=acc_ap[0:64, :], in1=psB[0:64, 0:n],
                                                    op=ALU.mult),
             (b_acc, b_psB), (b_osb,))
        P.dma("sp", out_dram_ap, o_sb[0:64, 0:n], (b_osb,), (), is_output=True)

    return part2


def emit_attn_consts(P, nc, es):
    A = {}
    A["ident"] = sb(nc, es, "ident", [128, 128], BF16)
    A["b_ident"] = Buf()
    P.op("pool", lambda: nc.gpsimd.memset(A["ident"][:], 1.0), (), (A["b_ident"],))
    P.op("pool", lambda: nc.gpsimd.affine_select(out=A["ident"][:], in_=A["ident"][:], pattern=[[-1, 128]],
                                                 compare_op=ALU.is_equal, fill=0.0, base=0, channel_multiplier=1),
         (A["b_ident"],), (A["b_ident"],))
    A["E"] = sb(nc, es, "Esel", [65, 64], BF16)
    A["b_E"] = Buf()
    P.op("dve", lambda: nc.vector.memset(A["E"][:], 0.0), (), (A["b_E"],))
    P.op("dve", lambda: nc.vector.memset(A["E"][64:65, :], 1.0), (), (A["b_E"],))
    for nm, dt in (("R32", F32), ("Rh", BF16), ("Rl", BF16)):
        A[nm] = [sb(nc, es, f"{nm}_{i}", [65, 512], dt) for i in range(2)]
        A["b_" + nm] = [Buf(), Buf()]
        for i in range(2):
            P.op("dve", lambda: nc.vector.memset(A[nm][i][:], 0.0), (), (A["b_" + nm][i],))
    return A


DSW_GROUPS = ((128, 1), (512, 4), (2048, 16))


def build_dsw(S, NHM):
    nc = bass.Bass("TRN2", target_bir_lowering=False)
    q_d = nc.dram_tensor("qT", [3, NHM * 64, S], BF16, kind="ExternalInput").ap()
    k_d = nc.dram_tensor("kT", [3, NHM * 64, S], BF16, kind="ExternalInput").ap()
    v_d = nc.dram_tensor("v", [S, 3, NHM * 64], BF16, kind="ExternalInput").ap()
    rb_d = nc.dram_tensor("rb", [32, 3 * NHM], F32, kind="ExternalInput").ap()
    oh_d = nc.dram_tensor("oh", [3, 33, 384], BF16, kind="ExternalInput").ap()
    o_d = nc.dram_tensor("oT", [NHM * 64, S], BF16, kind="ExternalOutput").ap()
    scr_d = nc.dram_tensor("scr", [3 * NHM, 128, 384], BF16, kind="Internal").ap()
    with contextlib.ExitStack() as es0:
        P = Prog(nc, es0)
        emit_dsw(P, nc, S, NHM, q_d, k_d, v_d, rb_d, oh_d, o_d, scr_d)
        P.finish()
    return nc


def emit_dsw(P, nc, S, NHM, q_d, k_d, v_d, rb_d, oh_d, o_d, scr_d):
    NB = S // 128
    vpitch = v_d.ap[0][0]
    gpitch = v_d.ap[1][0]
    with contextlib.ExitStack() as es:
        A = emit_attn_consts(P, nc, es)
        Bsb, b_B, ones33, b_o = emit_bias_rows(P, nc, es, rb_d, 3 * NHM)
        oh_sb = sb(nc, es, "oh_sb", [33, 3, 384], BF16)
        b_oh = Buf()
        for g in range(3):
            P.dma("sp", oh_sb[:, g, :], oh_d[g], (), (b_oh,))
        brep = sb(nc, es, "brep", [33, 128], BF16)
        b_brep = Buf()
        frow = sb(nc, es, "frow", [128, 384], BF16)
        b_frow = Buf()
        T = sb(nc, es, "Ttab", [128, 3 * NHM, 256], BF16)
        b_T = [Buf() for _ in range(3 * NHM)]
        psS = [(ps(nc, es, f"psS{i}", [128, 512]), Buf()) for i in range(3)]
        psO = [(ps(nc, es, f"psO{i}", [128, 512]), Buf()) for i in range(4)]
        pst, b_pst = ps(nc, es, "pst", [128, 512]), Buf()
        psB, b_psB = pst, b_pst
        for g in range(3):
            for hh in range(NHM):
                col = g * NHM + hh
                emit_toeplitz(P, nc, Bsb, b_B, ones33, b_o, col, oh_sb[:, g, :], b_oh, 384, brep, b_brep, frow, b_frow,
                              scr_d[col], pst, b_pst, T[:, col, :], 256, b_T[col])
        Vgs = [sb(nc, es, f"Vg{i}", [128, 3, NB, 2, 65], BF16) for i in range(2)]
        b_Vs = [Buf(), Buf()]
        for i in range(2):
            P.op("pool", lambda: nc.gpsimd.memset(Vgs[i][:], 1.0), (), (b_Vs[i],))

        def load_v(pair):
            Vg_, b_V_ = Vgs[pair % 2], b_Vs[pair % 2]
            for g, (win, r) in enumerate(DSW_GROUPS):
                nb = NB // r
                for c in range(r):
                    for hl in range(2):
                        hh = pair * 2 + hl
                        src = bass.AP(v_d.tensor, v_d.offset + c * vpitch + g * gpitch + hh * 64,
                                      [[r * vpitch, 128], [128 * r * vpitch, nb], [1, 64]])
                        P.dma("pool", Vg_[:, g, c * nb:(c + 1) * nb, hl, 0:64], src, (), (b_V_,))

        load_v(0)
        QT = [sb(nc, es, f"QT{i}", [128, S], BF16) for i in range(2)]
        KT = [sb(nc, es, f"KT{i}", [128, S], BF16) for i in range(2)]
        b_QT = [Buf(), Buf()]
        b_KT = [Buf(), Buf()]
        QN = sb(nc, es, "QN", [128, S], BF16)
        KN = sb(nc, es, "KN", [128, S], BF16)
        b_QN, b_KN = Buf(), Buf()
        acc = sb(nc, es, "acc", [65, 2, S], F32)
        b_acc = Buf()
        PT = [sb(nc, es, f"PT{i}", [128, 512], BF16) for i in range(4)]
        b_PT = [Buf() for _ in range(4)]
        o_sb = [sb(nc, es, f"o_sb{i}", [64, 512], BF16) for i in range(2)]
        b_osb = [Buf(), Buf()]
        oi = 0
        GORDER = (2, 1, 0)
        jobs = [(pair, g) for pair in range(NHM // 2) for g in GORDER]

        def load_qk(ji):
            pair, g = jobs[ji]
            slot = ji % 2
            r = DSW_GROUPS[g][1]
            qsrc = q_d[g, pair * 128:(pair + 1) * 128, :]
            ksrc = k_d[g, pair * 128:(pair + 1) * 128, :]
            if r == 1:
                P.dma("sp", QT[slot][:], qsrc, (), (b_QT[slot],))
                P.dma("sp", KT[slot][:], ksrc, (), (b_KT[slot],))
            else:
                P.dma("sp", QN[:], qsrc, (), (b_QN,))
                P.dma("sp", KN[:], ksrc, (), (b_KN,))
                P.op("dve", lambda: nc.vector.tensor_copy(out=QT[slot][:].rearrange("p (c l) -> p c l", c=r),
                                                          in_=QN[:].rearrange("p (l c) -> p c l", c=r)),
                     (b_QN,), (b_QT[slot],))
                P.op("dve", lambda: nc.vector.tensor_copy(out=KT[slot][:].rearrange("p (c l) -> p c l", c=r),
                                                          in_=KN[:].rearrange("p (l c) -> p c l", c=r)),
                     (b_KN,), (b_KT[slot],))

        tiles = []
        for ji, (pair, g) in enumerate(jobs):
            r = DSW_GROUPS[g][1]
            nb = NB // r
            for c in range(r):
                for j in range(nb):
                    tiles.append((ji, c, j))
        first_tile_of_job = {}
        for k, (ji, c, j) in enumerate(tiles):
            first_tile_of_job.setdefault(ji, k)

        def emit_S(tl, k):
            ji, c, j = tl
            pair, g = jobs[ji]
            slot = ji % 2
            r = DSW_GROUPS[g][1]
            nb = NB // r
            L = S // r
            qt, kt = QT[slot], KT[slot]
            nqb = 2 if j + 1 < nb else 1
            k0 = c * L + 128 * j
            pS, b_pS = psS[k % 3]
            for hl in range(2):
                hh = pair * 2 + hl
                rows = slice(hl * 64, (hl + 1) * 64)
                cols = slice(hl * 256, hl * 256 + 128 * nqb)
                P.mm(pS[:, cols], kt[rows, k0:k0 + 128], qt[rows, k0:k0 + 128 * nqb],
                     True, False, (b_KT[slot], b_QT[slot]), (b_pS,), last=False)
                P.mm(pS[:, cols], A["ident"][:], T[:, g * NHM + hh, 0:128 * nqb], False, True,
                     (A["b_ident"], b_T[g * NHM + hh]), (b_pS,), last=(hl == 1))
            pt, b_pt = PT[k % 4], b_PT[k % 4]
            if nqb == 2:
                src, dst = pS[:, 0:512], pt[:, 0:512]
            else:
                src = pS[:, 0:512].rearrange("p (h x) -> p h x", h=2)[:, :, 0:128]
                dst = pt[:, 0:512].rearrange("p (h x) -> p h x", h=2)[:, :, 0:128]
            P.op("act", lambda: nc.scalar.activation(out=dst, in_=src, func=AF.Exp, scale=SCALE),
                 (b_pS,), (b_pt,))

        deferred = []

        def emit_PV(tl, k):
            nonlocal oi
            ji, c, j = tl
            pair, g = jobs[ji]
            r = DSW_GROUPS[g][1]
            nb = NB // r
            Vg, b_Vp = Vgs[pair % 2], b_Vs[pair % 2]
            nqb = 2 if j + 1 < nb else 1
            pt, b_pt = PT[k % 4], b_PT[k % 4]
            for qi in range(nqb):
                i = j + qi
                start = (i == 0) or (j == i - 1)
                stop = (j == i)
                for hl in range(2):
                    pO, b_pO = psO[(i % 2) * 2 + hl]
                    P.mm(pO[0:65, 0:128], Vg[:, g, c * nb + j, hl, :],
                         pt[:, hl * 256 + qi * 128:hl * 256 + (qi + 1) * 128],
                         start, stop, (b_Vp, b_pt), (b_pO,), last=True)
                    if stop:
                        p0 = c + 128 * i * r
                        dst = acc[:, hl, sstep(p0, 128, r)]
                        if g == GORDER[0]:
                            P.op("dve", lambda: nc.vector.tensor_copy(out=dst, in_=pO[0:65, 0:128]), (b_pO,), (b_acc,))
                        else:
                            P.op("dve", lambda: nc.vector.tensor_tensor(out=dst, in0=pO[0:65, 0:128], in1=dst,
                                                                        op=ALU.add),
                                 (b_pO, b_acc), (b_acc,))
                if stop and g == GORDER[-1] and i % 4 == 3:
                    ch = i // 4
                    cs = slice(ch * 512, (ch + 1) * 512)
                    for hl in range(2):
                        hh = pair * 2 + hl
                        p2 = emit_normalize(P, nc, acc[:, hl, cs], b_acc, 512, A, psB, b_psB,
                                            o_sb[oi % 2], b_osb[oi % 2], o_d[hh * 64:(hh + 1) * 64, cs], slot=oi % 2)
                        deferred.append((k + 2 + hl, p2))
                        oi += 1

        DEPTH = 2
        load_qk(0)
        n = len(tiles)
        for k in range(n + DEPTH):
            if k < n:
                ji = tiles[k][0]
                if first_tile_of_job[ji] == k and ji + 1 < len(jobs):
                    load_qk(ji + 1)
                emit_S(tiles[k], k)
            if k >= DEPTH:
                emit_PV(tiles[k - DEPTH], k - DEPTH)
            while deferred and deferred[0][0] <= k:
                deferred.pop(0)[1]()
            if k < n:
                ji2 = tiles[k][0]
                pair2, g2 = jobs[ji2]
                if g2 == GORDER[0] and k == first_tile_of_job[ji2] + DEPTH and pair2 + 1 < NHM // 2:
                    load_v(pair2 + 1)
        while deferred:
            deferred.pop(0)[1]()
        P.barrier()


def moba_consts(S):
    nblk = S // 256
    kind = np.zeros((16, S), np.float32)
    for n in range(nblk):
        kind[n, n * 256:(n + 1) * 256] = 1.0
    nqt = S // 128
    cm = np.full((nqt, 16), -1e30, np.float32)
    for qt in range(nqt):
        cm[qt, :qt // 2] = 0.0
        cm[qt, qt // 2] = 1e30
    cm = np.broadcast_to(cm.reshape(1, nqt * 16), (128, nqt * 16)).copy()
    return kind.astype(ml_dtypes.bfloat16), cm


def build_moba(S, NH):
    nc = bass.Bass("TRN2", target_bir_lowering=False)
    NQT = S // 128
    NQC = S // 512
    q_d = nc.dram_tensor("qT", [NH * 64, S], BF16, kind="ExternalInput").ap()
    k_d = nc.dram_tensor("kT", [NH * 64, S], BF16, kind="ExternalInput").ap()
    v_d = nc.dram_tensor("v", [S, NH * 64], BF16, kind="ExternalInput").ap()
    rb_d = nc.dram_tensor("rb", [32, NH], F32, kind="ExternalInput").ap()
    oh_d = nc.dram_tensor("oh", [33, MOBA_L], BF16, kind="ExternalInput").ap()
    kind_d = nc.dram_tensor("kind", [16, S], BF16, kind="ExternalInput").ap()
    cm_d = nc.dram_tensor("cm", [128, NQT * 16], F32, kind="ExternalInput").ap()
    o_d = nc.dram_tensor("oT", [NH * 64, S], BF16, kind="ExternalOutput").ap()
    scr_d = nc.dram_tensor("scr", [NH, 128, MOBA_L], BF16, kind="Internal").ap()
    with contextlib.ExitStack() as es0:
        P = Prog(nc, es0)
        emit_moba(P, nc, S, NH, q_d, k_d, v_d, rb_d, oh_d, kind_d, cm_d, o_d, scr_d)
        P.finish()
    return nc


def emit_moba(P, nc, S, NH, q_d, k_d, v_d, rb_d, oh_d, kind_d, cm_d, o_d, scr_d):
    NQT = S // 128
    NQC = S // 512
    vpitch = v_d.ap[0][0]
    with contextlib.ExitStack() as es:
        A = emit_attn_consts(P, nc, es)
        Bsb, b_B, ones33, b_o = emit_bias_rows(P, nc, es, rb_d, NH)
        oh_sb = sb(nc, es, "oh_sb", [33, MOBA_L], BF16)
        b_oh = Buf()
        P.dma("sp", oh_sb[:], oh_d, (), (b_oh,))
        cm_sb = sb(nc, es, "cm_sb", [128, NQT * 16], F32)
        b_cm = Buf()
        P.dma("sp", cm_sb[:], cm_d, (), (b_cm,))
        brep = sb(nc, es, "brep", [33, 128], BF16)
        b_brep = Buf()
        frow = sb(nc, es, "frow", [128, MOBA_L], BF16)
        b_frow = Buf()
        TB = [sb(nc, es, f"TB{i}", [128, MOBA_W], BF16) for i in range(2)]
        b_TB = [Buf(), Buf()]
        psS = [(ps(nc, es, f"psS{i}", [128, 512]), Buf()) for i in range(3)]
        psO = [(ps(nc, es, f"psO{i}", [128, 512]), Buf()) for i in range(2)]
        psB, b_psB = ps(nc, es, "psB", [128, 512]), Buf()
        pst, b_pst = ps(nc, es, "pst", [128, 512]), Buf()
        psG, b_psG = pst, b_pst
        psT, b_psT = ps(nc, es, "psT", [128, 1024], BF16), Buf()
        Vh = sb(nc, es, "Vh", [128, NQT, NH, 65], BF16)
        b_V = Buf()
        P.op("pool", lambda: nc.gpsimd.memset(Vh[:], 1.0), (), (b_V,))
        for h in range(NH):
            src = bass.AP(v_d.tensor, v_d.offset + h * 64, [[vpitch, 128], [128 * vpitch, NQT], [1, 64]])
            P.dma("pool", Vh[:, :, h, 0:64], src, (), (b_V,))
        QTa = [sb(nc, es, f"QTa{i}", [128, S], BF16) for i in range(2)]
        KTa = [sb(nc, es, f"KTa{i}", [128, S], BF16) for i in range(2)]
        b_Q = [Buf(), Buf()]
        b_K = [Buf(), Buf()]
        for i in range(2):
            P.op("pool", lambda: nc.gpsimd.memset(QTa[i][0:64, :], 0.0), (), (b_Q[i],))
            P.op("pool", lambda: nc.gpsimd.memset(KTa[i][0:64, :], 0.0), (), (b_K[i],))
            P.dma("sp", KTa[i][0:16, :], kind_d, (), (b_K[i],))
        km = sb(nc, es, "km", [128, 16], F32)
        kmh = sb(nc, es, "kmh", [128, 16], BF16)
        kml = sb(nc, es, "kml", [128, 16], BF16)
        kmr = sb(nc, es, "kmr", [128, 16], F32)
        b_km, b_kmh, b_kml, b_kmr = Buf(), Buf(), Buf(), Buf()
        gm = sb(nc, es, "gm", [128, NQT * 16], F32)
        b_gm = Buf()
        top8 = sb(nc, es, "top8", [128, NQT * 8], F32)
        b_top8 = Buf()
        thr = sb(nc, es, "thr", [128, NQT], F32)
        b_thr = Buf()
        mb = sb(nc, es, "mb", [128, NQT * 16], BF16)
        b_mb = Buf()
        PT = [sb(nc, es, f"PT{i}", [128, 512], BF16) for i in range(4)]
        b_PT = [Buf() for _ in range(4)]
        accs = [sb(nc, es, f"accs{i}", [65, 512], F32) for i in range(2)]
        b_accs = [Buf(), Buf()]
        o_sb = [sb(nc, es, f"o_sb{i}", [64, 512], BF16) for i in range(2)]
        b_osb = [Buf(), Buf()]
        ti = 0
        oi = 0

        def preamble(h):
            sl = h % 2
            qa, ka = QTa[sl], KTa[sl]
            P.dma("sp", qa[64:128, :], q_d[h * 64:(h + 1) * 64, :], (), (b_Q[sl],))
            P.dma("sp", ka[64:128, :], k_d[h * 64:(h + 1) * 64, :], (), (b_K[sl],))
            emit_toeplitz(P, nc, Bsb, b_B, ones33, b_o, h, oh_sb, b_oh, MOBA_L, brep, b_brep, frow, b_frow,
                          scr_d[h], pst, b_pst, TB[sl][:], MOBA_W, b_TB[sl])
            P.op("dve", lambda: nc.vector.tensor_reduce(out=km[64:128, 0:S // 256],
                                                        in_=ka[64:128, :].rearrange("p (n k) -> p n k", k=256),
                                                        axis=AX.X, op=ALU.add),
                 (b_K[sl],), (b_km,))
            if S // 256 < 16:
                P.op("dve", lambda: nc.vector.memset(km[64:128, S // 256:16], 0.0), (), (b_km,))
            P.op("dve", lambda: nc.vector.tensor_scalar(out=kmh[64:128, :], in0=km[64:128, :], scalar1=1.0 / 256.0,
                                                        scalar2=None, op0=ALU.mult),
                 (b_km,), (b_kmh,))
            P.op("dve", lambda: nc.vector.scalar_tensor_tensor(out=kmr[64:128, :], in0=km[64:128, :], scalar=1.0 / 256.0,
                                                               in1=kmh[64:128, :], op0=ALU.mult, op1=ALU.subtract),
                 (b_km, b_kmh), (b_kmr,))
            P.op("dve", lambda: nc.vector.tensor_copy(out=kml[64:128, :], in_=kmr[64:128, :]), (b_kmr,), (b_kml,))
            for qt in range(NQT):
                P.mm(psG[:, qt * 16:(qt + 1) * 16], qa[64:128, qt * 128:(qt + 1) * 128], kmh[64:128, :], True, False,
                     (b_Q[sl], b_kmh), (b_psG,))
                P.mm(psG[:, qt * 16:(qt + 1) * 16], qa[64:128, qt * 128:(qt + 1) * 128], kml[64:128, :], False, True,
                     (b_Q[sl], b_kml), (b_psG,), last=(qt == NQT - 1))
            P.op("dve", lambda: nc.vector.tensor_tensor(out=gm[:], in0=psG[:, 0:NQT * 16], in1=cm_sb[:], op=ALU.add),
                 (b_psG, b_cm), (b_gm,))
            yield
            for qt in range(NQT):
                P.op("dve", lambda: nc.vector.max(out=top8[:, qt * 8:(qt + 1) * 8], in_=gm[:, qt * 16:(qt + 1) * 16]),
                     (b_gm,), (b_top8,))
            P.op("dve", lambda: nc.vector.tensor_scalar(out=thr[:], in0=top8[:, 3:NQT * 8:8], scalar1=-1e29,
                                                        scalar2=None, op0=ALU.max),
                 (b_top8,), (b_thr,))
            for qt in range(NQT):
                P.op("dve", lambda: nc.vector.tensor_scalar(out=mb[:, qt * 16:(qt + 1) * 16],
                                                            in0=gm[:, qt * 16:(qt + 1) * 16],
                                                            scalar1=thr[:, qt:qt + 1], scalar2=NEGB,
                                                            op0=ALU.is_lt, op1=ALU.mult),
                     (b_gm, b_thr), (b_mb,))
            yield
            for q8 in range(0, NQT, 8):
                n8 = min(8, NQT - q8)
                for qq in range(n8):
                    qt = q8 + qq
                    P.op("pe", lambda: nc.tensor.transpose(out=psT[0:16, qq * 128:(qq + 1) * 128],
                                                           in_=mb[:, qt * 16:(qt + 1) * 16], identity=A["ident"][:]),
                         (b_mb, A["b_ident"]), (b_psT,), inc=(qq == n8 - 1))
                P.op("act", lambda: nc.scalar.copy(out=qa[0:16, q8 * 128:(q8 + n8) * 128], in_=psT[0:16, 0:n8 * 128]),
                     (b_psT,), (b_Q[sl],))

        def run_all(gen):
            for _ in gen:
                pass

        run_all(preamble(0))
        DEPTH = 2
        deferred = []
        for h in range(NH):
            sl = h % 2
            qa, ka = QTa[sl], KTa[sl]
            nxt = preamble(h + 1) if h + 1 < NH else iter(())
            tl = [(qc, kt_) for qc in range(NQC) for kt_ in range(4 * qc + 4)]
            inject = {8, len(tl) // 3, (2 * len(tl)) // 3}

            def emit_S(tile, k):
                qc, kt_ = tile
                q0 = qc * 512
                half = kt_ >= 4 * qc + 2
                qs = q0 + 256 if half else q0
                nq = 256 if half else 512
                pS, b_pS = psS[k % 3]
                P.mm(pS[:, 0:nq], ka[0:128, kt_ * 128:(kt_ + 1) * 128], qa[0:128, qs:qs + nq], True, False,
                     (b_K[sl], b_Q[sl]), (b_pS,))
                z0 = min(qs - 128 * kt_ + 128, MOBA_W - nq)
                P.mm(pS[:, 0:nq], A["ident"][:], TB[sl][:, z0:z0 + nq], False, True,
                     (A["b_ident"], b_TB[sl]), (b_pS,))
                pt, b_pt = PT[k % 4], b_PT[k % 4]
                P.op("act", lambda: nc.scalar.activation(out=pt[:, 0:nq], in_=pS[:, 0:nq], func=AF.Exp, scale=SCALE),
                     (b_pS,), (b_pt,))

            def emit_PV(tile, k):
                nonlocal oi
                qc, kt_ = tile
                q0 = qc * 512
                half = kt_ >= 4 * qc + 2
                c0 = 256 if half else 0
                nq = 256 if half else 512
                pO, b_pO = psO[qc % 2]
                pt, b_pt = PT[k % 4], b_PT[k % 4]
                lastk = 4 * qc + 3
                P.mm(pO[0:65, c0:c0 + nq], Vh[:, kt_, h, :], pt[:, 0:nq], kt_ == 0, kt_ == lastk,
                     (b_V, b_pt), (b_pO,), last=True)
                if kt_ == lastk:
                    ac, b_ac = accs[oi % 2], b_accs[oi % 2]
                    P.op("dve", lambda: nc.vector.tensor_copy(out=ac[:], in_=pO[0:65, :]), (b_pO,), (b_ac,))
                    p2 = emit_normalize(P, nc, ac[:, :], b_ac, 512, A, psB, b_psB, o_sb[oi % 2], b_osb[oi % 2],
                                        o_d[h * 64:(h + 1) * 64, q0:q0 + 512], slot=oi % 2)
                    deferred.append((k + 3, p2))
                    oi += 1

            n = len(tl)
            for k in range(n + DEPTH):
                if k < n:
                    emit_S(tl[k], ti + k)
                if k >= DEPTH:
                    emit_PV(tl[k - DEPTH], ti + k - DEPTH)
                while deferred and deferred[0][0] <= ti + k:
                    deferred.pop(0)[1]()
                if k in inject:
                    next(nxt, None)
            ti += n
            run_all(nxt)
        while deferred:
            deferred.pop(0)[1]()
        P.barrier()


def _g_layout(g):
    return np.ascontiguousarray(np.asarray(g, np.float32).reshape(8, 128).T)


def _gain_layout(qg, kg):
    return np.ascontiguousarray(np.stack([np.tile(np.asarray(qg, np.float32), 2),
                                          np.tile(np.asarray(kg, np.float32), 2)], axis=1))


def _run(nc, in_maps):
    res = run_bass_kernel_spmd(nc, in_maps, core_ids=list(range(8)))
    return res.results


def build_fused():
    nc = bass.Bass("TRN2", target_bir_lowering=False)
    S = SEQ
    xT = nc.dram_tensor("xT", [D, S], F32, kind="ExternalInput").ap()
    rb = nc.dram_tensor("rel_bias", [32, 24], F32, kind="ExternalInput").ap()
    g_mix = [nc.dram_tensor(f"g_mix{i}", [128, 8], F32, kind="ExternalInput").ap() for i in range(2)]
    g_ffn = [nc.dram_tensor(f"g_ffn{i}", [128, 8], F32, kind="ExternalInput").ap() for i in range(2)]
    gain = [nc.dram_tensor(f"gain{i}", [128, 2], F32, kind="ExternalInput").ap() for i in range(2)]
    wqkv = [nc.dram_tensor("a_w_qkv", [D, 4608], F32, kind="ExternalInput").ap(),
            nc.dram_tensor("b_w_qkv", [D, 3072], F32, kind="ExternalInput").ap()]
    wo = [nc.dram_tensor("a_w_o", [512, D], F32, kind="ExternalInput").ap(),
          nc.dram_tensor("b_w_o", [1024, D], F32, kind="ExternalInput").ap()]
    w1 = [nc.dram_tensor(f"w1_{i}", [D, 4096], F32, kind="ExternalInput").ap() for i in range(2)]
    w2 = [nc.dram_tensor(f"w2_{i}", [4096, D], F32, kind="ExternalInput").ap() for i in range(2)]
    oh_a = nc.dram_tensor("oh_a", [3, 33, 384], BF16, kind="ExternalInput").ap()
    oh_b = nc.dram_tensor("oh_b", [33, MOBA_L], BF16, kind="ExternalInput").ap()
    kind_d = nc.dram_tensor("kind", [16, S], BF16, kind="ExternalInput").ap()
    cm_d = nc.dram_tensor("cm", [128, (S // 128) * 16], F32, kind="ExternalInput").ap()
    outT = nc.dram_tensor("outT", [D, S], F32, kind="ExternalOutput").ap()
    qk_s = nc.dram_tensor("qk_s", [3072, S], BF16, kind="Internal").ap()
    v_s = nc.dram_tensor("v_s", [S, 1536], BF16, kind="Internal").ap()
    o_s = nc.dram_tensor("o_s", [1024, S], BF16, kind="Internal").ap()
    h1 = nc.dram_tensor("h1_s", [D, S], F32, kind="Internal").ap()
    scr_a = nc.dram_tensor("scr_a", [24, 128, 384], BF16, kind="Internal").ap()
    scr_b = nc.dram_tensor("scr_b", [16, 128, MOBA_L], BF16, kind="Internal").ap()
    TT = S // 2
    with contextlib.ExitStack() as es0:
        P = Prog(nc, es0)
        for layer in range(2):
            h_src = xT if layer == 0 else h1
            h_dst = h1 if layer == 0 else outT
            CQK, CV = (3072, 1536) if layer == 0 else (2048, 1024)
            KO = 512 if layer == 0 else 1024
            for half in range(2):
                tk = slice(half * TT, (half + 1) * TT)
                emit_lin_in(P, nc, TT, CQK, CV, h_src[:, tk], g_mix[layer], wqkv[layer], gain[layer],
                            qk_s[0:CQK, tk], v_s[tk, 0:CV])
            if layer == 0:
                emit_dsw(P, nc, S, 8,
                         qk_s[0:1536, :].rearrange("(g r) s -> g r s", g=3),
                         qk_s[1536:3072, :].rearrange("(g r) s -> g r s", g=3),
                         v_s.rearrange("s (g c) -> s g c", g=3),
                         rb, oh_a, o_s[0:512, :], scr_a)
            else:
                emit_moba(P, nc, S, 16, qk_s[0:1024, :], qk_s[1024:2048, :], v_s[:, 0:1024],
                          rb[:, 0:16], oh_b, kind_d, cm_d, o_s[0:1024, :], scr_b)
            for half in range(2):
                tk = slice(half * TT, (half + 1) * TT)
                emit_lin_out(P, nc, TT, KO, 4096, h_src[:, tk], o_s[0:KO, tk], g_ffn[layer], wo[layer],
                             w1[layer], w2[layer], h_dst[:, tk])
        P.finish()
    return nc


REAL_CORES = (0, 1, 4, 5)


def kernel(x, rel_bias, norm_mix, norm_ffn, a_w_qkv, a_q_gain, a_k_gain, a_w_o,
           b_w_qkv, b_q_gain, b_k_gain, b_w_o, ffn_w1, ffn_w2):
    f32 = lambda a: np.ascontiguousarray(np.asarray(a, np.float32))
    x = np.asarray(x, np.float32)
    kind, cm = moba_consts(SEQ)
    common = {
        "rel_bias": f32(rel_bias),
        "g_mix0": _g_layout(norm_mix[0]), "g_mix1": _g_layout(norm_mix[1]),
        "g_ffn0": _g_layout(norm_ffn[0]), "g_ffn1": _g_layout(norm_ffn[1]),
        "gain0": _gain_layout(a_q_gain[0], a_k_gain[0]), "gain1": _gain_layout(b_q_gain[0], b_k_gain[0]),
        "a_w_qkv": f32(a_w_qkv[0]), "b_w_qkv": f32(b_w_qkv[0]),
        "a_w_o": f32(a_w_o[0]), "b_w_o": f32(b_w_o[0]),
        "w1_0": f32(ffn_w1[0]), "w1_1": f32(ffn_w1[1]), "w2_0": f32(ffn_w2[0]), "w2_1": f32(ffn_w2[1]),
        "oh_a": np.stack([onehot_dsw(r) for _, r in DSW_GROUPS]), "oh_b": onehot_moba(),
        "kind": kind, "cm": cm,
    }
    zeros = np.zeros((D, SEQ), np.float32)
    in_maps = []
    for c in range(8):
        m = dict(common)
        m["xT"] = np.ascontiguousarray(x[REAL_CORES.index(c)].T) if c in REAL_CORES else zeros
        in_maps.append(m)
    nc = build_fused()
    res = run_bass_kernel_spmd(nc, in_maps, core_ids=list(range(8))).results
    out = np.empty((BATCH, SEQ, D), np.float32)
    for b, c in enumerate(REAL_CORES):
        out[b] = np.asarray(res[c]["outT"], np.float32).T
    return out


def kernel_unfused(x, rel_bias, norm_mix, norm_ffn, a_w_qkv, a_q_gain, a_k_gain, a_w_o,
           b_w_qkv, b_q_gain, b_k_gain, b_w_o, ffn_w1, ffn_w2):
    x = np.asarray(x, np.float32)
    rel_bias = np.ascontiguousarray(np.asarray(rel_bias, np.float32))
    TT = SEQ // 2
    hT = [np.ascontiguousarray(x[c // 2, (c % 2) * TT:(c % 2 + 1) * TT].T) for c in range(8)]
    oh_a = np.stack([onehot_dsw(r) for _, r in DSW_GROUPS])
    oh_b = onehot_moba()
    kind, cm = moba_consts(SEQ)
    for layer in range(2):
        is_a = layer == 0
        w_qkv = np.ascontiguousarray(np.asarray(a_w_qkv[0] if is_a else b_w_qkv[0], np.float32))
        CQK, CV = (3072, 1536) if is_a else (2048, 1024)
        gain = _gain_layout(a_q_gain[0], a_k_gain[0]) if is_a else _gain_layout(b_q_gain[0], b_k_gain[0])
        g_mix = _g_layout(norm_mix[layer])
        nc1 = build_lin_in(TT, CQK, CV)
        r1 = _run(nc1, [{"hT": hT[c], "g": g_mix, "w": w_qkv, "gain": gain} for c in range(8)])
        in2 = []
        for c in range(8):
            b, r2 = c // 2, c % 2
            qk = np.concatenate([r1[2 * b]["qkT"], r1[2 * b + 1]["qkT"]], axis=1)
            v = np.concatenate([r1[2 * b]["v"], r1[2 * b + 1]["v"]], axis=0)
            if is_a:
                qT = np.stack([qk[g * 512 + r2 * 256:g * 512 + (r2 + 1) * 256] for g in range(3)])
                kT = np.stack([qk[1536 + g * 512 + r2 * 256:1536 + g * 512 + (r2 + 1) * 256] for g in range(3)])
                vv = np.stack([v[:, g * 512 + r2 * 256:g * 512 + (r2 + 1) * 256] for g in range(3)], axis=1)
                rb = np.concatenate([rel_bias[:, g * 8 + r2 * 4:g * 8 + r2 * 4 + 4] for g in range(3)], axis=1)
                in2.append({"qT": np.ascontiguousarray(qT), "kT": np.ascontiguousarray(kT),
                            "v": np.ascontiguousarray(vv), "rb": np.ascontiguousarray(rb), "oh": oh_a})
            else:
                in2.append({"qT": np.ascontiguousarray(qk[r2 * 512:(r2 + 1) * 512]),
                            "kT": np.ascontiguousarray(qk[1024 + r2 * 512:1024 + (r2 + 1) * 512]),
                            "v": np.ascontiguousarray(v[:, r2 * 512:(r2 + 1) * 512]),
                            "rb": np.ascontiguousarray(rel_bias[:, r2 * 8:(r2 + 1) * 8]),
                            "oh": oh_b, "kind": kind, "cm": cm})
        nc2 = build_dsw(SEQ, 4) if is_a else build_moba(SEQ, 8)
        r2_ = _run(nc2, in2)
        KO = 512 if is_a else 1024
        w_o = np.ascontiguousarray(np.asarray(a_w_o[0] if is_a else b_w_o[0], np.float32))
        w1 = np.ascontiguousarray(np.asarray(ffn_w1[layer], np.float32))
        w2 = np.ascontiguousarray(np.asarray(ffn_w2[layer], np.float32))
        g_ffn = _g_layout(norm_ffn[layer])
        in3 = []
        for c in range(8):
            b, r = c // 2, c % 2
            oT = np.concatenate([r2_[2 * b]["oT"], r2_[2 * b + 1]["oT"]], axis=0)
            in3.append({"hT": hT[c], "oT": np.ascontiguousarray(oT[:, r * TT:(r + 1) * TT]), "g": g_ffn,
                        "wo": w_o, "w1": w1, "w2": w2})
        nc3 = build_lin_out(TT, KO)
        r3 = _run(nc3, in3)
        hT = [np.asarray(r3[c]["hT_out"], np.float32) for c in range(8)]
    out = np.empty((BATCH, SEQ, D), np.float32)
    for c in range(8):
        out[c // 2, (c % 2) * TT:(c % 2 + 1) * TT] = hT[c].T
    return out
```
